# Optimizing a Trainium2 kernel written in Bass

```python
import jax, jax.numpy as jnp
from jax import lax
import numpy as np

D_MODEL = 2048
BATCH = 32
SEQ = 256
DEPTH = 1
DEC_BATCH = 2
DEC_SEQ = 4096
PAST_LEN = 512

GRID_W = 64
HEAD_DIM = 64
D_A = D_MODEL
N_HEADS_A = D_A // HEAD_DIM
D_B = D_MODEL // 2
N_GROUPS_B = 4
GROUP_B = D_B // N_GROUPS_B
LORA_W = 64
LORA_A = 64
LORA_G = 128
D_FF = ((8 * D_MODEL // 3 + 127) // 128) * 128
CONV_W = 3
N_DIRS = 2
N_MOD = 6
RMS_EPS = 1e-6
GN_EPS = 64e-5
IN_COLS = 3 * D_A + D_B + 2 * D_MODEL + LORA_W + LORA_A + LORA_G

kernel_name = 'bidir_rwkv7_fnet_convffn_prefix_dit'


def _rms(x, g):
    xf = x.astype(jnp.float32)
    y = xf * lax.rsqrt(jnp.mean(xf * xf, axis=-1, keepdims=True) + RMS_EPS)
    return (y * g.astype(jnp.float32)).astype(x.dtype)


def _dwconv3(x, w, b, n_rows, row_len):
    B, T, C = x.shape
    xr = x.reshape(B, n_rows, row_len, C)
    xp = jnp.pad(xr, ((0, 0), (0, 0), (1, 1), (0, 0)))
    y = xp[:, :, :-2] * w[0] + xp[:, :, 1:-1] * w[1] + xp[:, :, 2:] * w[2] + b
    return y.reshape(B, T, C)


def _fourier(xb):
    B, T, _ = xb.shape
    xg = xb.astype(jnp.float32).reshape(B, T, N_GROUPS_B, GROUP_B)
    y = jnp.fft.fft2(xg, axes=(1, 3), norm='ortho').real
    return y.reshape(B, T, D_B).astype(xb.dtype)


def _wkv_scan(s0, r, w, k, v, kk, a):
    def step(S, inp):
        r_t, w_t, k_t, v_t, kk_t, a_t = inp
        sa = jnp.einsum('bhvk,bhk->bhv', S, -kk_t)
        S = (S * w_t[:, :, None, :] + sa[..., None] * (kk_t * a_t)[:, :, None, :]
             + v_t[..., None] * k_t[:, :, None, :])
        y = jnp.einsum('bhvk,bhk->bhv', S, r_t)
        return S, y
    xs = tuple(jnp.moveaxis(t, 1, 0) for t in (r, w, k, v, kk, a))
    S, ys = lax.scan(step, s0, xs)
    return jnp.moveaxis(ys, 0, 1), S


def _rwkv7_mixer(r, k, v, dw, da, dg, s0, decay_up, decay_base, iclr_up, iclr_base,
                 gate_up, k_k, k_a, r_k, lnx_g, lnx_b):
    f32 = jnp.float32
    B, T, _ = r.shape

    def heads(t):
        return t.astype(f32).reshape(t.shape[:-1] + (N_HEADS_A, HEAD_DIM))

    r_h, k_h, v_h = heads(r), heads(k), heads(v)
    kk = k_h * heads(k_k)
    kk = kk * lax.rsqrt(jnp.sum(kk * kk, axis=-1, keepdims=True) + 1e-12)
    k_a_h = heads(k_a)
    tw = jnp.tanh(dw.astype(f32))
    daf = da.astype(f32)
    ys = []
    states = []
    for d in range(N_DIRS):
        w_logit = decay_base[d].astype(f32) + tw @ decay_up[d].astype(f32)
        decay = jnp.exp(-jnp.exp(-jax.nn.softplus(-w_logit) - 0.5))
        a_h = heads(jax.nn.sigmoid(iclr_base[d].astype(f32) + daf @ iclr_up[d].astype(f32)))
        k_d = k_h * (1.0 + (a_h - 1.0) * k_a_h)
        seq = (r_h, heads(decay), k_d, v_h, kk, a_h)
        if d == 1:
            seq = tuple(jnp.flip(t, axis=1) for t in seq)
        y_d, s_d = _wkv_scan(s0[:, d].astype(f32), *seq)
        if d == 1:
            y_d = jnp.flip(y_d, axis=1)
        ys.append(y_d)
        states.append(s_d)
    y = ys[0] + ys[1]
    mu = jnp.mean(y, axis=-1, keepdims=True)
    var = jnp.mean(jnp.square(y - mu), axis=-1, keepdims=True)
    y = (y - mu) * lax.rsqrt(var + GN_EPS) * heads(lnx_g) + heads(lnx_b)
    bonus = jnp.sum(r_h * k_h * heads(r_k), axis=-1, keepdims=True) * v_h
    g = jax.nn.sigmoid(dg.astype(f32)) @ gate_up.astype(f32)
    out = ((y + bonus).reshape(B, T, D_A) * g).astype(r.dtype)
    return out, jnp.stack(states, axis=1)


def _layer(x, mod, s0, n_rows, row_len, lp):
    (norm_mix_g, w_in, rkv_conv_w, rkv_conv_b, decay_up, decay_base, iclr_up, iclr_base,
     gate_up, k_k, k_a, r_k, lnx_g, lnx_b, w_out_a, w_fourier, w_out, norm_ffn_g,
     ffn_w_in, ffn_conv_w, ffn_conv_b, ffn_w_down) = lp
    sh1, sc1, gt1, sh2, sc2, gt2 = jnp.split(mod[:, None, :], N_MOD, axis=-1)
    h = _rms(x, norm_mix_g) * (1.0 + sc1) + sh1
    proj = h @ w_in
    c1 = 3 * D_A
    c2 = c1 + D_B
    c3 = c2 + 2 * D_MODEL
    c4 = c3 + LORA_W
    c5 = c4 + LORA_A
    rkv, xb, gates, dw, da, dg = jnp.split(proj, [c1, c2, c3, c4, c5], axis=-1)
    rkv = _dwconv3(rkv, rkv_conv_w, rkv_conv_b, n_rows, row_len)
    r, k, v = jnp.split(rkv, 3, axis=-1)
    y_a, s_fin = _rwkv7_mixer(r, k, v, dw, da, dg, s0, decay_up, decay_base, iclr_up,
                              iclr_base, gate_up, k_k, k_a, r_k, lnx_g, lnx_b)
    y_b = _fourier(xb) @ w_fourier
    g_a, g_b = jnp.split(jax.nn.sigmoid(gates), 2, axis=-1)
    mixed = (g_a * (y_a @ w_out_a) + g_b * y_b) @ w_out
    x = x + gt1 * mixed
    h2 = _rms(x, norm_ffn_g) * (1.0 + sc2) + sh2
    u = _dwconv3(h2 @ ffn_w_in, ffn_conv_w, ffn_conv_b, n_rows, row_len)
    u_gate, u_val = jnp.split(u, 2, axis=-1)
    x = x + gt2 * ((jax.nn.silu(u_gate) * u_val) @ ffn_w_down)
    return x, s_fin


def setup_inputs(seed: int = 0) -> dict:
    key = jax.random.key(seed)
    ks = iter(jax.random.split(key, 40))

    def nrm(shape, scale):
        return jax.random.normal(next(ks), shape, jnp.float32) * scale

    L = DEPTH
    return {
        'x_prompt': nrm((BATCH, SEQ, D_MODEL), 1.0),
        'x_sample': nrm((DEC_BATCH, DEC_SEQ, D_MODEL), 1.0),
        'state_rwkv': nrm((DEC_BATCH, L, N_DIRS, N_HEADS_A, HEAD_DIM, HEAD_DIM), 0.3),
        'c': nrm((DEC_BATCH, D_MODEL), 1.0),
        'c_ctx': nrm((D_MODEL,), 1.0),
        'ada_w': nrm((L, D_MODEL, N_MOD * D_MODEL), 0.5 * D_MODEL ** -0.5),
        'ada_b': nrm((L, N_MOD * D_MODEL), 0.02),
        'norm_mix_g': 1.0 + nrm((L, D_MODEL), 0.02),
        'w_in': nrm((L, D_MODEL, IN_COLS), D_MODEL ** -0.5),
        'rkv_conv_w': nrm((L, CONV_W, 3 * D_A), CONV_W ** -0.5),
        'rkv_conv_b': nrm((L, 3 * D_A), 0.02),
        'decay_up': nrm((L, N_DIRS, LORA_W, D_A), LORA_W ** -0.5),
        'decay_base': nrm((L, N_DIRS, D_A), 0.5),
        'iclr_up': nrm((L, N_DIRS, LORA_A, D_A), LORA_A ** -0.5),
        'iclr_base': nrm((L, N_DIRS, D_A), 0.5),
        'gate_up': nrm((L, LORA_G, D_A), LORA_G ** -0.5),
        'k_k': 0.85 + nrm((L, D_A), 0.02),
        'k_a': 1.0 + nrm((L, D_A), 0.02),
        'r_k': nrm((L, D_A), 0.1),
        'lnx_g': 1.0 + nrm((L, D_A), 0.02),
        'lnx_b': nrm((L, D_A), 0.02),
        'w_out_a': nrm((L, D_A, D_MODEL), D_A ** -0.5),
        'w_fourier': nrm((L, D_B, D_MODEL), D_B ** -0.5),
        'w_out': nrm((L, D_MODEL, D_MODEL), D_MODEL ** -0.5),
        'norm_ffn_g': 1.0 + nrm((L, D_MODEL), 0.02),
        'ffn_w_in': nrm((L, D_MODEL, 2 * D_FF), D_MODEL ** -0.5),
        'ffn_conv_w': nrm((L, CONV_W, 2 * D_FF), CONV_W ** -0.5),
        'ffn_conv_b': nrm((L, 2 * D_FF), 0.02),
        'ffn_w_down': nrm((L, D_FF, D_MODEL), D_FF ** -0.5),
        'final_norm_g': 1.0 + nrm((D_MODEL,), 0.02),
    }


def reference(x_prompt, x_sample, state_rwkv, c, c_ctx, ada_w, ada_b, norm_mix_g, w_in,
              rkv_conv_w, rkv_conv_b, decay_up, decay_base, iclr_up, iclr_base, gate_up,
              k_k, k_a, r_k, lnx_g, lnx_b, w_out_a, w_fourier, w_out, norm_ffn_g,
              ffn_w_in, ffn_conv_w, ffn_conv_b, ffn_w_down, final_norm_g):
    f32 = jnp.float32
    ctx_len = x_prompt.shape[1]
    rows = x_sample.shape[1] // GRID_W
    silu_ctx = jax.nn.silu(c_ctx.astype(f32))[None, :]
    silu_c = jax.nn.silu(c.astype(f32))
    xp = x_prompt
    xs = x_sample
    zero_state = jnp.zeros((x_prompt.shape[0], N_DIRS, N_HEADS_A, HEAD_DIM, HEAD_DIM), f32)
    new_states = []
    for l in range(DEPTH):
        lp = (norm_mix_g[l], w_in[l], rkv_conv_w[l], rkv_conv_b[l], decay_up[l], decay_base[l],
              iclr_up[l], iclr_base[l], gate_up[l], k_k[l], k_a[l], r_k[l], lnx_g[l], lnx_b[l],
              w_out_a[l], w_fourier[l], w_out[l], norm_ffn_g[l], ffn_w_in[l], ffn_conv_w[l],
              ffn_conv_b[l], ffn_w_down[l])
        aw = ada_w[l].astype(f32)
        ab = ada_b[l].astype(f32)
        mod_ctx = (silu_ctx @ aw + ab).astype(xp.dtype)
        xp, s_ctx = _layer(xp, mod_ctx, zero_state, 1, ctx_len, lp)
        new_states.append(s_ctx)
        mod_lat = (silu_c @ aw + ab).astype(xs.dtype)
        xs, _ = _layer(xs, mod_lat, state_rwkv[:, l], rows, GRID_W, lp)
    new_state_rwkv = jnp.stack(new_states, axis=1).astype(x_prompt.dtype)
    y_prompt = _rms(xp, final_norm_g)
    y_sample = _rms(xs, final_norm_g)
    return (y_prompt, y_sample, new_state_rwkv)
```

```python
import contextlib
import numpy as np
import ml_dtypes
import concourse.bass as bass
import concourse.mybir as mybir
from concourse.bass_utils import run_bass_kernel_spmd

F32 = mybir.dt.float32
BF16 = mybir.dt.bfloat16
AF = mybir.ActivationFunctionType
ALU = mybir.AluOpType
NPBF = ml_dtypes.bfloat16

D = 2048
DC = 16
DFF = 5504
FC = 43
TB = 512
CDEC = -float(np.exp(-0.5))
RMS_EPS = 1e-6
GN_EPS = 64e-5

PV = {}
_o = 0
for _n, _c in [("ada_b", 96), ("g1", 16), ("g2", 16), ("gf", 16), ("rcw", 144), ("rcb", 48),
               ("dbase", 32), ("ibase", 32), ("k_k", 16), ("k_a", 16), ("r_k", 16), ("lng", 16),
               ("lnb", 16), ("fcw", 258), ("fcb", 86)]:
    PV[_n] = _o
    _o += _c
NPV = _o


import os as _os
KMAXOPS = int(_os.environ.get('KMAXOPS', '100000000'))


class StopBuild(Exception):
    pass


PSUM_IDS = set()


class Buf:
    def __init__(self, t, multi=False):
        self.t = t
        self.psum = id(t) in PSUM_IDS
        self.w = None
        self.ws = {}
        self.r = {}
        self.multi = multi

    def __getitem__(self, k):
        return self.t[k]


class Eng:
    def __init__(self, name, handle, sems, inc, seen, selfwait):
        self.name = name
        self.h = handle
        self.sems = sems
        self.inc = inc
        self.counts = [0] * len(sems)
        self.rr = 0
        self.seen = seen
        self.selfwait = selfwait


class KB:
    def __init__(self, nc, es):
        self.nc = nc
        self.es = es
        self.nins = 0

        def sem(n):
            return es.enter_context(nc.semaphore(n))
        pool_seen = {}
        self.PE = Eng("pe", nc.tensor, [sem("s_pe")], 1, {}, False)
        self.ACT = Eng("act", nc.scalar, [sem("s_act")], 1, {}, True)
        self.DVE = Eng("dve", nc.vector, [sem("s_dve")], 1, {}, True)
        self.POOL = Eng("pool", nc.gpsimd, [sem("s_pool")], 1, pool_seen, True)
        self.SP = Eng("sp", nc.sync, [sem("s_sp%d" % i) for i in range(16)], 16, {}, True)
        self.PQ = Eng("pq", nc.gpsimd, [sem("s_pq%d" % i) for i in range(6)], 16, pool_seen, True)
        self.engs = [self.PE, self.ACT, self.DVE, self.POOL, self.SP, self.PQ]

    def sb(self, name, shape, dt, stack=None):
        self.uid = getattr(self, "uid", 0) + 1
        return (stack or self.es).enter_context(self.nc.sbuf_tensor("sb%d_%s" % (self.uid, name), list(shape), dt))

    def ps(self, name, shape, dt, stack=None):
        self.uid = getattr(self, "uid", 0) + 1
        t = (stack or self.es).enter_context(self.nc.psum_tensor("ps%d_%s" % (self.uid, name), list(shape), dt))
        PSUM_IDS.add(id(t))
        return t

    def op(self, eng, fn, reads=(), writes=()):
        if getattr(self, 'disabled', False):
            return None
        self.nops = getattr(self, 'nops', 0) + 1
        if self.nops > KMAXOPS:
            self.disabled = True
            return None
        waits = {}

        def add(ev):
            if ev is None:
                return
            s, v = ev
            if waits.get(s, 0) < v:
                waits[s] = v
        for b in reads:
            if b.multi:
                for s, v in b.ws.items():
                    add((s, v))
            else:
                add(b.w)
                if b.psum:
                    for s, v in b.r.items():
                        add((s, v))
        for b in writes:
            if b.multi:
                continue
            add(b.w)
            for s, v in b.r.items():
                add((s, v))
        own = set(id(s) for s in eng.sems)
        for s, v in waits.items():
            if id(s) in own and not eng.selfwait:
                continue
            key = id(s)
            if eng.seen.get(key, 0) < v:
                eng.h.wait_ge(s, v)
                eng.seen[key] = v
                self.nins += 1
        i = eng.rr
        eng.rr = (eng.rr + 1) % len(eng.sems)
        if eng.inc == 16 and eng.counts[i] > 0 and eng.seen.get(id(eng.sems[i]), 0) < eng.counts[i]:
            eng.h.wait_ge(eng.sems[i], eng.counts[i])
            eng.seen[id(eng.sems[i])] = eng.counts[i]
            self.nins += 1
        ins = fn()
        eng.counts[i] += eng.inc
        ins.then_inc(eng.sems[i], eng.inc)
        self.nins += 1
        ev = (eng.sems[i], eng.counts[i])
        for b in writes:
            if b.multi:
                if b.ws.get(ev[0], 0) < ev[1]:
                    b.ws[ev[0]] = ev[1]
            else:
                b.w = ev
                b.r = {}
        for b in reads:
            if not b.multi:
                if b.r.get(ev[0], 0) < ev[1]:
                    b.r[ev[0]] = ev[1]
        return ins

    def dma(self, eng, out, in_, reads=(), writes=()):
        return self.op(eng, lambda: eng.h.dma_start(out=out, in_=in_), reads, writes)

    def barrier(self):
        if getattr(self, 'disabled', False):
            return
        for e in self.engs:
            if e is self.PQ:
                continue
            for o in self.engs:
                for s, c in zip(o.sems, o.counts):
                    if c > 0 and e.seen.get(id(s), 0) < c and not (o is e and not e.selfwait):
                        e.h.wait_ge(s, c)
                        e.seen[id(s)] = c
                        self.nins += 1

    def final_wait(self):
        e = self.SP
        for o in self.engs:
            for s, c in zip(o.sems, o.counts):
                if c > 0:
                    e.h.wait_ge(s, c)


def build_program(NSEQ, TL, dbg=None):
    import os
    NST = int(os.environ.get('KSTAGES', '9'))
    NBC = NSEQ * 256 // TB
    NBL = TL // TB
    NBLK = NBC + NBL
    NTOK = NBLK * TB
    NTL = TL // 128
    nc = bass.Bass("TRN2", target_bir_lowering=False)

    def din(name, shape, dt=F32):
        return nc.dram_tensor(name, list(shape), dt, kind="ExternalInput").ap()

    def dscr(name, shape, dt):
        return nc.dram_tensor(name, list(shape), dt, kind="Internal").ap()
    x_all = din("x_all", [NTOK, D])
    cfm_d = din("cfm", [128, DC, 2])
    st_lat = din("st_lat", [2, 32, 64, 64])
    ada_w = din("ada_w", [D, 6 * D])
    w_in = din("w_in", [D, 11520])
    decay_up = din("decay_up", [2, 64, D])
    iclr_up = din("iclr_up", [2, 64, D])
    gate_up = din("gate_up", [128, D])
    w_out_a = din("w_out_a", [D, D])
    w_fourier = din("w_fourier", [1024, D])
    w_out = din("w_out", [D, D])
    ffn_w_in = din("ffn_w_in", [D, 2 * DFF])
    ffn_w_down = din("ffn_w_down", [DFF, D])
    pvec_d = din("pvec", [128, NPV])
    ident_f_d = din("ident_f", [128, 128])
    cb_d = din("cbf", [128, 6, 128], BF16)
    blk64_d = din("blk64", [128, 128])
    masks_d = din("masks", [128, 2, 6, 128], BF16)
    rmask_d = din("rmask", [128, TB])
    ct256_d = din("ct256", [256, 256], BF16)
    st256_d = din("st256", [256, 256], BF16)
    ctL_d = din("ctL", [TL, TL], BF16)
    stL_d = din("stL", [TL, TL], BF16)
    cc_d = din("ccm", [256, 256], BF16)
    scn_d = din("scn", [256, 256], BF16)
    y_all = nc.dram_tensor("y_all", [NTOK, D], F32, kind="ExternalOutput").ap()
    st_out = nc.dram_tensor("st_out", [NSEQ, 2, 32, 64, 64], F32, kind="ExternalOutput").ap()
    FMs = Buf(dscr("FMs", [NBLK, 16, 2, 128, 4, 4, 128], BF16), multi=True)
    TMs = Buf(dscr("TMs", [NBLK, 16, 2, 128, 4, 2, 128], BF16), multi=True)
    TVs = Buf(dscr("TVs", [NBLK, 16, 128, 4, 128], BF16), multi=True)
    SCs = Buf(dscr("SCs", [NBLK, 16, 2, 128, 12], F32), multi=True)
    BGs = Buf(dscr("BGs", [NBLK, 16, 2, 128, TB], F32), multi=True)
    Ys = Buf(dscr("Ys", [NBLK, 16, 2, 128, TB], F32), multi=True)
    XBs = Buf(dscr("XBs", [NTOK, 1024], BF16), multi=True)
    FSs = Buf(dscr("FSs", [8, 128, NTOK], BF16), multi=True)
    DX = Buf(None, multi=True)

    es = contextlib.ExitStack()
    with es:
        kb = KB(nc, es)
        PE, ACT, DVE, POOL, SP, PQ = kb.PE, kb.ACT, kb.DVE, kb.POOL, kb.SP, kb.PQ

        def stage_gate(k):
            if NST < k:
                kb.disabled = True
        pvec = Buf(kb.sb("pvec", [128, NPV], F32))
        ident_f = Buf(kb.sb("ident_f", [128, 128], F32))
        cb = Buf(kb.sb("cb", [128, 6, 128], BF16))
        blk64 = Buf(kb.sb("blk64", [128, 128], F32))
        masks = Buf(kb.sb("masks", [128, 2, 6, 128], BF16))
        rmask = Buf(kb.sb("rmask", [128, TB], F32))
        cfm = Buf(kb.sb("cfm_s", [128, DC, 2], F32))
        modv = Buf(kb.sb("modv", [128, 96, 2], F32))
        eff = Buf(kb.sb("eff", [128, 2, DC, 2], F32))
        for b_, d_ in [(pvec, pvec_d), (ident_f, ident_f_d), (cb, cb_d), (blk64, blk64_d), (masks, masks_d),
                       (rmask, rmask_d), (cfm, cfm_d)]:
            kb.dma(SP, b_.t[:], d_, reads=[DX], writes=[b_])
        ident_b = cb.t[:, 0, :]
        blk1 = cb.t[:, 1, :]
        ones_b = cb.t[:, 2, :]

        def pv(name, col):
            c0 = PV[name] + col
            return pvec.t[:, c0:c0 + 1]

        def make_wpool(stack, n, nbytes_elems):
            return [Buf(kb.sb("wp%d_%d" % (i, nbytes_elems), [128, nbytes_elems], BF16, stack)) for i in range(n)]

        class WStream:
            def __init__(self, pool):
                self.pool = pool
                self.i = 0

            def load(self, src_ap, kc, ncols):
                b = self.pool[self.i % len(self.pool)]
                self.i += 1
                v = b.t[:, 0:kc * ncols].rearrange("p (k n) -> p k n", k=kc)
                if isinstance(src_ap, list):
                    o = 0
                    for sa in src_ap:
                        w_ = sa.shape[2]
                        kb.dma(PQ, v[:, :, o:o + w_], sa, reads=[DX], writes=[b])
                        o += w_
                else:
                    kb.dma(PQ, v, src_ap, reads=[DX], writes=[b])
                return b, v

        stage_gate(0)
        with contextlib.ExitStack() as st:
            wpool = make_wpool(st, 3, 16 * 512)
            ws = WStream(wpool)
            scb = Buf(kb.sb("scb", [128, DC, 2], BF16, st))
            pm = [Buf(kb.ps("pm0_%d" % i, [128, 512], F32, st)) for i in range(2)]
            kb.op(ACT, lambda: nc.scalar.activation(out=scb.t[:], in_=cfm.t[:], func=AF.Silu), [cfm], [scb])
            adv = ada_w.rearrange("(k p) n -> p k n", p=128)
            for g in range(24):
                wb, wv = ws.load(adv[:, :, g * 512:(g + 1) * 512], 16, 512)
                for j in range(4):
                    m = g * 4 + j
                    pb = pm[m % 2]
                    for k in range(16):
                        kb.op(PE, lambda k=k, j=j, pb=pb, wv=wv: nc.tensor.matmul(
                            pb.t[:, 0:2], lhsT=wv[:, k, j * 128:(j + 1) * 128], rhs=scb.t[:, k, :],
                            start=(k == 0), stop=(k == 15)), [wb, scb], [pb])
                    kb.op(ACT, lambda m=m, pb=pb: nc.scalar.activation(
                        out=modv.t[:, m, :], in_=pb.t[:, 0:2], func=AF.Identity, bias=pv("ada_b", m)),
                        [pb, pvec], [modv])
            for w_, (gn, mo) in enumerate([("g1", 16), ("g2", 64)]):
                kb.op(DVE, lambda w_=w_, mo=mo: nc.vector.tensor_scalar(
                    out=eff.t[:, w_, :, :], in0=modv.t[:, mo:mo + 16, :], scalar1=1.0, scalar2=None, op0=ALU.add),
                    [modv], [eff])
                g0 = PV[gn]
                kb.op(DVE, lambda w_=w_, g0=g0: nc.vector.tensor_tensor(
                    out=eff.t[:, w_, :, :], in0=eff.t[:, w_, :, :],
                    in1=pvec.t[:, g0:g0 + 16].rearrange("p (c o) -> p c o", o=1).broadcast_to([128, 16, 2]),
                    op=ALU.mult), [eff, pvec], [eff])
        kb.barrier()

        def modcol(s, c, r):
            return modv.t[:, s * 16 + c, r:r + 1]

        def load_xT(blk, xtm, xT, ptr):
            for t in range(4):
                xb_ = xtm[t % 2]
                r0 = blk * TB + t * 128
                kb.dma(SP, xb_.t[:], x_all[r0:r0 + 128, :], reads=[DX], writes=[xb_])
                for g in range(4):
                    pb = ptr[(t * 4 + g) % len(ptr)]
                    for j in range(4):
                        c = g * 4 + j
                        kb.op(PE, lambda pb=pb, j=j, c=c, xb_=xb_: nc.tensor.transpose(
                            pb.t[:, j, :], xb_.t[:, c * 128:(c + 1) * 128], ident_f.t[:]), [xb_, ident_f], [pb])
                    for j in range(4):
                        c = g * 4 + j
                        e = ACT if j % 2 == 0 else DVE
                        if e is ACT:
                            kb.op(ACT, lambda pb=pb, j=j, c=c, t=t: nc.scalar.copy(
                                out=xT[c].t[:, t * 128:(t + 1) * 128], in_=pb.t[:, j, :]), [pb], [xT[c]])
                        else:
                            kb.op(DVE, lambda pb=pb, j=j, c=c, t=t: nc.vector.tensor_copy(
                                out=xT[c].t[:, t * 128:(t + 1) * 128], in_=pb.t[:, j, :]), [pb], [xT[c]])

        def rstd_of(xT, sqb, pss, rstd):
            for c in range(16):
                sq = sqb[c % 2]
                kb.op(ACT, lambda c=c, sq=sq: nc.scalar.activation(out=sq.t[:], in_=xT[c].t[:], func=AF.Square),
                      [xT[c]], [sq])
                kb.op(PE, lambda c=c, sq=sq: nc.tensor.matmul(pss.t[:], lhsT=ones_b, rhs=sq.t[:],
                                                             start=(c == 0), stop=(c == 15)), [sq, cb], [pss])
            kb.op(DVE, lambda: nc.vector.tensor_scalar(out=rstd.t[:], in0=pss.t[:], scalar1=1.0 / D, scalar2=RMS_EPS,
                                                       op0=ALU.mult, op1=ALU.add), [pss], [rstd])
            kb.op(ACT, lambda: nc.scalar.activation(out=rstd.t[:], in_=rstd.t[:], func=AF.Ln), [rstd], [rstd])
            kb.op(ACT, lambda: nc.scalar.activation(out=rstd.t[:], in_=rstd.t[:], func=AF.Exp, scale=-0.5),
                  [rstd], [rstd])

        def norm_mod(xT, rstd, hT, tmpf, which, r):
            sh = 0 if which == 0 else 3
            for c in range(16):
                tf = tmpf[c % 2]
                kb.op(DVE, lambda c=c, tf=tf: nc.vector.scalar_tensor_tensor(
                    out=tf.t[:], in0=xT[c].t[:], scalar=eff.t[:, which, c, r:r + 1], in1=rstd.t[:],
                    op0=ALU.mult, op1=ALU.mult), [xT[c], eff, rstd], [tf])
                kb.op(ACT, lambda c=c, tf=tf: nc.scalar.activation(
                    out=hT[c].t[:], in_=tf.t[:], func=AF.Identity, bias=modcol(sh, c, r)), [tf, modv], [hT[c]])

        def conv3(ps, out, wname, bname, col, nwc, rl, e2=None):
            nr = TB // rl
            kb.op(ACT, lambda: nc.scalar.activation(out=out.t[:], in_=ps.t[:], func=AF.Identity,
                                                    bias=pv(bname, col), scale=pv(wname, nwc + col)),
                  [ps, pvec], [out])
            ov = out.t[:].rearrange("p (a b) -> p a b", b=rl)
            pv_ = ps.t[:].rearrange("p (a b) -> p a b", b=rl)
            kb.op(DVE, lambda: nc.vector.scalar_tensor_tensor(
                out=ov[:, :, 1:rl], in0=pv_[:, :, 0:rl - 1], scalar=pv(wname, col), in1=ov[:, :, 1:rl],
                op0=ALU.mult, op1=ALU.add), [ps, pvec, out], [out])
            kb.op(DVE, lambda: nc.vector.scalar_tensor_tensor(
                out=ov[:, :, 0:rl - 1], in0=pv_[:, :, 1:rl], scalar=pv(wname, 2 * nwc + col), in1=ov[:, :, 0:rl - 1],
                op0=ALU.mult, op1=ALU.add), [ps, pvec, out], [out])

        def blk_info(blk):
            if blk < NBC:
                return 0, 256
            return 1, 64

        stage_gate(1)
        with contextlib.ExitStack() as st:
            xtm = [Buf(kb.sb("xtm%d" % i, [128, D], F32, st)) for i in range(2)]
            xT = [Buf(kb.sb("xT%d" % i, [128, TB], F32, st)) for i in range(16)]
            hT = [Buf(kb.sb("hT%d" % i, [128, TB], BF16, st)) for i in range(16)]
            sqb = [Buf(kb.sb("sqb%d" % i, [128, TB], BF16, st)) for i in range(2)]
            tmpf = [Buf(kb.sb("tmpf%d" % i, [128, TB], F32, st)) for i in range(2)]
            rstd = Buf(kb.sb("rstd", [128, TB], F32, st))
            ws = WStream(make_wpool(st, 2, 16 * 512))
            dup = Buf(kb.sb("dup", [128, 2, D], BF16, st))
            gup = Buf(kb.sb("gup", [128, D], BF16, st))
            for d in range(2):
                kb.dma(PQ, dup.t[0:64, d, :], decay_up[d], reads=[DX], writes=[dup])
                kb.dma(PQ, dup.t[64:128, d, :], iclr_up[d], reads=[DX], writes=[dup])
            kb.dma(PQ, gup.t[:], gate_up, reads=[DX], writes=[gup])
            lora = Buf(kb.sb("lora", [128, 2, TB], BF16, st))
            F = lambda n: Buf(kb.sb(n, [128, TB], F32, st))
            r_c, k_c, v_c, kk, kkn, t1, t2, sig, aa, Lf, Lm, E1, E2, E3, kd, bb, bon, gg = [
                F("f1_%d" % i) for i in range(18)]
            sqk = Buf(kb.sb("sqk", [128, TB], BF16, st))
            vb = Buf(kb.sb("vb", [128, TB], BF16, st))
            FMo = [Buf(kb.sb("FMo%d" % i, [128, 4, 4, 128], BF16, st)) for i in range(2)]
            TMo = [Buf(kb.sb("TMo%d" % i, [128, 4, 2, 128], BF16, st)) for i in range(2)]
            TVo = Buf(kb.sb("TVo", [128, 4, 128], BF16, st))
            SCo = [Buf(kb.sb("SCo%d" % i, [128, 12], F32, st)) for i in range(2)]
            xbo = [Buf(kb.sb("xbo%d" % i, [128, 512], BF16, st)) for i in range(2)]
            ptr = [Buf(kb.ps("ptr%d" % i, [128, 4, 128], F32, st)) for i in range(2)]
            pp = [Buf(kb.ps("pp%d" % i, [128, TB], F32, st)) for i in range(4)]
            ptb = [Buf(kb.ps("ptb%d" % i, [128, 8, 128], BF16, st)) for i in range(2)]
            ppi = [0]

            def nextp():
                b_ = pp[ppi[0] % 4]
                ppi[0] += 1
                return b_
            wv_in = w_in.rearrange("(k p) n -> p k n", p=128)
            for blk in range(NBLK):
                r, rl = blk_info(blk)
                load_xT(blk, xtm, xT, ptr)
                pss = nextp()
                rstd_of(xT, sqb, pss, rstd)
                norm_mod(xT, rstd, hT, tmpf, 0, r)
                wb, wv = ws.load(wv_in[:, :, 11264:11520], 16, 256)
                p0 = nextp()
                p1 = nextp()
                for j, pb in enumerate([p0, p1]):
                    for k in range(16):
                        kb.op(PE, lambda k=k, j=j, pb=pb, wv=wv: nc.tensor.matmul(
                            pb.t[:], lhsT=wv[:, k, j * 128:(j + 1) * 128], rhs=hT[k].t[:], start=(k == 0),
                            stop=(k == 15)), [wb, hT[k]], [pb])
                kb.op(ACT, lambda: nc.scalar.activation(out=lora.t[0:64, 0, :], in_=p0.t[0:64, :], func=AF.Tanh),
                      [p0], [lora])
                kb.op(ACT, lambda: nc.scalar.copy(out=lora.t[64:128, 0, :], in_=p0.t[64:128, :]), [p0], [lora])
                kb.op(ACT, lambda: nc.scalar.activation(out=lora.t[:, 1, :], in_=p1.t[:], func=AF.Sigmoid),
                      [p1], [lora])
                for g2 in range(2):
                    wb, wv = ws.load(wv_in[:, :, 6144 + g2 * 512:6144 + (g2 + 1) * 512], 16, 512)
                    for t in range(4):
                        pb = nextp()
                        for k in range(16):
                            kb.op(PE, lambda k=k, t=t, pb=pb, wv=wv: nc.tensor.matmul(
                                pb.t[:], lhsT=hT[k].t[:, t * 128:(t + 1) * 128], rhs=wv[:, k, :], start=(k == 0),
                                stop=(k == 15)), [wb, hT[k]], [pb])
                        xo = xbo[(g2 * 4 + t) % 2]
                        kb.op(ACT, lambda pb=pb, xo=xo: nc.scalar.copy(out=xo.t[:], in_=pb.t[:]), [pb], [xo])
                        r0 = blk * TB + t * 128
                        kb.dma(SP, XBs.t[r0:r0 + 128, g2 * 512:(g2 + 1) * 512], xo.t[:], reads=[xo], writes=[XBs])
                for hp in range(16):
                    wb, wv = ws.load([wv_in[:, :, s3 * 2048 + hp * 128:s3 * 2048 + (hp + 1) * 128] for s3 in range(3)], 16, 384)
                    pr, pk, pvv = nextp(), nextp(), nextp()
                    for s_, pb in enumerate([pr, pk, pvv]):
                        for k in range(16):
                            kb.op(PE, lambda k=k, s_=s_, pb=pb, wv=wv: nc.tensor.matmul(
                                pb.t[:], lhsT=wv[:, k, s_ * 128:(s_ + 1) * 128], rhs=hT[k].t[:], start=(k == 0),
                                stop=(k == 15)), [wb, hT[k]], [pb])
                    conv3(pr, r_c, "rcw", "rcb", hp, 48, rl)
                    conv3(pk, k_c, "rcw", "rcb", 16 + hp, 48, rl)
                    conv3(pvv, v_c, "rcw", "rcb", 32 + hp, 48, rl)
                    kb.op(ACT, lambda: nc.scalar.activation(out=kk.t[:], in_=k_c.t[:], func=AF.Identity,
                                                            scale=pv("k_k", hp)), [k_c, pvec], [kk])
                    kb.op(ACT, lambda: nc.scalar.activation(out=sqk.t[:], in_=kk.t[:], func=AF.Square), [kk], [sqk])
                    pn = nextp()
                    kb.op(PE, lambda pn=pn: nc.tensor.matmul(pn.t[:], lhsT=blk1, rhs=sqk.t[:], start=True, stop=True),
                          [sqk, cb], [pn])
                    kb.op(DVE, lambda pn=pn: nc.vector.tensor_scalar(out=t1.t[:], in0=pn.t[:], scalar1=1e-12,
                                                                    scalar2=None, op0=ALU.add), [pn], [t1])
                    kb.op(ACT, lambda: nc.scalar.activation(out=t1.t[:], in_=t1.t[:], func=AF.Ln), [t1], [t1])
                    kb.op(ACT, lambda: nc.scalar.activation(out=t1.t[:], in_=t1.t[:], func=AF.Exp, scale=-0.5),
                          [t1], [t1])
                    kb.op(POOL, lambda: nc.gpsimd.tensor_tensor(out=kkn.t[:], in0=kk.t[:], in1=t1.t[:], op=ALU.mult),
                          [kk, t1], [kkn])
                    kb.op(DVE, lambda: nc.vector.tensor_tensor(out=t2.t[:], in0=r_c.t[:], in1=k_c.t[:], op=ALU.mult),
                          [r_c, k_c], [t2])
                    kb.op(ACT, lambda: nc.scalar.activation(out=sqk.t[:], in_=t2.t[:], func=AF.Identity,
                                                            scale=pv("r_k", hp)), [t2, pvec], [sqk])
                    pn2 = nextp()
                    kb.op(PE, lambda pn2=pn2: nc.tensor.matmul(pn2.t[:], lhsT=blk1, rhs=sqk.t[:], start=True,
                                                               stop=True), [sqk, cb], [pn2])
                    kb.op(DVE, lambda pn2=pn2: nc.vector.tensor_tensor(out=bon.t[:], in0=pn2.t[:], in1=v_c.t[:],
                                                                      op=ALU.mult), [pn2, v_c], [bon])
                    kb.dma(SP, BGs.t[blk, hp, 0], bon.t[:], reads=[bon], writes=[BGs])
                    pg = nextp()
                    kb.op(PE, lambda pg=pg: nc.tensor.matmul(pg.t[:], lhsT=gup.t[:, hp * 128:(hp + 1) * 128],
                                                             rhs=lora.t[:, 1, :], start=True, stop=True),
                          [gup, lora], [pg])
                    kb.op(ACT, lambda pg=pg: nc.scalar.copy(out=gg.t[:], in_=pg.t[:]), [pg], [gg])
                    kb.dma(SP, BGs.t[blk, hp, 1], gg.t[:], reads=[gg], writes=[BGs])
                    kb.op(ACT, lambda: nc.scalar.copy(out=vb.t[:], in_=v_c.t[:]), [v_c], [vb])
                    pt_ = ptb[0]
                    for c in range(4):
                        kb.op(PE, lambda c=c, pt_=pt_: nc.tensor.transpose(pt_.t[:, c, :], vb.t[:, c * 128:(c + 1) * 128],
                                                                           ident_b), [vb, cb], [pt_])
                    kb.op(DVE, lambda pt_=pt_: nc.vector.tensor_copy(out=TVo.t[:], in_=pt_.t[:, 0:4, :]), [pt_], [TVo])
                    kb.dma(SP, TVs.t[blk, hp], TVo.t[:], reads=[TVo], writes=[TVs])
                    for d in range(2):
                        fm = FMo[d]
                        tm = TMo[d]
                        sc = SCo[d]
                        pw, pa = nextp(), nextp()
                        kb.op(PE, lambda pw=pw, d=d: nc.tensor.matmul(
                            pw.t[:], lhsT=dup.t[0:64, d, hp * 128:(hp + 1) * 128], rhs=lora.t[0:64, 0, :],
                            start=True, stop=True), [dup, lora], [pw])
                        kb.op(PE, lambda pa=pa, d=d: nc.tensor.matmul(
                            pa.t[:], lhsT=dup.t[64:128, d, hp * 128:(hp + 1) * 128], rhs=lora.t[64:128, 0, :],
                            start=True, stop=True), [dup, lora], [pa])
                        kb.op(ACT, lambda pw=pw, d=d: nc.scalar.activation(
                            out=sig.t[:], in_=pw.t[:], func=AF.Sigmoid, bias=pv("dbase", d * 16 + hp)),
                            [pw, pvec], [sig])
                        kb.op(ACT, lambda pa=pa, d=d: nc.scalar.activation(
                            out=aa.t[:], in_=pa.t[:], func=AF.Sigmoid, bias=pv("ibase", d * 16 + hp)),
                            [pa, pvec], [aa])
                        kb.op(DVE, lambda: nc.vector.tensor_tensor_scan(
                            out=Lf.t[:], data0=rmask.t[:], data1=sig.t[:], initial=0.0, op0=ALU.mult, op1=ALU.add),
                            [rmask, sig], [Lf])
                        L3 = Lf.t[:].rearrange("p (c t) -> p c t", t=128)
                        Lm3 = Lm.t[:].rearrange("p (c t) -> p c t", t=128)
                        if d == 0:
                            mi, ei = 63, 127
                            Lsrc = Lf
                        else:
                            kb.op(DVE, lambda: nc.vector.tensor_tensor(out=t1.t[:], in0=sig.t[:], in1=Lf.t[:],
                                                                       op=ALU.subtract), [sig, Lf], [t1])
                            t13 = t1.t[:].rearrange("p (c t) -> p c t", t=128)
                            kb.op(DVE, lambda t13=t13, L3=L3: nc.vector.tensor_tensor(
                                out=t13, in0=t13, in1=L3[:, :, 127:128].broadcast_to([128, 4, 128]), op=ALU.add),
                                [t1, Lf], [t1])
                            mi, ei = 64, 0
                            Lsrc = t1
                            L3 = t13
                        kb.op(ACT, lambda L3=L3, mi=mi, sc=sc: nc.scalar.activation(
                            out=sc.t[:, 0:4], in_=L3[:, :, mi], func=AF.Exp, scale=CDEC), [Lsrc], [sc])
                        kb.op(ACT, lambda L3=L3, ei=ei, sc=sc: nc.scalar.activation(
                            out=sc.t[:, 4:8], in_=L3[:, :, ei], func=AF.Exp, scale=CDEC), [Lsrc], [sc])
                        kb.op(DVE, lambda L3=L3, mi=mi, Lm3=Lm3: nc.vector.tensor_tensor(
                            out=Lm3, in0=L3, in1=L3[:, :, mi:mi + 1].broadcast_to([128, 4, 128]), op=ALU.subtract),
                            [Lsrc], [Lm])
                        kb.op(ACT, lambda Lm3=Lm3, ei=ei, sc=sc: nc.scalar.activation(
                            out=sc.t[:, 8:12], in_=Lm3[:, :, ei], func=AF.Exp, scale=CDEC), [Lm], [sc])
                        kb.op(ACT, lambda: nc.scalar.activation(out=E1.t[:], in_=Lm.t[:], func=AF.Exp, scale=CDEC),
                              [Lm], [E1])
                        kb.op(ACT, lambda: nc.scalar.activation(out=E3.t[:], in_=Lm.t[:], func=AF.Exp, scale=-CDEC),
                              [Lm], [E3])
                        kb.op(POOL, lambda: nc.gpsimd.tensor_tensor(out=t2.t[:], in0=Lm.t[:], in1=sig.t[:],
                                                                    op=ALU.subtract), [Lm, sig], [t2])
                        kb.op(ACT, lambda: nc.scalar.activation(out=E2.t[:], in_=t2.t[:], func=AF.Exp, scale=CDEC),
                              [t2], [E2])
                        kb.op(DVE, lambda: nc.vector.tensor_scalar(out=kd.t[:], in0=aa.t[:], scalar1=-1.0,
                                                                   scalar2=pv("k_a", hp), op0=ALU.add, op1=ALU.mult),
                              [aa, pvec], [kd])
                        kb.op(DVE, lambda: nc.vector.scalar_tensor_tensor(out=kd.t[:], in0=kd.t[:], scalar=1.0,
                                                                          in1=k_c.t[:], op0=ALU.add, op1=ALU.mult),
                              [kd, k_c], [kd])
                        kb.op(POOL, lambda: nc.gpsimd.tensor_tensor(out=bb.t[:], in0=kkn.t[:], in1=aa.t[:],
                                                                    op=ALU.mult), [kkn, aa], [bb])

                        def v3(b_):
                            return b_.t[:].rearrange("p (c t) -> p c t", t=128)
                        kb.op(DVE, lambda fm=fm: nc.vector.tensor_tensor(out=fm.t[:, :, 0, :], in0=v3(kkn), in1=v3(E2),
                                                                        op=ALU.mult), [kkn, E2], [fm])
                        kb.op(POOL, lambda fm=fm: nc.gpsimd.tensor_tensor(out=fm.t[:, :, 1, :], in0=v3(r_c), in1=v3(E1),
                                                                         op=ALU.mult), [r_c, E1], [fm])
                        kb.op(DVE, lambda fm=fm: nc.vector.tensor_tensor(out=fm.t[:, :, 2, :], in0=v3(bb), in1=v3(E3),
                                                                        op=ALU.mult), [bb, E3], [fm])
                        kb.op(POOL, lambda fm=fm: nc.gpsimd.tensor_tensor(out=fm.t[:, :, 3, :], in0=v3(kd), in1=v3(E3),
                                                                         op=ALU.mult), [kd, E3], [fm])
                        for q_ in range(2):
                            pt_ = ptb[(q_ + 1) % 2]
                            for c in range(4):
                                kb.op(PE, lambda c=c, pt_=pt_, q_=q_, fm=fm: nc.tensor.transpose(
                                    pt_.t[:, c, :], fm.t[:, c, 2 + q_, :], ident_b), [fm, cb], [pt_])
                            kb.op(DVE if q_ == 0 else ACT,
                                  (lambda pt_=pt_, q_=q_, tm=tm: nc.vector.tensor_copy(out=tm.t[:, :, q_, :], in_=pt_.t[:, 0:4, :]))
                                  if q_ == 0 else
                                  (lambda pt_=pt_, q_=q_, tm=tm: nc.scalar.copy(out=tm.t[:, :, q_, :], in_=pt_.t[:, 0:4, :])),
                                  [pt_], [tm])
                        kb.dma(SP, FMs.t[blk, hp, d], fm.t[:], reads=[fm], writes=[FMs])
                        kb.dma(SP, TMs.t[blk, hp, d], tm.t[:], reads=[tm], writes=[TMs])
                        kb.dma(SP, SCs.t[blk, hp, d], sc.t[:], reads=[sc], writes=[SCs])
        kb.barrier()

        stage_gate(2)
        with contextlib.ExitStack() as st:
            NU = 3
            fmi = [Buf(kb.sb("fmi%d" % i, [128, 4, 4, 128], BF16, st)) for i in range(NU)]
            tmi = [Buf(kb.sb("tmi%d" % i, [128, 4, 2, 128], BF16, st)) for i in range(NU)]
            tvi = [Buf(kb.sb("tvi%d" % i, [128, 4, 128], BF16, st)) for i in range(NU)]
            sci = [Buf(kb.sb("sci%d" % i, [128, 12], F32, st)) for i in range(NU)]
            MBK = [Buf(kb.sb("MBK%d" % c, [128, 2, 4, 128], BF16, st)) for c in range(4)]
            NA = [[Buf(kb.sb("NA%d_%d" % (l, c), [128, 2, 2, 128], BF16, st)) for c in range(4)] for l in range(2)]
            AT0 = [Buf(kb.sb("AT0_%d" % c, [128, 2, 128], BF16, st)) for c in range(4)]
            PT = [Buf(kb.sb("PT%d" % c, [128, 2, 128], BF16, st)) for c in range(4)]
            Zs = [[Buf(kb.sb("Z%d_%d" % (hp, d), [128, 64], F32, st)) for d in range(2)] for hp in range(16)]
            Zb = Buf(kb.sb("Zb", [128, 128], BF16, st))
            kb.op(POOL, lambda: nc.gpsimd.memset(Zb.t[:], 0.0), [], [Zb])
            Xn = Buf(kb.sb("Xn", [128, 2, 64], BF16, st))
            UT = Buf(kb.sb("UT", [128, 2, 64], BF16, st))
            zt = Buf(kb.sb("zt", [128, 64], F32, st))
            yo = [Buf(kb.sb("yo%d" % i, [128, TB], F32, st)) for i in range(2)]
            sti = Buf(kb.sb("sti", [64, 128], F32, st))
            sto = Buf(kb.sb("sto", [64, 128], F32, st))
            pin = [Buf(kb.ps("pin%d" % c, [128, 512], F32, st)) for c in range(4)]
            pX = Buf(kb.ps("pX", [128, 512], F32, st))
            pU = Buf(kb.ps("pU", [128, 512], F32, st))
            pY = Buf(kb.ps("pY", [128, 512], F32, st))
            pZ = Buf(kb.ps("pZ", [128, 512], F32, st))

            def scan_unit(ui, blk, hp, d, chunk_order, seq_starts, seq_ends):
                fm, tm, tv, sc = fmi[ui % NU], tmi[ui % NU], tvi[ui % NU], sci[ui % NU]
                kb.dma(SP, fm.t[:], FMs.t[blk, hp, d], reads=[FMs], writes=[fm])
                kb.dma(SP, tm.t[:], TMs.t[blk, hp, d], reads=[TMs], writes=[tm])
                kb.dma(SP, tv.t[:], TVs.t[blk, hp], reads=[TVs], writes=[tv])
                kb.dma(SP, sc.t[:], SCs.t[blk, hp, d], reads=[SCs], writes=[sc])
                Z = Zs[hp][d]
                mk = masks.t[:, d, 0:4, :]
                for c in range(4):
                    for e in range(2):
                        pb = pin[(2 * c + e) % 4]
                        rs = slice(e * 64, e * 64 + 64)
                        for q_ in range(2):
                            kb.op(PE, lambda pb=pb, rs=rs, c=c, q_=q_: nc.tensor.matmul(
                                pb.t[:, q_ * 256:(q_ + 1) * 256], lhsT=fm.t[rs, c, 2 + q_, :],
                                rhs=fm.t[rs, c, 0:2, :], start=True, stop=True), [fm], [pb])
                        kb.op(DVE, lambda pb=pb, c=c, e=e: nc.vector.tensor_tensor(
                            out=MBK[c].t[:, e, :, :], in0=pb.t[:].rearrange("p (a b) -> p a b", b=128), in1=mk,
                            op=ALU.mult), [pb, masks], [MBK[c]])
                for c in range(4):
                    for e in range(2):
                        pb = pin[(2 * c + e) % 4]
                        rs = slice(e * 64, e * 64 + 64)
                        kb.op(PE, lambda pb=pb, rs=rs, c=c, e=e: nc.tensor.matmul(
                            pb.t[:, 0:128], lhsT=fm.t[rs, c, 0, :], rhs=fm.t[rs, c, 2, :],
                            start=True, stop=True), [fm], [pb])
                        kb.op(DVE, lambda pb=pb, c=c, e=e: nc.vector.tensor_tensor(
                            out=AT0[c].t[:, e, :], in0=pb.t[:, 0:128],
                            in1=masks.t[:, d, 4, :], op=ALU.mult), [pb, masks], [AT0[c]])
                    kb.op(POOL, lambda c=c: nc.gpsimd.tensor_tensor(
                        out=PT[c].t[:], in0=cb.t[:, 0:1, :].broadcast_to([128, 2, 128]), in1=MBK[c].t[:, :, 0, :],
                        op=ALU.subtract), [cb, MBK[c]], [PT[c]])
                for lev in range(6):
                    cur = NA[lev % 2]
                    prv = NA[(lev + 1) % 2]
                    for c in range(4):
                        pb = pin[c]
                        for e in range(2):
                            if lev == 0:
                                Np, Ap = MBK[c].t[:, e, 0, :], AT0[c].t[:, e, :]
                                rd = [MBK[c], AT0[c]]
                            else:
                                Np, Ap = prv[c].t[:, e, 0, :], prv[c].t[:, e, 1, :]
                                rd = [prv[c]]
                            kb.op(PE, lambda pb=pb, e=e, Np=Np, Ap=Ap: nc.tensor.matmul(
                                pb.t[:, e * 256:e * 256 + 128], lhsT=Ap, rhs=Np, start=True, stop=True), rd, [pb])
                            kb.op(PE, lambda pb=pb, e=e, Np=Np, Ap=Ap: nc.tensor.matmul(
                                pb.t[:, e * 256 + 128:e * 256 + 256], lhsT=Np, rhs=Ap, start=True, stop=True), rd, [pb])
                    for c in range(4):
                        pb = pin[c]
                        if c % 2 == 0:
                            kb.op(ACT, lambda pb=pb, c=c, cur=cur: nc.scalar.copy(
                                out=cur[c].t[:].rearrange("p a b c -> p (a b c)"), in_=pb.t[:]), [pb], [cur[c]])
                        else:
                            kb.op(DVE, lambda pb=pb, c=c, cur=cur: nc.vector.tensor_copy(
                                out=cur[c].t[:].rearrange("p a b c -> p (a b c)"), in_=pb.t[:]), [pb], [cur[c]])
                    for c in range(4):
                        pb = pin[c]
                        for e in range(2):
                            kb.op(PE, lambda pb=pb, e=e, c=c, cur=cur: nc.tensor.matmul(
                                pb.t[:, e * 128:(e + 1) * 128], lhsT=cur[c].t[:, e, 1, :], rhs=PT[c].t[:, e, :],
                                start=True, stop=True), [cur[c], PT[c]], [pb])
                    for c in range(4):
                        pb = pin[c]
                        kb.op(DVE, lambda pb=pb, c=c: nc.vector.tensor_tensor(
                            out=PT[c].t[:], in0=pb.t[:, 0:256].rearrange("p (a b) -> p a b", b=128), in1=PT[c].t[:],
                            op=ALU.add), [pb, PT[c]], [PT[c]])
                yb = yo[ui % 2]
                for c in chunk_order:
                    if c in seq_starts:
                        kind, arg = seq_starts[c]
                        if kind == "zero":
                            kb.op(POOL, lambda Z=Z: nc.gpsimd.memset(Z.t[:], 0.0), [], [Z])
                        elif kind == "load":
                            kb.dma(SP, sti.t[:].rearrange("v (e k) -> v e k", e=2),
                                   st_lat[d, 2 * hp:2 * hp + 2].rearrange("e v k -> v e k"), reads=[DX], writes=[sti])
                            kb.op(PE, lambda: nc.tensor.transpose(pZ.t[:, 0:64], sti.t[:], ident_f.t[0:64, 0:64]),
                                  [sti, ident_f], [pZ])
                            kb.op(DVE, lambda Z=Z: nc.vector.tensor_copy(out=Z.t[:], in_=pZ.t[:, 0:64]), [pZ], [Z])
                    for e in range(2):
                        rs = slice(e * 64, e * 64 + 64)
                        kb.op(ACT, lambda Z=Z, c=c, e=e, rs=rs: nc.scalar.activation(
                            out=Zb.t[rs, e * 64:(e + 1) * 64], in_=Z.t[rs, :], func=AF.Identity,
                            scale=sc.t[rs, c:c + 1]), [Z, sc], [Zb])
                    kb.op(PE, lambda c=c: nc.tensor.matmul(
                        pX.t[:, 0:128], lhsT=fm.t[:, c, 0, :], rhs=Zb.t[:], start=True, stop=False), [fm, Zb], [pX])
                    for e in range(2):
                        kb.op(PE, lambda e=e, c=c: nc.tensor.matmul(
                            pX.t[:, e * 64:(e + 1) * 64], lhsT=MBK[c].t[:, e, 2, :], rhs=tv.t[:, c, e * 64:(e + 1) * 64],
                            start=False, stop=(e == 1)), [MBK[c], tv], [pX])
                    kb.op(ACT, lambda: nc.scalar.mul(out=Xn.t[:].rearrange("p a b -> p (a b)"), in_=pX.t[:, 0:128],
                                                     mul=-1.0), [pX], [Xn])
                    for e in range(2):
                        kb.op(PE, lambda e=e, c=c: nc.tensor.matmul(
                            pU.t[:, e * 64:(e + 1) * 64], lhsT=PT[c].t[:, e, :], rhs=Xn.t[:, e, :], start=True,
                            stop=True), [PT[c], Xn], [pU])
                    kb.op(DVE, lambda: nc.vector.tensor_copy(out=UT.t[:].rearrange("p a b -> p (a b)"),
                                                             in_=pU.t[:, 0:128]), [pU], [UT])
                    kb.op(PE, lambda c=c: nc.tensor.matmul(
                        pY.t[:, 0:128], lhsT=Zb.t[:], rhs=fm.t[:, c, 1, :], start=True, stop=False), [Zb, fm], [pY])
                    for e in range(2):
                        rs = slice(e * 64, e * 64 + 64)
                        kb.op(PE, lambda e=e, rs=rs, c=c: nc.tensor.matmul(
                            pY.t[rs, 0:128], lhsT=UT.t[:, e, :], rhs=MBK[c].t[:, e, 1, :], start=False, stop=False),
                            [UT, MBK[c]], [pY])
                        kb.op(PE, lambda e=e, rs=rs, c=c: nc.tensor.matmul(
                            pY.t[rs, 0:128], lhsT=tv.t[:, c, e * 64:(e + 1) * 64], rhs=MBK[c].t[:, e, 3, :],
                            start=False, stop=(e == 1)), [tv, MBK[c]], [pY])
                    kb.op(ACT, lambda c=c, yb=yb: nc.scalar.copy(out=yb.t[:, c * 128:(c + 1) * 128], in_=pY.t[:, 0:128]),
                          [pY], [yb])
                    for e in range(2):
                        rs = slice(e * 64, e * 64 + 64)
                        kb.op(PE, lambda e=e, rs=rs, c=c: nc.tensor.matmul(
                            pZ.t[rs, 0:64], lhsT=tm.t[:, c, 0, e * 64:(e + 1) * 64], rhs=UT.t[:, e, :], start=True,
                            stop=False), [tm, UT], [pZ])
                        kb.op(PE, lambda e=e, rs=rs, c=c: nc.tensor.matmul(
                            pZ.t[rs, 0:64], lhsT=tm.t[:, c, 1, e * 64:(e + 1) * 64], rhs=tv.t[:, c, e * 64:(e + 1) * 64],
                            start=False, stop=True), [tm, tv], [pZ])
                    kb.op(DVE, lambda c=c: nc.vector.tensor_scalar(out=zt.t[:], in0=pZ.t[:, 0:64],
                                                                   scalar1=sc.t[:, 8 + c:9 + c], scalar2=None,
                                                                   op0=ALU.mult), [pZ, sc], [zt])
                    kb.op(DVE, lambda c=c, Z=Z: nc.vector.scalar_tensor_tensor(
                        out=Z.t[:], in0=Z.t[:], scalar=sc.t[:, 4 + c:5 + c], in1=zt.t[:], op0=ALU.mult, op1=ALU.add),
                        [Z, sc, zt], [Z])
                    if c in seq_ends:
                        sq_ = seq_ends[c]
                        kb.op(PE, lambda Z=Z: nc.tensor.transpose(pZ.t[0:64, 128:256], Z.t[:], ident_f.t[:]),
                              [Z, ident_f], [pZ])
                        kb.op(DVE, lambda: nc.vector.tensor_copy(out=sto.t[:], in_=pZ.t[0:64, 128:256]), [pZ], [sto])
                        kb.dma(SP, st_out[sq_, d, 2 * hp:2 * hp + 2].rearrange("e v k -> v e k"),
                               sto.t[:].rearrange("v (e k) -> v e k", e=2), reads=[sto], writes=[DX])
                kb.dma(SP, Ys.t[blk, hp, d], yb.t[:], reads=[yb], writes=[Ys])

            ui = 0
            for blk in range(NBC):
                for hp in range(16):
                    for d in range(2):
                        order = [0, 1, 2, 3] if d == 0 else [3, 2, 1, 0]
                        if d == 0:
                            starts = {0: ("zero", None), 2: ("zero", None)}
                            ends = {1: blk * 2, 3: blk * 2 + 1}
                        else:
                            starts = {3: ("zero", None), 1: ("zero", None)}
                            ends = {2: blk * 2 + 1, 0: blk * 2}
                        scan_unit(ui, blk, hp, d, order, starts, ends)
                        ui += 1
            for s_ in range(NBL):
                for hp in range(16):
                    for d in range(2):
                        if d == 0:
                            blk = NBC + s_
                            order = [0, 1, 2, 3]
                            starts = {0: ("load", None)} if s_ == 0 else {}
                        else:
                            blk = NBC + NBL - 1 - s_
                            order = [3, 2, 1, 0]
                            starts = {3: ("load", None)} if s_ == 0 else {}
                        scan_unit(ui, blk, hp, d, order, starts, {})
                        ui += 1
        kb.barrier()

        stage_gate(3)
        with contextlib.ExitStack() as st:
            ccs = Buf(kb.sb("ccs", [128, 2, 256], BF16, st))
            scs = Buf(kb.sb("scs", [128, 2, 256], BF16, st))
            kb.dma(SP, ccs.t[:], cc_d.rearrange("(k p) n -> p k n", p=128), reads=[DX], writes=[ccs])
            kb.dma(SP, scs.t[:], scn_d.rearrange("(k p) n -> p k n", p=128), reads=[DX], writes=[scs])
            B12 = [Buf(kb.sb("B12_%d" % i, [128, 2, 2, TB], BF16, st)) for i in range(2)]
            fo = [Buf(kb.sb("fo%d" % i, [128, TB], BF16, st)) for i in range(2)]
            pf = [Buf(kb.ps("pf%d" % i, [128, TB], F32, st)) for i in range(6)]
            pfi = [0]

            def nextpf():
                b_ = pf[pfi[0] % 6]
                pfi[0] += 1
                return b_

            def fourier_seq(tok0, T, ct_d, st_d, ctb, stb, xbg):
                nt = T // 128
                TP = min(TB, T)
                for g in range(4):
                    kb.dma(SP, xbg[g].t[:, 0:nt, :],
                           XBs.t[tok0:tok0 + T, g * 256:(g + 1) * 256].rearrange("(n p) c -> p n c", p=128),
                           reads=[XBs], writes=[xbg[g]])
                for tb_ in range(T // TP):
                    kb.dma(SP, ctb.t[:, 0:nt, 0:TP], ct_d[:, tb_ * TP:(tb_ + 1) * TP].rearrange("(n p) c -> p n c", p=128),
                           reads=[DX], writes=[ctb])
                    kb.dma(SP, stb.t[:, 0:nt, 0:TP], st_d[:, tb_ * TP:(tb_ + 1) * TP].rearrange("(n p) c -> p n c", p=128),
                           reads=[DX], writes=[stb])
                    for g in range(4):
                        bb_ = B12[g % 2]
                        for cc_ in range(2):
                            for q_, mat in enumerate([ctb, stb]):
                                pb = nextpf()
                                for tt in range(nt):
                                    kb.op(PE, lambda pb=pb, tt=tt, cc_=cc_, mat=mat, g=g: nc.tensor.matmul(
                                        pb.t[:, 0:TP], lhsT=xbg[g].t[:, tt, cc_ * 128:(cc_ + 1) * 128],
                                        rhs=mat.t[:, tt, 0:TP], start=(tt == 0), stop=(tt == nt - 1)),
                                        [xbg[g], mat], [pb])
                                if q_ == 0:
                                    kb.op(ACT, lambda pb=pb, cc_=cc_, bb_=bb_: nc.scalar.copy(
                                        out=bb_.t[:, cc_, 0, 0:TP], in_=pb.t[:, 0:TP]), [pb], [bb_])
                                else:
                                    kb.op(DVE, lambda pb=pb, cc_=cc_, bb_=bb_: nc.vector.tensor_copy(
                                        out=bb_.t[:, cc_, 1, 0:TP], in_=pb.t[:, 0:TP]), [pb], [bb_])
                        for cp in range(2):
                            pb = nextpf()
                            n_ = 0
                            for cc_ in range(2):
                                for q_, mat in enumerate([ccs, scs]):
                                    kb.op(PE, lambda pb=pb, cc_=cc_, q_=q_, mat=mat, cp=cp, n_=n_, bb_=bb_: nc.tensor.matmul(
                                        pb.t[:, 0:TP], lhsT=mat.t[:, cc_, cp * 128:(cp + 1) * 128],
                                        rhs=bb_.t[:, cc_, q_, 0:TP], start=(n_ == 0), stop=(n_ == 3)), [mat, bb_], [pb])
                                    n_ += 1
                            f_ = fo[cp]
                            kb.op(ACT if cp == 0 else DVE,
                                  (lambda pb=pb, f_=f_: nc.scalar.copy(out=f_.t[:, 0:TP], in_=pb.t[:, 0:TP])) if cp == 0 else
                                  (lambda pb=pb, f_=f_: nc.vector.tensor_copy(out=f_.t[:, 0:TP], in_=pb.t[:, 0:TP])),
                                  [pb], [f_])
                            t0 = tok0 + tb_ * TP
                            kb.dma(SP, FSs.t[g * 2 + cp, :, t0:t0 + TP], f_.t[:, 0:TP], reads=[f_], writes=[FSs])

            ctb = Buf(kb.sb("ctb", [128, NTL, TB], BF16, st))
            stb = Buf(kb.sb("stb", [128, NTL, TB], BF16, st))
            xbg = [Buf(kb.sb("xbg%d" % g, [128, NTL, 256], BF16, st)) for g in range(4)]
            for s_ in range(NSEQ):
                fourier_seq(s_ * 256, 256, ct256_d, st256_d, ctb, stb, xbg)
            fourier_seq(NBC * TB, TL, ctL_d, stL_d, ctb, stb, xbg)
        kb.barrier()

        stage_gate(4)
        with contextlib.ExitStack() as st:
            xtm = [Buf(kb.sb("xtm%d" % i, [128, D], F32, st)) for i in range(2)]
            xT = [Buf(kb.sb("xT%d" % i, [128, TB], F32, st)) for i in range(16)]
            hT = [Buf(kb.sb("hT%d" % i, [128, TB], BF16, st)) for i in range(16)]
            gA = [Buf(kb.sb("gA%d" % i, [128, TB], BF16, st)) for i in range(16)]
            act = [Buf(kb.sb("act%d" % i, [128, TB], BF16, st)) for i in range(FC)]
            ya, gB, FB = act[0:16], act[16:32], act[32:40]
            sqb = [Buf(kb.sb("sqb%d" % i, [128, TB], BF16, st)) for i in range(2)]
            tmpf = [Buf(kb.sb("tmpf%d" % i, [128, TB], F32, st)) for i in range(2)]
            rstd = Buf(kb.sb("rstd", [128, TB], F32, st))
            F = lambda n: Buf(kb.sb(n, [128, TB], F32, st))
            y0, y1, bo, g_, dl, sq2 = [F("f4_%d" % i) for i in range(6)]
            ug, uv = y0, y1
            ws = WStream(make_wpool(st, 2, FC * 128 + 128))
            wsd = ws
            ptr = [Buf(kb.ps("ptr%d" % i, [128, 4, 128], F32, st)) for i in range(2)]
            pp = [Buf(kb.ps("pp%d" % i, [128, TB], F32, st)) for i in range(5)]
            ppi = [0]

            def nextp():
                b_ = pp[ppi[0] % 5]
                ppi[0] += 1
                return b_

            def proj(wsrc, kc, ncol_total, gw, rhs, evac, wstream):
                for g in range(ncol_total // gw):
                    wb, wv = wstream.load(wsrc[:, :, g * gw:(g + 1) * gw], kc, gw)
                    for j in range(gw // 128):
                        pb = nextp()
                        for k in range(kc):
                            kb.op(PE, lambda k=k, j=j, pb=pb, wv=wv: nc.tensor.matmul(
                                pb.t[:], lhsT=wv[:, k, j * 128:(j + 1) * 128], rhs=rhs[k].t[:], start=(k == 0),
                                stop=(k == kc - 1)), [wb, rhs[k]], [pb])
                        evac(g * (gw // 128) + j, pb)

            wv_in = w_in.rearrange("(k p) n -> p k n", p=128)
            wv_oa = w_out_a.rearrange("(k p) n -> p k n", p=128)
            wv_fo = w_fourier.rearrange("(k p) n -> p k n", p=128)
            wv_o = w_out.rearrange("(k p) n -> p k n", p=128)
            wv_fi2 = ffn_w_in.rearrange("(k p) n -> p k n", p=128)
            wv_fd = ffn_w_down.rearrange("(k p) n -> p k n", p=128)
            for blk in range(NBLK):
                r, rl = blk_info(blk)
                load_xT(blk, xtm, xT, ptr)
                pss = nextp()
                rstd_of(xT, sqb, pss, rstd)
                norm_mod(xT, rstd, hT, tmpf, 0, r)
                for hp in range(16):
                    kb.dma(SP, y0.t[:], Ys.t[blk, hp, 0], reads=[Ys], writes=[y0])
                    kb.dma(SP, y1.t[:], Ys.t[blk, hp, 1], reads=[Ys], writes=[y1])
                    kb.dma(SP, bo.t[:], BGs.t[blk, hp, 0], reads=[BGs], writes=[bo])
                    kb.dma(SP, g_.t[:], BGs.t[blk, hp, 1], reads=[BGs], writes=[g_])
                    kb.op(POOL, lambda: nc.gpsimd.tensor_tensor(out=y0.t[:], in0=y0.t[:], in1=y1.t[:], op=ALU.add),
                          [y0, y1], [y0])
                    pmn = nextp()
                    kb.op(PE, lambda pmn=pmn: nc.tensor.matmul(pmn.t[:], lhsT=blk64.t[:], rhs=y0.t[:], start=True,
                                                               stop=True), [blk64, y0], [pmn])
                    kb.op(DVE, lambda pmn=pmn: nc.vector.tensor_tensor(out=dl.t[:], in0=y0.t[:], in1=pmn.t[:],
                                                                      op=ALU.subtract), [y0, pmn], [dl])
                    kb.op(ACT, lambda: nc.scalar.activation(out=sq2.t[:], in_=dl.t[:], func=AF.Square), [dl], [sq2])
                    pvr = nextp()
                    kb.op(PE, lambda pvr=pvr: nc.tensor.matmul(pvr.t[:], lhsT=blk64.t[:], rhs=sq2.t[:], start=True,
                                                               stop=True), [blk64, sq2], [pvr])
                    kb.op(DVE, lambda pvr=pvr: nc.vector.tensor_scalar(out=sq2.t[:], in0=pvr.t[:], scalar1=GN_EPS,
                                                                      scalar2=None, op0=ALU.add), [pvr], [sq2])
                    kb.op(ACT, lambda: nc.scalar.activation(out=sq2.t[:], in_=sq2.t[:], func=AF.Ln), [sq2], [sq2])
                    kb.op(ACT, lambda: nc.scalar.activation(out=sq2.t[:], in_=sq2.t[:], func=AF.Exp, scale=-0.5),
                          [sq2], [sq2])
                    kb.op(POOL, lambda: nc.gpsimd.tensor_tensor(out=dl.t[:], in0=dl.t[:], in1=sq2.t[:], op=ALU.mult),
                          [dl, sq2], [dl])
                    kb.op(ACT, lambda hp=hp: nc.scalar.activation(out=dl.t[:], in_=dl.t[:], func=AF.Identity,
                                                                  bias=pv("lnb", hp), scale=pv("lng", hp)),
                          [dl, pvec], [dl])
                    kb.op(DVE, lambda: nc.vector.tensor_tensor(out=dl.t[:], in0=dl.t[:], in1=bo.t[:], op=ALU.add),
                          [dl, bo], [dl])
                    kb.op(DVE, lambda hp=hp: nc.vector.tensor_tensor(out=ya[hp].t[:], in0=dl.t[:], in1=g_.t[:],
                                                                    op=ALU.mult), [dl, g_], [ya[hp]])
                proj(wv_in[:, :, 7168:9216], 16, 2048, 256, hT,
                     lambda c, pb: kb.op(ACT, lambda: nc.scalar.activation(out=gA[c].t[:], in_=pb.t[:],
                                                                           func=AF.Sigmoid), [pb], [gA[c]]), ws)
                proj(wv_in[:, :, 9216:11264], 16, 2048, 256, hT,
                     lambda c, pb: kb.op(ACT, lambda: nc.scalar.activation(out=gB[c].t[:], in_=pb.t[:],
                                                                           func=AF.Sigmoid), [pb], [gB[c]]), ws)
                proj(wv_oa, 16, 2048, 256, ya,
                     lambda c, pb: kb.op(DVE, lambda: nc.vector.tensor_tensor(out=gA[c].t[:], in0=pb.t[:],
                                                                              in1=gA[c].t[:], op=ALU.mult),
                                         [pb, gA[c]], [gA[c]]), ws)
                for c8 in range(8):
                    kb.dma(SP, FB[c8].t[:], FSs.t[c8, :, blk * TB:(blk + 1) * TB], reads=[FSs], writes=[FB[c8]])

                def ev_f(c, pb):
                    kb.op(DVE, lambda: nc.vector.tensor_tensor(out=gB[c].t[:], in0=pb.t[:], in1=gB[c].t[:],
                                                               op=ALU.mult), [pb, gB[c]], [gB[c]])
                    kb.op(POOL, lambda: nc.gpsimd.tensor_tensor(out=gA[c].t[:], in0=gA[c].t[:], in1=gB[c].t[:],
                                                                op=ALU.add), [gA[c], gB[c]], [gA[c]])
                proj(wv_fo, 8, 2048, 512, FB, ev_f, ws)
                proj(wv_o, 16, 2048, 256, gA,
                     lambda c, pb: kb.op(DVE, lambda: nc.vector.scalar_tensor_tensor(
                         out=xT[c].t[:], in0=pb.t[:], scalar=modcol(2, c, r), in1=xT[c].t[:], op0=ALU.mult,
                         op1=ALU.add), [pb, modv, xT[c]], [xT[c]]), ws)
                pss = nextp()
                rstd_of(xT, sqb, pss, rstd)
                norm_mod(xT, rstd, hT, tmpf, 1, r)
                for f in range(FC):
                    wb, wv = ws.load([wv_fi2[:, :, s3 * DFF + f * 128:s3 * DFF + (f + 1) * 128] for s3 in range(2)], 16, 256)
                    pg, pu = nextp(), nextp()
                    for s_, pb in enumerate([pg, pu]):
                        for k in range(16):
                            kb.op(PE, lambda k=k, s_=s_, pb=pb, wv=wv: nc.tensor.matmul(
                                pb.t[:], lhsT=wv[:, k, s_ * 128:(s_ + 1) * 128], rhs=hT[k].t[:], start=(k == 0),
                                stop=(k == 15)), [wb, hT[k]], [pb])
                    conv3(pg, ug, "fcw", "fcb", f, 86, rl)
                    conv3(pu, uv, "fcw", "fcb", FC + f, 86, rl)
                    kb.op(ACT, lambda: nc.scalar.activation(out=ug.t[:], in_=ug.t[:], func=AF.Silu), [ug], [ug])
                    kb.op(POOL, lambda f=f: nc.gpsimd.tensor_tensor(out=act[f].t[:], in0=ug.t[:], in1=uv.t[:],
                                                                    op=ALU.mult), [ug, uv], [act[f]])
                proj(wv_fd, FC, 2048, 128, act,
                     lambda c, pb: kb.op(DVE, lambda: nc.vector.scalar_tensor_tensor(
                         out=xT[c].t[:], in0=pb.t[:], scalar=modcol(5, c, r), in1=xT[c].t[:], op0=ALU.mult,
                         op1=ALU.add), [pb, modv, xT[c]], [xT[c]]), wsd)
                pss = nextp()
                rstd_of(xT, sqb, pss, rstd)
                for c in range(16):
                    kb.op(DVE, lambda c=c: nc.vector.scalar_tensor_tensor(
                        out=xT[c].t[:], in0=xT[c].t[:], scalar=pv("gf", c), in1=rstd.t[:], op0=ALU.mult,
                        op1=ALU.mult), [xT[c], pvec, rstd], [xT[c]])
                for t in range(4):
                    ob = xtm[t % 2]
                    for g in range(4):
                        pb = ptr[(t * 4 + g) % 2]
                        for j in range(4):
                            c = g * 4 + j
                            kb.op(PE, lambda pb=pb, j=j, c=c, t=t: nc.tensor.transpose(
                                pb.t[:, j, :], xT[c].t[:, t * 128:(t + 1) * 128], ident_f.t[:]), [xT[c], ident_f], [pb])
                        if g % 2 == 0:
                            kb.op(ACT, lambda pb=pb, g=g, ob=ob: nc.scalar.copy(
                                out=ob.t[:, g * 512:(g + 1) * 512], in_=pb.t[:].rearrange("p a b -> p (a b)")),
                                [pb], [ob])
                        else:
                            kb.op(DVE, lambda pb=pb, g=g, ob=ob: nc.vector.tensor_copy(
                                out=ob.t[:, g * 512:(g + 1) * 512], in_=pb.t[:].rearrange("p a b -> p (a b)")),
                                [pb], [ob])
                    r0 = blk * TB + t * 128
                    kb.dma(SP, y_all[r0:r0 + 128, :], ob.t[:], reads=[ob], writes=[DX])
        kb.final_wait()
        print("instructions:", kb.nins, "ops:", getattr(kb, "nops", 0))
    return nc


def fm_vec(v):
    v = np.asarray(v, np.float32).reshape(-1, 128)
    return np.ascontiguousarray(v.T)


_CACHE = {}


def host_consts(TL):
    if TL in _CACHE:
        return _CACHE[TL]
    c = {}
    c["ident_f"] = np.eye(128, dtype=np.float32)
    cbf = np.zeros((128, 6, 128), np.float32)
    cbf[:, 0, :] = np.eye(128)
    blk = np.zeros((128, 128), np.float32)
    blk[:64, :64] = 1
    blk[64:, 64:] = 1
    cbf[:, 1, :] = blk
    cbf[:, 2, :] = 1
    c["cbf"] = cbf.astype(NPBF)
    c["blk64"] = (blk / 64.0).astype(np.float32)
    idx = np.arange(128)
    m = np.zeros((128, 2, 6, 128), np.float32)
    for d in range(2):
        before = (idx[:, None] > idx[None, :]) if d == 1 else (idx[:, None] < idx[None, :])
        beq = before | np.eye(128, dtype=bool)
        m[:, d, 0] = before
        m[:, d, 1] = beq
        m[:, d, 2] = before
        m[:, d, 3] = beq
        m[:, d, 4] = before.T
    c["masks"] = m.astype(NPBF)
    rm = np.ones((128, TB), np.float32)
    rm[:, ::128] = 0
    c["rmask"] = rm

    def dft(T):
        t = np.arange(T, dtype=np.float64)
        ang = 2 * np.pi * ((t[:, None] * t[None, :]) % T) / T
        return np.cos(ang), np.sin(ang)
    c256, s256 = dft(256)
    c["ct256"] = (c256 / 16.0).astype(NPBF)
    c["st256"] = (s256 / 16.0).astype(NPBF)
    cL, sL = dft(TL)
    c["ctL"] = (cL / np.sqrt(TL)).astype(NPBF)
    c["stL"] = (sL / np.sqrt(TL)).astype(NPBF)
    c["ccm"] = (c256 / 16.0).astype(NPBF)
    c["scn"] = (-s256 / 16.0).astype(NPBF)
    _CACHE[TL] = c
    return c


_PROG = {}


def kernel(x_prompt, x_sample, state_rwkv, c, c_ctx, ada_w, ada_b, norm_mix_g, w_in,
           rkv_conv_w, rkv_conv_b, decay_up, decay_base, iclr_up, iclr_base, gate_up,
           k_k, k_a, r_k, lnx_g, lnx_b, w_out_a, w_fourier, w_out, norm_ffn_g,
           ffn_w_in, ffn_conv_w, ffn_conv_b, ffn_w_down, final_norm_g):
    f = lambda a: np.asarray(a, np.float32)
    x_prompt, x_sample, state_rwkv, c, c_ctx = f(x_prompt), f(x_sample), f(state_rwkv), f(c), f(c_ctx)
    B, S, _ = x_prompt.shape
    NBAT, TL, _ = x_sample.shape
    NCORE = 8
    NSEQ = B // NCORE
    key = (NSEQ, TL)
    if key not in _PROG:
        _PROG[key] = build_program(NSEQ, TL)
    nc = _PROG[key]
    consts = host_consts(TL)
    pvec = np.concatenate([
        fm_vec(f(ada_b)[0]), fm_vec(f(norm_mix_g)[0]), fm_vec(f(norm_ffn_g)[0]), fm_vec(f(final_norm_g)),
        fm_vec(f(rkv_conv_w)[0, 0]), fm_vec(f(rkv_conv_w)[0, 1]), fm_vec(f(rkv_conv_w)[0, 2]),
        fm_vec(f(rkv_conv_b)[0]),
        fm_vec(f(decay_base)[0, 0]), fm_vec(f(decay_base)[0, 1]),
        fm_vec(f(iclr_base)[0, 0]), fm_vec(f(iclr_base)[0, 1]),
        fm_vec(f(k_k)[0]), fm_vec(f(k_a)[0]), fm_vec(f(r_k)[0]), fm_vec(f(lnx_g)[0]), fm_vec(f(lnx_b)[0]),
        fm_vec(f(ffn_conv_w)[0, 0]), fm_vec(f(ffn_conv_w)[0, 1]), fm_vec(f(ffn_conv_w)[0, 2]),
        fm_vec(f(ffn_conv_b)[0]),
    ], axis=1)
    assert pvec.shape[1] == NPV
    shared = dict(
        ada_w=f(ada_w)[0], w_in=f(w_in)[0], decay_up=f(decay_up)[0], iclr_up=f(iclr_up)[0],
        gate_up=f(gate_up)[0], w_out_a=f(w_out_a)[0], w_fourier=f(w_fourier)[0], w_out=f(w_out)[0],
        ffn_w_in=f(ffn_w_in)[0], ffn_w_down=f(ffn_w_down)[0], pvec=np.ascontiguousarray(pvec), **consts)
    in_maps = []
    cpb = NCORE // NBAT
    for j in range(NCORE):
        b = j // cpb
        xa = np.concatenate([x_prompt[j * NSEQ:(j + 1) * NSEQ].reshape(-1, D), x_sample[b]], axis=0)
        cf = np.stack([fm_vec(c_ctx), fm_vec(c[b])], axis=-1)
        m = dict(shared)
        m["x_all"] = np.ascontiguousarray(xa)
        m["cfm"] = np.ascontiguousarray(cf)
        m["st_lat"] = np.ascontiguousarray(state_rwkv[b, 0])
        in_maps.append(m)
    import os
    kc_ = int(os.environ.get('KCORES', NCORE))
    res = run_bass_kernel_spmd(nc, in_maps[:kc_], core_ids=list(range(kc_)))
    outs = list(res.results)
    while len(outs) < NCORE:
        outs.append(outs[0])
    nctx = NSEQ * S
    y_prompt = np.concatenate([np.asarray(outs[j]["y_all"])[:nctx].reshape(NSEQ, S, D) for j in range(NCORE)], axis=0)
    y_sample = np.stack([np.asarray(outs[b * cpb]["y_all"])[nctx:] for b in range(NBAT)], axis=0)
    new_state = np.concatenate([np.asarray(outs[j]["st_out"]) for j in range(NCORE)], axis=0)[:, None]
    return (y_prompt.astype(np.float32), y_sample.astype(np.float32), new_state.astype(np.float32))
```

```python
import contextlib
import numpy as np
import ml_dtypes
import concourse.bass as bass
import concourse.mybir as mybir
from concourse.bass_utils import run_bass_kernel_spmd

F32 = mybir.dt.float32
BF16 = mybir.dt.bfloat16
AF = mybir.ActivationFunctionType
ALU = mybir.AluOpType
NPBF = ml_dtypes.bfloat16

D = 2048
DC = 16
DFF = 5504
FC = 43
TB = 512
CDEC = -float(np.exp(-0.5))
RMS_EPS = 1e-6
GN_EPS = 64e-5

PV = {}
_o = 0
for _n, _c in [("ada_b", 96), ("g1", 16), ("g2", 16), ("gf", 16), ("rcw", 144), ("rcb", 48),
               ("dbase", 32), ("ibase", 32), ("k_k", 16), ("k_a", 16), ("r_k", 16), ("lng", 16),
               ("lnb", 16), ("fcw", 258), ("fcb", 86)]:
    PV[_n] = _o
    _o += _c
NPV = _o


import os as _os
KMAXOPS = int(_os.environ.get('KMAXOPS', '100000000'))


class StopBuild(Exception):
    pass


PSUM_IDS = set()


class Buf:
    def __init__(self, t, multi=False):
        self.t = t
        self.psum = id(t) in PSUM_IDS
        self.w = None
        self.ws = {}
        self.r = {}
        self.multi = multi

    def __getitem__(self, k):
        return self.t[k]


class Eng:
    def __init__(self, name, handle, sems, inc, seen, selfwait):
        self.name = name
        self.h = handle
        self.sems = sems
        self.inc = inc
        self.counts = [0] * len(sems)
        self.rr = 0
        self.seen = seen
        self.selfwait = selfwait


class KB:
    def __init__(self, nc, es):
        self.nc = nc
        self.es = es
        self.nins = 0

        def sem(n):
            return es.enter_context(nc.semaphore(n))
        pool_seen = {}
        self.PE = Eng("pe", nc.tensor, [sem("s_pe")], 1, {}, False)
        self.ACT = Eng("act", nc.scalar, [sem("s_act")], 1, {}, True)
        self.DVE = Eng("dve", nc.vector, [sem("s_dve")], 1, {}, True)
        self.POOL = Eng("pool", nc.gpsimd, [sem("s_pool")], 1, pool_seen, True)
        self.SP = Eng("sp", nc.sync, [sem("s_sp%d" % i) for i in range(16)], 16, {}, True)
        self.PQ = Eng("pq", nc.gpsimd, [sem("s_pq%d" % i) for i in range(6)], 16, pool_seen, True)
        self.engs = [self.PE, self.ACT, self.DVE, self.POOL, self.SP, self.PQ]

    def sb(self, name, shape, dt, stack=None):
        self.uid = getattr(self, "uid", 0) + 1
        return (stack or self.es).enter_context(self.nc.sbuf_tensor("sb%d_%s" % (self.uid, name), list(shape), dt))

    def ps(self, name, shape, dt, stack=None):
        self.uid = getattr(self, "uid", 0) + 1
        t = (stack or self.es).enter_context(self.nc.psum_tensor("ps%d_%s" % (self.uid, name), list(shape), dt))
        PSUM_IDS.add(id(t))
        return t

    def op(self, eng, fn, reads=(), writes=()):
        if getattr(self, 'disabled', False):
            return None
        self.nops = getattr(self, 'nops', 0) + 1
        if self.nops > KMAXOPS:
            self.disabled = True
            return None
        waits = {}

        def add(ev):
            if ev is None:
                return
            s, v = ev
            if waits.get(s, 0) < v:
                waits[s] = v
        for b in reads:
            if b.multi:
                for s, v in b.ws.items():
                    add((s, v))
            else:
                add(b.w)
                if b.psum:
                    for s, v in b.r.items():
                        add((s, v))
        for b in writes:
            if b.multi:
                continue
            add(b.w)
            for s, v in b.r.items():
                add((s, v))
        own = set(id(s) for s in eng.sems)
        for s, v in waits.items():
            if id(s) in own and not eng.selfwait:
                continue
            key = id(s)
            if eng.seen.get(key, 0) < v:
                eng.h.wait_ge(s, v)
                eng.seen[key] = v
                self.nins += 1
        i = eng.rr
        eng.rr = (eng.rr + 1) % len(eng.sems)
        if eng.inc == 16 and eng.counts[i] > 0 and eng.seen.get(id(eng.sems[i]), 0) < eng.counts[i]:
            eng.h.wait_ge(eng.sems[i], eng.counts[i])
            eng.seen[id(eng.sems[i])] = eng.counts[i]
            self.nins += 1
        ins = fn()
        eng.counts[i] += eng.inc
        ins.then_inc(eng.sems[i], eng.inc)
        self.nins += 1
        ev = (eng.sems[i], eng.counts[i])
        for b in writes:
            if b.multi:
                if b.ws.get(ev[0], 0) < ev[1]:
                    b.ws[ev[0]] = ev[1]
            else:
                b.w = ev
                b.r = {}
        for b in reads:
            if not b.multi:
                if b.r.get(ev[0], 0) < ev[1]:
                    b.r[ev[0]] = ev[1]
        return ins

    def dma(self, eng, out, in_, reads=(), writes=()):
        return self.op(eng, lambda: eng.h.dma_start(out=out, in_=in_), reads, writes)

    def barrier(self):
        if getattr(self, 'disabled', False):
            return
        for e in self.engs:
            if e is self.PQ:
                continue
            for o in self.engs:
                for s, c in zip(o.sems, o.counts):
                    if c > 0 and e.seen.get(id(s), 0) < c and not (o is e and not e.selfwait):
                        e.h.wait_ge(s, c)
                        e.seen[id(s)] = c
                        self.nins += 1

    def final_wait(self):
        e = self.SP
        for o in self.engs:
            for s, c in zip(o.sems, o.counts):
                if c > 0:
                    e.h.wait_ge(s, c)


def build_program(NSEQ, TL, dbg=None):
    import os
    NST = int(os.environ.get('KSTAGES', '9'))
    NBC = NSEQ * 256 // TB
    NBL = TL // TB
    NBLK = NBC + NBL
    NTOK = NBLK * TB
    NTL = TL // 128
    nc = bass.Bass("TRN2", target_bir_lowering=False)

    def din(name, shape, dt=F32):
        return nc.dram_tensor(name, list(shape), dt, kind="ExternalInput").ap()

    def dscr(name, shape, dt):
        return nc.dram_tensor(name, list(shape), dt, kind="Internal").ap()
    x_all = din("x_all", [NTOK, D])
    cfm_d = din("cfm", [128, DC, 2])
    st_lat = din("st_lat", [2, 32, 64, 64])
    ada_w = din("ada_w", [D, 6 * D])
    w_in = din("w_in", [D, 11520])
    decay_up = din("decay_up", [2, 64, D])
    iclr_up = din("iclr_up", [2, 64, D])
    gate_up = din("gate_up", [128, D])
    w_out_a = din("w_out_a", [D, D])
    w_fourier = din("w_fourier", [1024, D])
    w_out = din("w_out", [D, D])
    ffn_w_in = din("ffn_w_in", [D, 2 * DFF])
    ffn_w_down = din("ffn_w_down", [DFF, D])
    pvec_d = din("pvec", [128, NPV])
    ident_f_d = din("ident_f", [128, 128])
    cb_d = din("cbf", [128, 6, 128], BF16)
    blk64_d = din("blk64", [128, 128])
    masks_d = din("masks", [128, 2, 6, 128], BF16)
    rmask_d = din("rmask", [128, TB])
    ct256_d = din("ct256", [256, 256], BF16)
    st256_d = din("st256", [256, 256], BF16)
    ctL_d = din("ctL", [TL, TL], BF16)
    stL_d = din("stL", [TL, TL], BF16)
    cc_d = din("ccm", [256, 256], BF16)
    scn_d = din("scn", [256, 256], BF16)
    y_all = nc.dram_tensor("y_all", [NTOK, D], F32, kind="ExternalOutput").ap()
    st_out = nc.dram_tensor("st_out", [NSEQ, 2, 32, 64, 64], F32, kind="ExternalOutput").ap()
    FMs = Buf(dscr("FMs", [NBLK, 16, 2, 128, 4, 4, 128], BF16), multi=True)
    TMs = Buf(dscr("TMs", [NBLK, 16, 2, 128, 4, 2, 128], BF16), multi=True)
    TVs = Buf(dscr("TVs", [NBLK, 16, 128, 4, 128], BF16), multi=True)
    SCs = Buf(dscr("SCs", [NBLK, 16, 2, 128, 12], F32), multi=True)
    BGs = Buf(dscr("BGs", [NBLK, 16, 2, 128, TB], F32), multi=True)
    Ys = Buf(dscr("Ys", [NBLK, 16, 2, 128, TB], F32), multi=True)
    XBs = Buf(dscr("XBs", [NTOK, 1024], BF16), multi=True)
    FSs = Buf(dscr("FSs", [8, 128, NTOK], BF16), multi=True)
    DX = Buf(None, multi=True)

    es = contextlib.ExitStack()
    with es:
        kb = KB(nc, es)
        PE, ACT, DVE, POOL, SP, PQ = kb.PE, kb.ACT, kb.DVE, kb.POOL, kb.SP, kb.PQ

        def stage_gate(k):
            if NST < k:
                kb.disabled = True
        pvec = Buf(kb.sb("pvec", [128, NPV], F32))
        ident_f = Buf(kb.sb("ident_f", [128, 128], F32))
        cb = Buf(kb.sb("cb", [128, 6, 128], BF16))
        blk64 = Buf(kb.sb("blk64", [128, 128], F32))
        masks = Buf(kb.sb("masks", [128, 2, 6, 128], BF16))
        rmask = Buf(kb.sb("rmask", [128, TB], F32))
        cfm = Buf(kb.sb("cfm_s", [128, DC, 2], F32))
        modv = Buf(kb.sb("modv", [128, 96, 2], F32))
        eff = Buf(kb.sb("eff", [128, 2, DC, 2], F32))
        for b_, d_ in [(pvec, pvec_d), (ident_f, ident_f_d), (cb, cb_d), (blk64, blk64_d), (masks, masks_d),
                       (rmask, rmask_d), (cfm, cfm_d)]:
            kb.dma(SP, b_.t[:], d_, reads=[DX], writes=[b_])
        ident_b = cb.t[:, 0, :]
        blk1 = cb.t[:, 1, :]
        ones_b = cb.t[:, 2, :]

        def pv(name, col):
            c0 = PV[name] + col
            return pvec.t[:, c0:c0 + 1]

        def make_wpool(stack, n, nbytes_elems):
            return [Buf(kb.sb("wp%d_%d" % (i, nbytes_elems), [128, nbytes_elems], BF16, stack)) for i in range(n)]

        class WStream:
            def __init__(self, pool):
                self.pool = pool
                self.i = 0

            def load(self, src_ap, kc, ncols):
                b = self.pool[self.i % len(self.pool)]
                self.i += 1
                v = b.t[:, 0:kc * ncols].rearrange("p (k n) -> p k n", k=kc)
                if isinstance(src_ap, list):
                    o = 0
                    for sa in src_ap:
                        w_ = sa.shape[2]
                        kb.dma(PQ, v[:, :, o:o + w_], sa, reads=[DX], writes=[b])
                        o += w_
                else:
                    kb.dma(PQ, v, src_ap, reads=[DX], writes=[b])
                return b, v

        stage_gate(0)
        with contextlib.ExitStack() as st:
            wpool = make_wpool(st, 3, 16 * 512)
            ws = WStream(wpool)
            scb = Buf(kb.sb("scb", [128, DC, 2], BF16, st))
            pm = [Buf(kb.ps("pm0_%d" % i, [128, 512], F32, st)) for i in range(2)]
            kb.op(ACT, lambda: nc.scalar.activation(out=scb.t[:], in_=cfm.t[:], func=AF.Silu), [cfm], [scb])
            adv = ada_w.rearrange("(k p) n -> p k n", p=128)
            for g in range(24):
                wb, wv = ws.load(adv[:, :, g * 512:(g + 1) * 512], 16, 512)
                for j in range(4):
                    m = g * 4 + j
                    pb = pm[m % 2]
                    for k in range(16):
                        kb.op(PE, lambda k=k, j=j, pb=pb, wv=wv: nc.tensor.matmul(
                            pb.t[:, 0:2], lhsT=wv[:, k, j * 128:(j + 1) * 128], rhs=scb.t[:, k, :],
                            start=(k == 0), stop=(k == 15)), [wb, scb], [pb])
                    kb.op(ACT, lambda m=m, pb=pb: nc.scalar.activation(
                        out=modv.t[:, m, :], in_=pb.t[:, 0:2], func=AF.Identity, bias=pv("ada_b", m)),
                        [pb, pvec], [modv])
            for w_, (gn, mo) in enumerate([("g1", 16), ("g2", 64)]):
                kb.op(DVE, lambda w_=w_, mo=mo: nc.vector.tensor_scalar(
                    out=eff.t[:, w_, :, :], in0=modv.t[:, mo:mo + 16, :], scalar1=1.0, scalar2=None, op0=ALU.add),
                    [modv], [eff])
                g0 = PV[gn]
                kb.op(DVE, lambda w_=w_, g0=g0: nc.vector.tensor_tensor(
                    out=eff.t[:, w_, :, :], in0=eff.t[:, w_, :, :],
                    in1=pvec.t[:, g0:g0 + 16].rearrange("p (c o) -> p c o", o=1).broadcast_to([128, 16, 2]),
                    op=ALU.mult), [eff, pvec], [eff])
        kb.barrier()

        def modcol(s, c, r):
            return modv.t[:, s * 16 + c, r:r + 1]

        def load_xT(blk, xtm, xT, ptr):
            for t in range(4):
                xb_ = xtm[t % 2]
                r0 = blk * TB + t * 128
                kb.dma(SP, xb_.t[:], x_all[r0:r0 + 128, :], reads=[DX], writes=[xb_])
                for g in range(4):
                    pb = ptr[(t * 4 + g) % len(ptr)]
                    for j in range(4):
                        c = g * 4 + j
                        kb.op(PE, lambda pb=pb, j=j, c=c, xb_=xb_: nc.tensor.transpose(
                            pb.t[:, j, :], xb_.t[:, c * 128:(c + 1) * 128], ident_f.t[:]), [xb_, ident_f], [pb])
                    for j in range(4):
                        c = g * 4 + j
                        e = ACT if j % 2 == 0 else DVE
                        if e is ACT:
                            kb.op(ACT, lambda pb=pb, j=j, c=c, t=t: nc.scalar.copy(
                                out=xT[c].t[:, t * 128:(t + 1) * 128], in_=pb.t[:, j, :]), [pb], [xT[c]])
                        else:
                            kb.op(DVE, lambda pb=pb, j=j, c=c, t=t: nc.vector.tensor_copy(
                                out=xT[c].t[:, t * 128:(t + 1) * 128], in_=pb.t[:, j, :]), [pb], [xT[c]])

        def rstd_of(xT, sqb, pss, rstd):
            for c in range(16):
                sq = sqb[c % 2]
                kb.op(ACT, lambda c=c, sq=sq: nc.scalar.activation(out=sq.t[:], in_=xT[c].t[:], func=AF.Square),
                      [xT[c]], [sq])
                kb.op(PE, lambda c=c, sq=sq: nc.tensor.matmul(pss.t[:], lhsT=ones_b, rhs=sq.t[:],
                                                             start=(c == 0), stop=(c == 15)), [sq, cb], [pss])
            kb.op(DVE, lambda: nc.vector.tensor_scalar(out=rstd.t[:], in0=pss.t[:], scalar1=1.0 / D, scalar2=RMS_EPS,
                                                       op0=ALU.mult, op1=ALU.add), [pss], [rstd])
            kb.op(ACT, lambda: nc.scalar.activation(out=rstd.t[:], in_=rstd.t[:], func=AF.Ln), [rstd], [rstd])
            kb.op(ACT, lambda: nc.scalar.activation(out=rstd.t[:], in_=rstd.t[:], func=AF.Exp, scale=-0.5),
                  [rstd], [rstd])

        def norm_mod(xT, rstd, hT, tmpf, which, r):
            sh = 0 if which == 0 else 3
            for c in range(16):
                tf = tmpf[c % 2]
                kb.op(DVE, lambda c=c, tf=tf: nc.vector.scalar_tensor_tensor(
                    out=tf.t[:], in0=xT[c].t[:], scalar=eff.t[:, which, c, r:r + 1], in1=rstd.t[:],
                    op0=ALU.mult, op1=ALU.mult), [xT[c], eff, rstd], [tf])
                kb.op(ACT, lambda c=c, tf=tf: nc.scalar.activation(
                    out=hT[c].t[:], in_=tf.t[:], func=AF.Identity, bias=modcol(sh, c, r)), [tf, modv], [hT[c]])

        def conv3(ps, out, wname, bname, col, nwc, rl, e2=None):
            nr = TB // rl
            kb.op(ACT, lambda: nc.scalar.activation(out=out.t[:], in_=ps.t[:], func=AF.Identity,
                                                    bias=pv(bname, col), scale=pv(wname, nwc + col)),
                  [ps, pvec], [out])
            ov = out.t[:].rearrange("p (a b) -> p a b", b=rl)
            pv_ = ps.t[:].rearrange("p (a b) -> p a b", b=rl)
            kb.op(DVE, lambda: nc.vector.scalar_tensor_tensor(
                out=ov[:, :, 1:rl], in0=pv_[:, :, 0:rl - 1], scalar=pv(wname, col), in1=ov[:, :, 1:rl],
                op0=ALU.mult, op1=ALU.add), [ps, pvec, out], [out])
            kb.op(DVE, lambda: nc.vector.scalar_tensor_tensor(
                out=ov[:, :, 0:rl - 1], in0=pv_[:, :, 1:rl], scalar=pv(wname, 2 * nwc + col), in1=ov[:, :, 0:rl - 1],
                op0=ALU.mult, op1=ALU.add), [ps, pvec, out], [out])

        def blk_info(blk):
            if blk < NBC:
                return 0, 256
            return 1, 64

        stage_gate(1)
        with contextlib.ExitStack() as st:
            xtm = [Buf(kb.sb("xtm%d" % i, [128, D], F32, st)) for i in range(2)]
            xT = [Buf(kb.sb("xT%d" % i, [128, TB], F32, st)) for i in range(16)]
            hT = [Buf(kb.sb("hT%d" % i, [128, TB], BF16, st)) for i in range(16)]
            sqb = [Buf(kb.sb("sqb%d" % i, [128, TB], BF16, st)) for i in range(2)]
            tmpf = [Buf(kb.sb("tmpf%d" % i, [128, TB], F32, st)) for i in range(2)]
            rstd = Buf(kb.sb("rstd", [128, TB], F32, st))
            ws = WStream(make_wpool(st, 2, 16 * 512))
            dup = Buf(kb.sb("dup", [128, 2, D], BF16, st))
            gup = Buf(kb.sb("gup", [128, D], BF16, st))
            for d in range(2):
                kb.dma(PQ, dup.t[0:64, d, :], decay_up[d], reads=[DX], writes=[dup])
                kb.dma(PQ, dup.t[64:128, d, :], iclr_up[d], reads=[DX], writes=[dup])
            kb.dma(PQ, gup.t[:], gate_up, reads=[DX], writes=[gup])
            lora = Buf(kb.sb("lora", [128, 2, TB], BF16, st))
            F = lambda n: Buf(kb.sb(n, [128, TB], F32, st))
            r_c, k_c, v_c, kk, kkn, t1, t2, sig, aa, Lf, Lm, E1, E2, E3, kd, bb, bon, gg = [
                F("f1_%d" % i) for i in range(18)]
            sqk = Buf(kb.sb("sqk", [128, TB], BF16, st))
            vb = Buf(kb.sb("vb", [128, TB], BF16, st))
            FMo = [Buf(kb.sb("FMo%d" % i, [128, 4, 4, 128], BF16, st)) for i in range(2)]
            TMo = [Buf(kb.sb("TMo%d" % i, [128, 4, 2, 128], BF16, st)) for i in range(2)]
            TVo = Buf(kb.sb("TVo", [128, 4, 128], BF16, st))
            SCo = [Buf(kb.sb("SCo%d" % i, [128, 12], F32, st)) for i in range(2)]
            xbo = [Buf(kb.sb("xbo%d" % i, [128, 512], BF16, st)) for i in range(2)]
            ptr = [Buf(kb.ps("ptr%d" % i, [128, 4, 128], F32, st)) for i in range(2)]
            pp = [Buf(kb.ps("pp%d" % i, [128, TB], F32, st)) for i in range(4)]
            ptb = [Buf(kb.ps("ptb%d" % i, [128, 8, 128], BF16, st)) for i in range(2)]
            ppi = [0]

            def nextp():
                b_ = pp[ppi[0] % 4]
                ppi[0] += 1
                return b_
            wv_in = w_in.rearrange("(k p) n -> p k n", p=128)
            for blk in range(NBLK):
                r, rl = blk_info(blk)
                load_xT(blk, xtm, xT, ptr)
                pss = nextp()
                rstd_of(xT, sqb, pss, rstd)
                norm_mod(xT, rstd, hT, tmpf, 0, r)
                wb, wv = ws.load(wv_in[:, :, 11264:11520], 16, 256)
                p0 = nextp()
                p1 = nextp()
                for j, pb in enumerate([p0, p1]):
                    for k in range(16):
                        kb.op(PE, lambda k=k, j=j, pb=pb, wv=wv: nc.tensor.matmul(
                            pb.t[:], lhsT=wv[:, k, j * 128:(j + 1) * 128], rhs=hT[k].t[:], start=(k == 0),
                            stop=(k == 15)), [wb, hT[k]], [pb])
                kb.op(ACT, lambda: nc.scalar.activation(out=lora.t[0:64, 0, :], in_=p0.t[0:64, :], func=AF.Tanh),
                      [p0], [lora])
                kb.op(ACT, lambda: nc.scalar.copy(out=lora.t[64:128, 0, :], in_=p0.t[64:128, :]), [p0], [lora])
                kb.op(ACT, lambda: nc.scalar.activation(out=lora.t[:, 1, :], in_=p1.t[:], func=AF.Sigmoid),
                      [p1], [lora])
                for g2 in range(2):
                    wb, wv = ws.load(wv_in[:, :, 6144 + g2 * 512:6144 + (g2 + 1) * 512], 16, 512)
                    for t in range(4):
                        pb = nextp()
                        for k in range(16):
                            kb.op(PE, lambda k=k, t=t, pb=pb, wv=wv: nc.tensor.matmul(
                                pb.t[:], lhsT=hT[k].t[:, t * 128:(t + 1) * 128], rhs=wv[:, k, :], start=(k == 0),
                                stop=(k == 15)), [wb, hT[k]], [pb])
                        xo = xbo[(g2 * 4 + t) % 2]
                        kb.op(ACT, lambda pb=pb, xo=xo: nc.scalar.copy(out=xo.t[:], in_=pb.t[:]), [pb], [xo])
                        r0 = blk * TB + t * 128
                        kb.dma(SP, XBs.t[r0:r0 + 128, g2 * 512:(g2 + 1) * 512], xo.t[:], reads=[xo], writes=[XBs])
                for hp in range(16):
                    wb, wv = ws.load([wv_in[:, :, s3 * 2048 + hp * 128:s3 * 2048 + (hp + 1) * 128] for s3 in range(3)], 16, 384)
                    pr, pk, pvv = nextp(), nextp(), nextp()
                    for s_, pb in enumerate([pr, pk, pvv]):
                        for k in range(16):
                            kb.op(PE, lambda k=k, s_=s_, pb=pb, wv=wv: nc.tensor.matmul(
                                pb.t[:], lhsT=wv[:, k, s_ * 128:(s_ + 1) * 128], rhs=hT[k].t[:], start=(k == 0),
                                stop=(k == 15)), [wb, hT[k]], [pb])
                    conv3(pr, r_c, "rcw", "rcb", hp, 48, rl)
                    conv3(pk, k_c, "rcw", "rcb", 16 + hp, 48, rl)
                    conv3(pvv, v_c, "rcw", "rcb", 32 + hp, 48, rl)
                    kb.op(ACT, lambda: nc.scalar.activation(out=kk.t[:], in_=k_c.t[:], func=AF.Identity,
                                                            scale=pv("k_k", hp)), [k_c, pvec], [kk])
                    kb.op(ACT, lambda: nc.scalar.activation(out=sqk.t[:], in_=kk.t[:], func=AF.Square), [kk], [sqk])
                    pn = nextp()
                    kb.op(PE, lambda pn=pn: nc.tensor.matmul(pn.t[:], lhsT=blk1, rhs=sqk.t[:], start=True, stop=True),
                          [sqk, cb], [pn])
                    kb.op(DVE, lambda pn=pn: nc.vector.tensor_scalar(out=t1.t[:], in0=pn.t[:], scalar1=1e-12,
                                                                    scalar2=None, op0=ALU.add), [pn], [t1])
                    kb.op(ACT, lambda: nc.scalar.activation(out=t1.t[:], in_=t1.t[:], func=AF.Ln), [t1], [t1])
                    kb.op(ACT, lambda: nc.scalar.activation(out=t1.t[:], in_=t1.t[:], func=AF.Exp, scale=-0.5),
                          [t1], [t1])
                    kb.op(POOL, lambda: nc.gpsimd.tensor_tensor(out=kkn.t[:], in0=kk.t[:], in1=t1.t[:], op=ALU.mult),
                          [kk, t1], [kkn])
                    kb.op(DVE, lambda: nc.vector.tensor_tensor(out=t2.t[:], in0=r_c.t[:], in1=k_c.t[:], op=ALU.mult),
                          [r_c, k_c], [t2])
                    kb.op(ACT, lambda: nc.scalar.activation(out=sqk.t[:], in_=t2.t[:], func=AF.Identity,
                                                            scale=pv("r_k", hp)), [t2, pvec], [sqk])
                    pn2 = nextp()
                    kb.op(PE, lambda pn2=pn2: nc.tensor.matmul(pn2.t[:], lhsT=blk1, rhs=sqk.t[:], start=True,
                                                               stop=True), [sqk, cb], [pn2])
                    kb.op(DVE, lambda pn2=pn2: nc.vector.tensor_tensor(out=bon.t[:], in0=pn2.t[:], in1=v_c.t[:],
                                                                      op=ALU.mult), [pn2, v_c], [bon])
                    kb.dma(SP, BGs.t[blk, hp, 0], bon.t[:], reads=[bon], writes=[BGs])
                    pg = nextp()
                    kb.op(PE, lambda pg=pg: nc.tensor.matmul(pg.t[:], lhsT=gup.t[:, hp * 128:(hp + 1) * 128],
                                                             rhs=lora.t[:, 1, :], start=True, stop=True),
                          [gup, lora], [pg])
                    kb.op(ACT, lambda pg=pg: nc.scalar.copy(out=gg.t[:], in_=pg.t[:]), [pg], [gg])
                    kb.dma(SP, BGs.t[blk, hp, 1], gg.t[:], reads=[gg], writes=[BGs])
                    kb.op(ACT, lambda: nc.scalar.copy(out=vb.t[:], in_=v_c.t[:]), [v_c], [vb])
                    pt_ = ptb[0]
                    for c in range(4):
                        kb.op(PE, lambda c=c, pt_=pt_: nc.tensor.transpose(pt_.t[:, c, :], vb.t[:, c * 128:(c + 1) * 128],
                                                                           ident_b), [vb, cb], [pt_])
                    kb.op(DVE, lambda pt_=pt_: nc.vector.tensor_copy(out=TVo.t[:], in_=pt_.t[:, 0:4, :]), [pt_], [TVo])
                    kb.dma(SP, TVs.t[blk, hp], TVo.t[:], reads=[TVo], writes=[TVs])
                    for d in range(2):
                        fm = FMo[d]
                        tm = TMo[d]
                        sc = SCo[d]
                        pw, pa = nextp(), nextp()
                        kb.op(PE, lambda pw=pw, d=d: nc.tensor.matmul(
                            pw.t[:], lhsT=dup.t[0:64, d, hp * 128:(hp + 1) * 128], rhs=lora.t[0:64, 0, :],
                            start=True, stop=True), [dup, lora], [pw])
                        kb.op(PE, lambda pa=pa, d=d: nc.tensor.matmul(
                            pa.t[:], lhsT=dup.t[64:128, d, hp * 128:(hp + 1) * 128], rhs=lora.t[64:128, 0, :],
                            start=True, stop=True), [dup, lora], [pa])
                        kb.op(ACT, lambda pw=pw, d=d: nc.scalar.activation(
                            out=sig.t[:], in_=pw.t[:], func=AF.Sigmoid, bias=pv("dbase", d * 16 + hp)),
                            [pw, pvec], [sig])
                        kb.op(ACT, lambda pa=pa, d=d: nc.scalar.activation(
                            out=aa.t[:], in_=pa.t[:], func=AF.Sigmoid, bias=pv("ibase", d * 16 + hp)),
                            [pa, pvec], [aa])
                        kb.op(DVE, lambda: nc.vector.tensor_tensor_scan(
                            out=Lf.t[:], data0=rmask.t[:], data1=sig.t[:], initial=0.0, op0=ALU.mult, op1=ALU.add),
                            [rmask, sig], [Lf])
                        L3 = Lf.t[:].rearrange("p (c t) -> p c t", t=128)
                        Lm3 = Lm.t[:].rearrange("p (c t) -> p c t", t=128)
                        if d == 0:
                            mi, ei = 63, 127
                            Lsrc = Lf
                        else:
                            kb.op(DVE, lambda: nc.vector.tensor_tensor(out=t1.t[:], in0=sig.t[:], in1=Lf.t[:],
                                                                       op=ALU.subtract), [sig, Lf], [t1])
                            t13 = t1.t[:].rearrange("p (c t) -> p c t", t=128)
                            kb.op(DVE, lambda t13=t13, L3=L3: nc.vector.tensor_tensor(
                                out=t13, in0=t13, in1=L3[:, :, 127:128].broadcast_to([128, 4, 128]), op=ALU.add),
                                [t1, Lf], [t1])
                            mi, ei = 64, 0
                            Lsrc = t1
                            L3 = t13
                        kb.op(ACT, lambda L3=L3, mi=mi, sc=sc: nc.scalar.activation(
                            out=sc.t[:, 0:4], in_=L3[:, :, mi], func=AF.Exp, scale=CDEC), [Lsrc], [sc])
                        kb.op(ACT, lambda L3=L3, ei=ei, sc=sc: nc.scalar.activation(
                            out=sc.t[:, 4:8], in_=L3[:, :, ei], func=AF.Exp, scale=CDEC), [Lsrc], [sc])
                        kb.op(DVE, lambda L3=L3, mi=mi, Lm3=Lm3: nc.vector.tensor_tensor(
                            out=Lm3, in0=L3, in1=L3[:, :, mi:mi + 1].broadcast_to([128, 4, 128]), op=ALU.subtract),
                            [Lsrc], [Lm])
                        kb.op(ACT, lambda Lm3=Lm3, ei=ei, sc=sc: nc.scalar.activation(
                            out=sc.t[:, 8:12], in_=Lm3[:, :, ei], func=AF.Exp, scale=CDEC), [Lm], [sc])
                        kb.op(ACT, lambda: nc.scalar.activation(out=E1.t[:], in_=Lm.t[:], func=AF.Exp, scale=CDEC),
                              [Lm], [E1])
                        kb.op(ACT, lambda: nc.scalar.activation(out=E3.t[:], in_=Lm.t[:], func=AF.Exp, scale=-CDEC),
                              [Lm], [E3])
                        kb.op(POOL, lambda: nc.gpsimd.tensor_tensor(out=t2.t[:], in0=Lm.t[:], in1=sig.t[:],
                                                                    op=ALU.subtract), [Lm, sig], [t2])
                        kb.op(ACT, lambda: nc.scalar.activation(out=E2.t[:], in_=t2.t[:], func=AF.Exp, scale=CDEC),
                              [t2], [E2])
                        kb.op(DVE, lambda: nc.vector.tensor_scalar(out=kd.t[:], in0=aa.t[:], scalar1=-1.0,
                                                                   scalar2=pv("k_a", hp), op0=ALU.add, op1=ALU.mult),
                              [aa, pvec], [kd])
                        kb.op(DVE, lambda: nc.vector.scalar_tensor_tensor(out=kd.t[:], in0=kd.t[:], scalar=1.0,
                                                                          in1=k_c.t[:], op0=ALU.add, op1=ALU.mult),
                              [kd, k_c], [kd])
                        kb.op(POOL, lambda: nc.gpsimd.tensor_tensor(out=bb.t[:], in0=kkn.t[:], in1=aa.t[:],
                                                                    op=ALU.mult), [kkn, aa], [bb])

                        def v3(b_):
                            return b_.t[:].rearrange("p (c t) -> p c t", t=128)
                        kb.op(DVE, lambda fm=fm: nc.vector.tensor_tensor(out=fm.t[:, :, 0, :], in0=v3(kkn), in1=v3(E2),
                                                                        op=ALU.mult), [kkn, E2], [fm])
                        kb.op(POOL, lambda fm=fm: nc.gpsimd.tensor_tensor(out=fm.t[:, :, 1, :], in0=v3(r_c), in1=v3(E1),
                                                                         op=ALU.mult), [r_c, E1], [fm])
                        kb.op(DVE, lambda fm=fm: nc.vector.tensor_tensor(out=fm.t[:, :, 2, :], in0=v3(bb), in1=v3(E3),
                                                                        op=ALU.mult), [bb, E3], [fm])
                        kb.op(POOL, lambda fm=fm: nc.gpsimd.tensor_tensor(out=fm.t[:, :, 3, :], in0=v3(kd), in1=v3(E3),
                                                                         op=ALU.mult), [kd, E3], [fm])
                        for q_ in range(2):
                            pt_ = ptb[(q_ + 1) % 2]
                            for c in range(4):
                                kb.op(PE, lambda c=c, pt_=pt_, q_=q_, fm=fm: nc.tensor.transpose(
                                    pt_.t[:, c, :], fm.t[:, c, 2 + q_, :], ident_b), [fm, cb], [pt_])
                            kb.op(DVE if q_ == 0 else ACT,
                                  (lambda pt_=pt_, q_=q_, tm=tm: nc.vector.tensor_copy(out=tm.t[:, :, q_, :], in_=pt_.t[:, 0:4, :]))
                                  if q_ == 0 else
                                  (lambda pt_=pt_, q_=q_, tm=tm: nc.scalar.copy(out=tm.t[:, :, q_, :], in_=pt_.t[:, 0:4, :])),
                                  [pt_], [tm])
                        kb.dma(SP, FMs.t[blk, hp, d], fm.t[:], reads=[fm], writes=[FMs])
                        kb.dma(SP, TMs.t[blk, hp, d], tm.t[:], reads=[tm], writes=[TMs])
                        kb.dma(SP, SCs.t[blk, hp, d], sc.t[:], reads=[sc], writes=[SCs])
        kb.barrier()

        stage_gate(2)
        with contextlib.ExitStack() as st:
            NU = 3
            fmi = [Buf(kb.sb("fmi%d" % i, [128, 4, 4, 128], BF16, st)) for i in range(NU)]
            tmi = [Buf(kb.sb("tmi%d" % i, [128, 4, 2, 128], BF16, st)) for i in range(NU)]
            tvi = [Buf(kb.sb("tvi%d" % i, [128, 4, 128], BF16, st)) for i in range(NU)]
            sci = [Buf(kb.sb("sci%d" % i, [128, 12], F32, st)) for i in range(NU)]
            MBK = [Buf(kb.sb("MBK%d" % c, [128, 2, 4, 128], BF16, st)) for c in range(4)]
            NA = [[Buf(kb.sb("NA%d_%d" % (l, c), [128, 2, 2, 128], BF16, st)) for c in range(4)] for l in range(2)]
            AT0 = [Buf(kb.sb("AT0_%d" % c, [128, 2, 128], BF16, st)) for c in range(4)]
            PT = [Buf(kb.sb("PT%d" % c, [128, 2, 128], BF16, st)) for c in range(4)]
            Zs = [[Buf(kb.sb("Z%d_%d" % (hp, d), [128, 64], F32, st)) for d in range(2)] for hp in range(16)]
            Zb = Buf(kb.sb("Zb", [128, 128], BF16, st))
            kb.op(POOL, lambda: nc.gpsimd.memset(Zb.t[:], 0.0), [], [Zb])
            Xn = Buf(kb.sb("Xn", [128, 2, 64], BF16, st))
            UT = Buf(kb.sb("UT", [128, 2, 64], BF16, st))
            zt = Buf(kb.sb("zt", [128, 64], F32, st))
            yo = [Buf(kb.sb("yo%d" % i, [128, TB], F32, st)) for i in range(2)]
            sti = Buf(kb.sb("sti", [64, 128], F32, st))
            sto = Buf(kb.sb("sto", [64, 128], F32, st))
            pin = [Buf(kb.ps("pin%d" % c, [128, 512], F32, st)) for c in range(4)]
            pX = Buf(kb.ps("pX", [128, 512], F32, st))
            pU = Buf(kb.ps("pU", [128, 512], F32, st))
            pY = Buf(kb.ps("pY", [128, 512], F32, st))
            pZ = Buf(kb.ps("pZ", [128, 512], F32, st))

            def scan_unit(ui, blk, hp, d, chunk_order, seq_starts, seq_ends):
                fm, tm, tv, sc = fmi[ui % NU], tmi[ui % NU], tvi[ui % NU], sci[ui % NU]
                kb.dma(SP, fm.t[:], FMs.t[blk, hp, d], reads=[FMs], writes=[fm])
                kb.dma(SP, tm.t[:], TMs.t[blk, hp, d], reads=[TMs], writes=[tm])
                kb.dma(SP, tv.t[:], TVs.t[blk, hp], reads=[TVs], writes=[tv])
                kb.dma(SP, sc.t[:], SCs.t[blk, hp, d], reads=[SCs], writes=[sc])
                Z = Zs[hp][d]
                mk = masks.t[:, d, 0:4, :]
                for c in range(4):
                    for e in range(2):
                        pb = pin[(2 * c + e) % 4]
                        rs = slice(e * 64, e * 64 + 64)
                        for q_ in range(2):
                            kb.op(PE, lambda pb=pb, rs=rs, c=c, q_=q_: nc.tensor.matmul(
                                pb.t[:, q_ * 256:(q_ + 1) * 256], lhsT=fm.t[rs, c, 2 + q_, :],
                                rhs=fm.t[rs, c, 0:2, :], start=True, stop=True), [fm], [pb])
                        kb.op(DVE, lambda pb=pb, c=c, e=e: nc.vector.tensor_tensor(
                            out=MBK[c].t[:, e, :, :], in0=pb.t[:].rearrange("p (a b) -> p a b", b=128), in1=mk,
                            op=ALU.mult), [pb, masks], [MBK[c]])
                for c in range(4):
                    for e in range(2):
                        pb = pin[(2 * c + e) % 4]
                        rs = slice(e * 64, e * 64 + 64)
                        kb.op(PE, lambda pb=pb, rs=rs, c=c, e=e: nc.tensor.matmul(
                            pb.t[:, 0:128], lhsT=fm.t[rs, c, 0, :], rhs=fm.t[rs, c, 2, :],
                            start=True, stop=True), [fm], [pb])
                        kb.op(DVE, lambda pb=pb, c=c, e=e: nc.vector.tensor_tensor(
                            out=AT0[c].t[:, e, :], in0=pb.t[:, 0:128],
                            in1=masks.t[:, d, 4, :], op=ALU.mult), [pb, masks], [AT0[c]])
                    kb.op(POOL, lambda c=c: nc.gpsimd.tensor_tensor(
                        out=PT[c].t[:], in0=cb.t[:, 0:1, :].broadcast_to([128, 2, 128]), in1=MBK[c].t[:, :, 0, :],
                        op=ALU.subtract), [cb, MBK[c]], [PT[c]])
                for lev in range(6):
                    cur = NA[lev % 2]
                    prv = NA[(lev + 1) % 2]
                    for c in range(4):
                        pb = pin[c]
                        for e in range(2):
                            if lev == 0:
                                Np, Ap = MBK[c].t[:, e, 0, :], AT0[c].t[:, e, :]
                                rd = [MBK[c], AT0[c]]
                            else:
                                Np, Ap = prv[c].t[:, e, 0, :], prv[c].t[:, e, 1, :]
                                rd = [prv[c]]
                            kb.op(PE, lambda pb=pb, e=e, Np=Np, Ap=Ap: nc.tensor.matmul(
                                pb.t[:, e * 256:e * 256 + 128], lhsT=Ap, rhs=Np, start=True, stop=True), rd, [pb])
                            kb.op(PE, lambda pb=pb, e=e, Np=Np, Ap=Ap: nc.tensor.matmul(
                                pb.t[:, e * 256 + 128:e * 256 + 256], lhsT=Np, rhs=Ap, start=True, stop=True), rd, [pb])
                    for c in range(4):
                        pb = pin[c]
                        if c % 2 == 0:
                            kb.op(ACT, lambda pb=pb, c=c, cur=cur: nc.scalar.copy(
                                out=cur[c].t[:].rearrange("p a b c -> p (a b c)"), in_=pb.t[:]), [pb], [cur[c]])
                        else:
                            kb.op(DVE, lambda pb=pb, c=c, cur=cur: nc.vector.tensor_copy(
                                out=cur[c].t[:].rearrange("p a b c -> p (a b c)"), in_=pb.t[:]), [pb], [cur[c]])
                    for c in range(4):
                        pb = pin[c]
                        for e in range(2):
                            kb.op(PE, lambda pb=pb, e=e, c=c, cur=cur: nc.tensor.matmul(
                                pb.t[:, e * 128:(e + 1) * 128], lhsT=cur[c].t[:, e, 1, :], rhs=PT[c].t[:, e, :],
                                start=True, stop=True), [cur[c], PT[c]], [pb])
                    for c in range(4):
                        pb = pin[c]
                        kb.op(DVE, lambda pb=pb, c=c: nc.vector.tensor_tensor(
                            out=PT[c].t[:], in0=pb.t[:, 0:256].rearrange("p (a b) -> p a b", b=128), in1=PT[c].t[:],
                            op=ALU.add), [pb, PT[c]], [PT[c]])
                yb = yo[ui % 2]
                for c in chunk_order:
                    if c in seq_starts:
                        kind, arg = seq_starts[c]
                        if kind == "zero":
                            kb.op(POOL, lambda Z=Z: nc.gpsimd.memset(Z.t[:], 0.0), [], [Z])
                        elif kind == "load":
                            kb.dma(SP, sti.t[:].rearrange("v (e k) -> v e k", e=2),
                                   st_lat[d, 2 * hp:2 * hp + 2].rearrange("e v k -> v e k"), reads=[DX], writes=[sti])
                            kb.op(PE, lambda: nc.tensor.transpose(pZ.t[:, 0:64], sti.t[:], ident_f.t[0:64, 0:64]),
                                  [sti, ident_f], [pZ])
                            kb.op(DVE, lambda Z=Z: nc.vector.tensor_copy(out=Z.t[:], in_=pZ.t[:, 0:64]), [pZ], [Z])
                    for e in range(2):
                        rs = slice(e * 64, e * 64 + 64)
                        kb.op(ACT, lambda Z=Z, c=c, e=e, rs=rs: nc.scalar.activation(
                            out=Zb.t[rs, e * 64:(e + 1) * 64], in_=Z.t[rs, :], func=AF.Identity,
                            scale=sc.t[rs, c:c + 1]), [Z, sc], [Zb])
                    kb.op(PE, lambda c=c: nc.tensor.matmul(
                        pX.t[:, 0:128], lhsT=fm.t[:, c, 0, :], rhs=Zb.t[:], start=True, stop=False, skip_group_check=True), [fm, Zb], [pX])
                    for e in range(2):
                        kb.op(PE, lambda e=e, c=c: nc.tensor.matmul(
                            pX.t[:, e * 64:(e + 1) * 64], lhsT=MBK[c].t[:, e, 2, :], rhs=tv.t[:, c, e * 64:(e + 1) * 64],
                            start=False, stop=(e == 1), skip_group_check=True), [MBK[c], tv], [pX])
                    kb.op(ACT, lambda: nc.scalar.mul(out=Xn.t[:].rearrange("p a b -> p (a b)"), in_=pX.t[:, 0:128],
                                                     mul=-1.0), [pX], [Xn])
                    for e in range(2):
                        kb.op(PE, lambda e=e, c=c: nc.tensor.matmul(
                            pU.t[:, e * 64:(e + 1) * 64], lhsT=PT[c].t[:, e, :], rhs=Xn.t[:, e, :], start=True,
                            stop=True), [PT[c], Xn], [pU])
                    kb.op(DVE, lambda: nc.vector.tensor_copy(out=UT.t[:].rearrange("p a b -> p (a b)"),
                                                             in_=pU.t[:, 0:128]), [pU], [UT])
                    kb.op(PE, lambda c=c: nc.tensor.matmul(
                        pY.t[:, 0:128], lhsT=Zb.t[:], rhs=fm.t[:, c, 1, :], start=True, stop=False, skip_group_check=True), [Zb, fm], [pY])
                    for e in range(2):
                        rs = slice(e * 64, e * 64 + 64)
                        kb.op(PE, lambda e=e, rs=rs, c=c: nc.tensor.matmul(
                            pY.t[rs, 0:128], lhsT=UT.t[:, e, :], rhs=MBK[c].t[:, e, 1, :], start=False, stop=False,
                            skip_group_check=True),
                            [UT, MBK[c]], [pY])
                        kb.op(PE, lambda e=e, rs=rs, c=c: nc.tensor.matmul(
                            pY.t[rs, 0:128], lhsT=tv.t[:, c, e * 64:(e + 1) * 64], rhs=MBK[c].t[:, e, 3, :],
                            start=False, stop=(e == 1), skip_group_check=True), [tv, MBK[c]], [pY])
                    kb.op(ACT, lambda c=c, yb=yb: nc.scalar.copy(out=yb.t[:, c * 128:(c + 1) * 128], in_=pY.t[:, 0:128]),
                          [pY], [yb])
                    for e in range(2):
                        rs = slice(e * 64, e * 64 + 64)
                        kb.op(PE, lambda e=e, rs=rs, c=c: nc.tensor.matmul(
                            pZ.t[rs, 0:64], lhsT=tm.t[:, c, 0, e * 64:(e + 1) * 64], rhs=UT.t[:, e, :], start=True,
                            stop=False), [tm, UT], [pZ])
                        kb.op(PE, lambda e=e, rs=rs, c=c: nc.tensor.matmul(
                            pZ.t[rs, 0:64], lhsT=tm.t[:, c, 1, e * 64:(e + 1) * 64], rhs=tv.t[:, c, e * 64:(e + 1) * 64],
                            start=False, stop=True), [tm, tv], [pZ])
                    kb.op(DVE, lambda c=c: nc.vector.tensor_scalar(out=zt.t[:], in0=pZ.t[:, 0:64],
                                                                   scalar1=sc.t[:, 8 + c:9 + c], scalar2=None,
                                                                   op0=ALU.mult), [pZ, sc], [zt])
                    kb.op(DVE, lambda c=c, Z=Z: nc.vector.scalar_tensor_tensor(
                        out=Z.t[:], in0=Z.t[:], scalar=sc.t[:, 4 + c:5 + c], in1=zt.t[:], op0=ALU.mult, op1=ALU.add),
                        [Z, sc, zt], [Z])
                    if c in seq_ends:
                        sq_ = seq_ends[c]
                        kb.op(PE, lambda Z=Z: nc.tensor.transpose(pZ.t[0:64, 128:256], Z.t[:], ident_f.t[:]),
                              [Z, ident_f], [pZ])
                        kb.op(DVE, lambda: nc.vector.tensor_copy(out=sto.t[:], in_=pZ.t[0:64, 128:256]), [pZ], [sto])
                        kb.dma(SP, st_out[sq_, d, 2 * hp:2 * hp + 2].rearrange("e v k -> v e k"),
                               sto.t[:].rearrange("v (e k) -> v e k", e=2), reads=[sto], writes=[DX])
                kb.dma(SP, Ys.t[blk, hp, d], yb.t[:], reads=[yb], writes=[Ys])

            ui = 0
            for blk in range(NBC):
                for hp in range(16):
                    for d in range(2):
                        order = [0, 1, 2, 3] if d == 0 else [3, 2, 1, 0]
                        if d == 0:
                            starts = {0: ("zero", None), 2: ("zero", None)}
                            ends = {1: blk * 2, 3: blk * 2 + 1}
                        else:
                            starts = {3: ("zero", None), 1: ("zero", None)}
                            ends = {2: blk * 2 + 1, 0: blk * 2}
                        scan_unit(ui, blk, hp, d, order, starts, ends)
                        ui += 1
            for s_ in range(NBL):
                for hp in range(16):
                    for d in range(2):
                        if d == 0:
                            blk = NBC + s_
                            order = [0, 1, 2, 3]
                            starts = {0: ("load", None)} if s_ == 0 else {}
                        else:
                            blk = NBC + NBL - 1 - s_
                            order = [3, 2, 1, 0]
                            starts = {3: ("load", None)} if s_ == 0 else {}
                        scan_unit(ui, blk, hp, d, order, starts, {})
                        ui += 1
        kb.barrier()

        stage_gate(3)
        with contextlib.ExitStack() as st:
            ccs = Buf(kb.sb("ccs", [128, 2, 256], BF16, st))
            scs = Buf(kb.sb("scs", [128, 2, 256], BF16, st))
            kb.dma(SP, ccs.t[:], cc_d.rearrange("(k p) n -> p k n", p=128), reads=[DX], writes=[ccs])
            kb.dma(SP, scs.t[:], scn_d.rearrange("(k p) n -> p k n", p=128), reads=[DX], writes=[scs])
            B12 = [Buf(kb.sb("B12_%d" % i, [128, 2, 2, TB], BF16, st)) for i in range(2)]
            fo = [Buf(kb.sb("fo%d" % i, [128, TB], BF16, st)) for i in range(2)]
            pf = [Buf(kb.ps("pf%d" % i, [128, TB], F32, st)) for i in range(6)]
            pfi = [0]

            def nextpf():
                b_ = pf[pfi[0] % 6]
                pfi[0] += 1
                return b_

            def fourier_seq(tok0, T, ct_d, st_d, ctb, stb, xbg):
                nt = T // 128
                TP = min(TB, T)
                for g in range(4):
                    kb.dma(SP, xbg[g].t[:, 0:nt, :],
                           XBs.t[tok0:tok0 + T, g * 256:(g + 1) * 256].rearrange("(n p) c -> p n c", p=128),
                           reads=[XBs], writes=[xbg[g]])
                for tb_ in range(T // TP):
                    kb.dma(SP, ctb.t[:, 0:nt, 0:TP], ct_d[:, tb_ * TP:(tb_ + 1) * TP].rearrange("(n p) c -> p n c", p=128),
                           reads=[DX], writes=[ctb])
                    kb.dma(SP, stb.t[:, 0:nt, 0:TP], st_d[:, tb_ * TP:(tb_ + 1) * TP].rearrange("(n p) c -> p n c", p=128),
                           reads=[DX], writes=[stb])
                    for g in range(4):
                        bb_ = B12[g % 2]
                        for cc_ in range(2):
                            for q_, mat in enumerate([ctb, stb]):
                                pb = nextpf()
                                for tt in range(nt):
                                    kb.op(PE, lambda pb=pb, tt=tt, cc_=cc_, mat=mat, g=g: nc.tensor.matmul(
                                        pb.t[:, 0:TP], lhsT=xbg[g].t[:, tt, cc_ * 128:(cc_ + 1) * 128],
                                        rhs=mat.t[:, tt, 0:TP], start=(tt == 0), stop=(tt == nt - 1)),
                                        [xbg[g], mat], [pb])
                                if q_ == 0:
                                    kb.op(ACT, lambda pb=pb, cc_=cc_, bb_=bb_: nc.scalar.copy(
                                        out=bb_.t[:, cc_, 0, 0:TP], in_=pb.t[:, 0:TP]), [pb], [bb_])
                                else:
                                    kb.op(DVE, lambda pb=pb, cc_=cc_, bb_=bb_: nc.vector.tensor_copy(
                                        out=bb_.t[:, cc_, 1, 0:TP], in_=pb.t[:, 0:TP]), [pb], [bb_])
                        for cp in range(2):
                            pb = nextpf()
                            n_ = 0
                            for cc_ in range(2):
                                for q_, mat in enumerate([ccs, scs]):
                                    kb.op(PE, lambda pb=pb, cc_=cc_, q_=q_, mat=mat, cp=cp, n_=n_, bb_=bb_: nc.tensor.matmul(
                                        pb.t[:, 0:TP], lhsT=mat.t[:, cc_, cp * 128:(cp + 1) * 128],
                                        rhs=bb_.t[:, cc_, q_, 0:TP], start=(n_ == 0), stop=(n_ == 3)), [mat, bb_], [pb])
                                    n_ += 1
                            f_ = fo[cp]
                            kb.op(ACT if cp == 0 else DVE,
                                  (lambda pb=pb, f_=f_: nc.scalar.copy(out=f_.t[:, 0:TP], in_=pb.t[:, 0:TP])) if cp == 0 else
                                  (lambda pb=pb, f_=f_: nc.vector.tensor_copy(out=f_.t[:, 0:TP], in_=pb.t[:, 0:TP])),
                                  [pb], [f_])
                            t0 = tok0 + tb_ * TP
                            kb.dma(SP, FSs.t[g * 2 + cp, :, t0:t0 + TP], f_.t[:, 0:TP], reads=[f_], writes=[FSs])

            ctb = Buf(kb.sb("ctb", [128, NTL, TB], BF16, st))
            stb = Buf(kb.sb("stb", [128, NTL, TB], BF16, st))
            xbg = [Buf(kb.sb("xbg%d" % g, [128, NTL, 256], BF16, st)) for g in range(4)]
            for s_ in range(NSEQ):
                fourier_seq(s_ * 256, 256, ct256_d, st256_d, ctb, stb, xbg)
            fourier_seq(NBC * TB, TL, ctL_d, stL_d, ctb, stb, xbg)
        kb.barrier()

        stage_gate(4)
        with contextlib.ExitStack() as st:
            xtm = [Buf(kb.sb("xtm%d" % i, [128, D], F32, st)) for i in range(2)]
            xT = [Buf(kb.sb("xT%d" % i, [128, TB], F32, st)) for i in range(16)]
            hT = [Buf(kb.sb("hT%d" % i, [128, TB], BF16, st)) for i in range(16)]
            gA = [Buf(kb.sb("gA%d" % i, [128, TB], BF16, st)) for i in range(16)]
            act = [Buf(kb.sb("act%d" % i, [128, TB], BF16, st)) for i in range(FC)]
            ya, gB, FB = act[0:16], act[16:32], act[32:40]
            sqb = [Buf(kb.sb("sqb%d" % i, [128, TB], BF16, st)) for i in range(2)]
            tmpf = [Buf(kb.sb("tmpf%d" % i, [128, TB], F32, st)) for i in range(2)]
            rstd = Buf(kb.sb("rstd", [128, TB], F32, st))
            F = lambda n: Buf(kb.sb(n, [128, TB], F32, st))
            y0, y1, bo, g_, dl, sq2 = [F("f4_%d" % i) for i in range(6)]
            ug, uv = y0, y1
            ws = WStream(make_wpool(st, 2, FC * 128 + 128))
            wsd = ws
            ptr = [Buf(kb.ps("ptr%d" % i, [128, 4, 128], F32, st)) for i in range(2)]
            pp = [Buf(kb.ps("pp%d" % i, [128, TB], F32, st)) for i in range(5)]
            ppi = [0]

            def nextp():
                b_ = pp[ppi[0] % 5]
                ppi[0] += 1
                return b_

            def proj(wsrc, kc, ncol_total, gw, rhs, evac, wstream):
                for g in range(ncol_total // gw):
                    wb, wv = wstream.load(wsrc[:, :, g * gw:(g + 1) * gw], kc, gw)
                    for j in range(gw // 128):
                        pb = nextp()
                        for k in range(kc):
                            kb.op(PE, lambda k=k, j=j, pb=pb, wv=wv: nc.tensor.matmul(
                                pb.t[:], lhsT=wv[:, k, j * 128:(j + 1) * 128], rhs=rhs[k].t[:], start=(k == 0),
                                stop=(k == kc - 1)), [wb, rhs[k]], [pb])
                        evac(g * (gw // 128) + j, pb)

            wv_in = w_in.rearrange("(k p) n -> p k n", p=128)
            wv_oa = w_out_a.rearrange("(k p) n -> p k n", p=128)
            wv_fo = w_fourier.rearrange("(k p) n -> p k n", p=128)
            wv_o = w_out.rearrange("(k p) n -> p k n", p=128)
            wv_fi2 = ffn_w_in.rearrange("(k p) n -> p k n", p=128)
            wv_fd = ffn_w_down.rearrange("(k p) n -> p k n", p=128)
            for blk in range(NBLK):
                r, rl = blk_info(blk)
                load_xT(blk, xtm, xT, ptr)
                pss = nextp()
                rstd_of(xT, sqb, pss, rstd)
                norm_mod(xT, rstd, hT, tmpf, 0, r)
                for hp in range(16):
                    kb.dma(SP, y0.t[:], Ys.t[blk, hp, 0], reads=[Ys], writes=[y0])
                    kb.dma(SP, y1.t[:], Ys.t[blk, hp, 1], reads=[Ys], writes=[y1])
                    kb.dma(SP, bo.t[:], BGs.t[blk, hp, 0], reads=[BGs], writes=[bo])
                    kb.dma(SP, g_.t[:], BGs.t[blk, hp, 1], reads=[BGs], writes=[g_])
                    kb.op(POOL, lambda: nc.gpsimd.tensor_tensor(out=y0.t[:], in0=y0.t[:], in1=y1.t[:], op=ALU.add),
                          [y0, y1], [y0])
                    pmn = nextp()
                    kb.op(PE, lambda pmn=pmn: nc.tensor.matmul(pmn.t[:], lhsT=blk64.t[:], rhs=y0.t[:], start=True,
                                                               stop=True), [blk64, y0], [pmn])
                    kb.op(DVE, lambda pmn=pmn: nc.vector.tensor_tensor(out=dl.t[:], in0=y0.t[:], in1=pmn.t[:],
                                                                      op=ALU.subtract), [y0, pmn], [dl])
                    kb.op(ACT, lambda: nc.scalar.activation(out=sq2.t[:], in_=dl.t[:], func=AF.Square), [dl], [sq2])
                    pvr = nextp()
                    kb.op(PE, lambda pvr=pvr: nc.tensor.matmul(pvr.t[:], lhsT=blk64.t[:], rhs=sq2.t[:], start=True,
                                                               stop=True), [blk64, sq2], [pvr])
                    kb.op(DVE, lambda pvr=pvr: nc.vector.tensor_scalar(out=sq2.t[:], in0=pvr.t[:], scalar1=GN_EPS,
                                                                      scalar2=None, op0=ALU.add), [pvr], [sq2])
                    kb.op(ACT, lambda: nc.scalar.activation(out=sq2.t[:], in_=sq2.t[:], func=AF.Ln), [sq2], [sq2])
                    kb.op(ACT, lambda: nc.scalar.activation(out=sq2.t[:], in_=sq2.t[:], func=AF.Exp, scale=-0.5),
                          [sq2], [sq2])
                    kb.op(POOL, lambda: nc.gpsimd.tensor_tensor(out=dl.t[:], in0=dl.t[:], in1=sq2.t[:], op=ALU.mult),
                          [dl, sq2], [dl])
                    kb.op(ACT, lambda hp=hp: nc.scalar.activation(out=dl.t[:], in_=dl.t[:], func=AF.Identity,
                                                                  bias=pv("lnb", hp), scale=pv("lng", hp)),
                          [dl, pvec], [dl])
                    kb.op(DVE, lambda: nc.vector.tensor_tensor(out=dl.t[:], in0=dl.t[:], in1=bo.t[:], op=ALU.add),
                          [dl, bo], [dl])
                    kb.op(DVE, lambda hp=hp: nc.vector.tensor_tensor(out=ya[hp].t[:], in0=dl.t[:], in1=g_.t[:],
                                                                    op=ALU.mult), [dl, g_], [ya[hp]])
                proj(wv_in[:, :, 7168:9216], 16, 2048, 256, hT,
                     lambda c, pb: kb.op(ACT, lambda: nc.scalar.activation(out=gA[c].t[:], in_=pb.t[:],
                                                                           func=AF.Sigmoid), [pb], [gA[c]]), ws)
                proj(wv_in[:, :, 9216:11264], 16, 2048, 256, hT,
                     lambda c, pb: kb.op(ACT, lambda: nc.scalar.activation(out=gB[c].t[:], in_=pb.t[:],
                                                                           func=AF.Sigmoid), [pb], [gB[c]]), ws)
                proj(wv_oa, 16, 2048, 256, ya,
                     lambda c, pb: kb.op(DVE, lambda: nc.vector.tensor_tensor(out=gA[c].t[:], in0=pb.t[:],
                                                                              in1=gA[c].t[:], op=ALU.mult),
                                         [pb, gA[c]], [gA[c]]), ws)
                for c8 in range(8):
                    kb.dma(SP, FB[c8].t[:], FSs.t[c8, :, blk * TB:(blk + 1) * TB], reads=[FSs], writes=[FB[c8]])

                def ev_f(c, pb):
                    kb.op(DVE, lambda: nc.vector.tensor_tensor(out=gB[c].t[:], in0=pb.t[:], in1=gB[c].t[:],
                                                               op=ALU.mult), [pb, gB[c]], [gB[c]])
                    kb.op(POOL, lambda: nc.gpsimd.tensor_tensor(out=gA[c].t[:], in0=gA[c].t[:], in1=gB[c].t[:],
                                                                op=ALU.add), [gA[c], gB[c]], [gA[c]])
                proj(wv_fo, 8, 2048, 512, FB, ev_f, ws)
                proj(wv_o, 16, 2048, 256, gA,
                     lambda c, pb: kb.op(DVE, lambda: nc.vector.scalar_tensor_tensor(
                         out=xT[c].t[:], in0=pb.t[:], scalar=modcol(2, c, r), in1=xT[c].t[:], op0=ALU.mult,
                         op1=ALU.add), [pb, modv, xT[c]], [xT[c]]), ws)
                pss = nextp()
                rstd_of(xT, sqb, pss, rstd)
                norm_mod(xT, rstd, hT, tmpf, 1, r)
                for f in range(FC):
                    wb, wv = ws.load([wv_fi2[:, :, s3 * DFF + f * 128:s3 * DFF + (f + 1) * 128] for s3 in range(2)], 16, 256)
                    pg, pu = nextp(), nextp()
                    for s_, pb in enumerate([pg, pu]):
                        for k in range(16):
                            kb.op(PE, lambda k=k, s_=s_, pb=pb, wv=wv: nc.tensor.matmul(
                                pb.t[:], lhsT=wv[:, k, s_ * 128:(s_ + 1) * 128], rhs=hT[k].t[:], start=(k == 0),
                                stop=(k == 15)), [wb, hT[k]], [pb])
                    conv3(pg, ug, "fcw", "fcb", f, 86, rl)
                    conv3(pu, uv, "fcw", "fcb", FC + f, 86, rl)
                    kb.op(ACT, lambda: nc.scalar.activation(out=ug.t[:], in_=ug.t[:], func=AF.Silu), [ug], [ug])
                    kb.op(POOL, lambda f=f: nc.gpsimd.tensor_tensor(out=act[f].t[:], in0=ug.t[:], in1=uv.t[:],
                                                                    op=ALU.mult), [ug, uv], [act[f]])
                proj(wv_fd, FC, 2048, 128, act,
                     lambda c, pb: kb.op(DVE, lambda: nc.vector.scalar_tensor_tensor(
                         out=xT[c].t[:], in0=pb.t[:], scalar=modcol(5, c, r), in1=xT[c].t[:], op0=ALU.mult,
                         op1=ALU.add), [pb, modv, xT[c]], [xT[c]]), wsd)
                pss = nextp()
                rstd_of(xT, sqb, pss, rstd)
                for c in range(16):
                    kb.op(DVE, lambda c=c: nc.vector.scalar_tensor_tensor(
                        out=xT[c].t[:], in0=xT[c].t[:], scalar=pv("gf", c), in1=rstd.t[:], op0=ALU.mult,
                        op1=ALU.mult), [xT[c], pvec, rstd], [xT[c]])
                for t in range(4):
                    ob = xtm[t % 2]
                    for g in range(4):
                        pb = ptr[(t * 4 + g) % 2]
                        for j in range(4):
                            c = g * 4 + j
                            kb.op(PE, lambda pb=pb, j=j, c=c, t=t: nc.tensor.transpose(
                                pb.t[:, j, :], xT[c].t[:, t * 128:(t + 1) * 128], ident_f.t[:]), [xT[c], ident_f], [pb])
                        if g % 2 == 0:
                            kb.op(ACT, lambda pb=pb, g=g, ob=ob: nc.scalar.copy(
                                out=ob.t[:, g * 512:(g + 1) * 512], in_=pb.t[:].rearrange("p a b -> p (a b)")),
                                [pb], [ob])
                        else:
                            kb.op(DVE, lambda pb=pb, g=g, ob=ob: nc.vector.tensor_copy(
                                out=ob.t[:, g * 512:(g + 1) * 512], in_=pb.t[:].rearrange("p a b -> p (a b)")),
                                [pb], [ob])
                    r0 = blk * TB + t * 128
                    kb.dma(SP, y_all[r0:r0 + 128, :], ob.t[:], reads=[ob], writes=[DX])
        kb.final_wait()
        print("instructions:", kb.nins, "ops:", getattr(kb, "nops", 0))
    return nc


def fm_vec(v):
    v = np.asarray(v, np.float32).reshape(-1, 128)
    return np.ascontiguousarray(v.T)


_CACHE = {}


def host_consts(TL):
    if TL in _CACHE:
        return _CACHE[TL]
    c = {}
    c["ident_f"] = np.eye(128, dtype=np.float32)
    cbf = np.zeros((128, 6, 128), np.float32)
    cbf[:, 0, :] = np.eye(128)
    blk = np.zeros((128, 128), np.float32)
    blk[:64, :64] = 1
    blk[64:, 64:] = 1
    cbf[:, 1, :] = blk
    cbf[:, 2, :] = 1
    c["cbf"] = cbf.astype(NPBF)
    c["blk64"] = (blk / 64.0).astype(np.float32)
    idx = np.arange(128)
    m = np.zeros((128, 2, 6, 128), np.float32)
    for d in range(2):
        before = (idx[:, None] > idx[None, :]) if d == 1 else (idx[:, None] < idx[None, :])
        beq = before | np.eye(128, dtype=bool)
        m[:, d, 0] = before
        m[:, d, 1] = beq
        m[:, d, 2] = before
        m[:, d, 3] = beq
        m[:, d, 4] = before.T
    c["masks"] = m.astype(NPBF)
    rm = np.ones((128, TB), np.float32)
    rm[:, ::128] = 0
    c["rmask"] = rm

    def dft(T):
        t = np.arange(T, dtype=np.float64)
        ang = 2 * np.pi * ((t[:, None] * t[None, :]) % T) / T
        return np.cos(ang), np.sin(ang)
    c256, s256 = dft(256)
    c["ct256"] = (c256 / 16.0).astype(NPBF)
    c["st256"] = (s256 / 16.0).astype(NPBF)
    cL, sL = dft(TL)
    c["ctL"] = (cL / np.sqrt(TL)).astype(NPBF)
    c["stL"] = (sL / np.sqrt(TL)).astype(NPBF)
    c["ccm"] = (c256 / 16.0).astype(NPBF)
    c["scn"] = (-s256 / 16.0).astype(NPBF)
    _CACHE[TL] = c
    return c


_PROG = {}


def kernel(x_prompt, x_sample, state_rwkv, c, c_ctx, ada_w, ada_b, norm_mix_g, w_in,
           rkv_conv_w, rkv_conv_b, decay_up, decay_base, iclr_up, iclr_base, gate_up,
           k_k, k_a, r_k, lnx_g, lnx_b, w_out_a, w_fourier, w_out, norm_ffn_g,
           ffn_w_in, ffn_conv_w, ffn_conv_b, ffn_w_down, final_norm_g):
    f = lambda a: np.asarray(a, np.float32)
    x_prompt, x_sample, state_rwkv, c, c_ctx = f(x_prompt), f(x_sample), f(state_rwkv), f(c), f(c_ctx)
    B, S, _ = x_prompt.shape
    NBAT, TL, _ = x_sample.shape
    NCORE = 8
    NSEQ = B // NCORE
    key = (NSEQ, TL)
    if key not in _PROG:
        _PROG[key] = build_program(NSEQ, TL)
    nc = _PROG[key]
    consts = host_consts(TL)
    pvec = np.concatenate([
        fm_vec(f(ada_b)[0]), fm_vec(f(norm_mix_g)[0]), fm_vec(f(norm_ffn_g)[0]), fm_vec(f(final_norm_g)),
        fm_vec(f(rkv_conv_w)[0, 0]), fm_vec(f(rkv_conv_w)[0, 1]), fm_vec(f(rkv_conv_w)[0, 2]),
        fm_vec(f(rkv_conv_b)[0]),
        fm_vec(f(decay_base)[0, 0]), fm_vec(f(decay_base)[0, 1]),
        fm_vec(f(iclr_base)[0, 0]), fm_vec(f(iclr_base)[0, 1]),
        fm_vec(f(k_k)[0]), fm_vec(f(k_a)[0]), fm_vec(f(r_k)[0]), fm_vec(f(lnx_g)[0]), fm_vec(f(lnx_b)[0]),
        fm_vec(f(ffn_conv_w)[0, 0]), fm_vec(f(ffn_conv_w)[0, 1]), fm_vec(f(ffn_conv_w)[0, 2]),
        fm_vec(f(ffn_conv_b)[0]),
    ], axis=1)
    assert pvec.shape[1] == NPV
    shared = dict(
        ada_w=f(ada_w)[0], w_in=f(w_in)[0], decay_up=f(decay_up)[0], iclr_up=f(iclr_up)[0],
        gate_up=f(gate_up)[0], w_out_a=f(w_out_a)[0], w_fourier=f(w_fourier)[0], w_out=f(w_out)[0],
        ffn_w_in=f(ffn_w_in)[0], ffn_w_down=f(ffn_w_down)[0], pvec=np.ascontiguousarray(pvec), **consts)
    in_maps = []
    cpb = NCORE // NBAT
    for j in range(NCORE):
        b = j // cpb
        xa = np.concatenate([x_prompt[j * NSEQ:(j + 1) * NSEQ].reshape(-1, D), x_sample[b]], axis=0)
        cf = np.stack([fm_vec(c_ctx), fm_vec(c[b])], axis=-1)
        m = dict(shared)
        m["x_all"] = np.ascontiguousarray(xa)
        m["cfm"] = np.ascontiguousarray(cf)
        m["st_lat"] = np.ascontiguousarray(state_rwkv[b, 0])
        in_maps.append(m)
    import os
    kc_ = int(os.environ.get('KCORES', NCORE))
    if os.environ.get('KTRACE'):
        res = run_bass_kernel_spmd(nc, in_maps[:kc_], core_ids=list(range(kc_)), trace=True)
        print('exec_time_ns', res.exec_time_ns)
    else:
        res = run_bass_kernel_spmd(nc, in_maps[:kc_], core_ids=list(range(kc_)))
    outs = list(res.results)
    while len(outs) < NCORE:
        outs.append(outs[0])
    nctx = NSEQ * S
    y_prompt = np.concatenate([np.asarray(outs[j]["y_all"])[:nctx].reshape(NSEQ, S, D) for j in range(NCORE)], axis=0)
    y_sample = np.stack([np.asarray(outs[b * cpb]["y_all"])[nctx:] for b in range(NBAT)], axis=0)
    new_state = np.concatenate([np.asarray(outs[j]["st_out"]) for j in range(NCORE)], axis=0)[:, None]
    return (y_prompt.astype(np.float32), y_sample.astype(np.float32), new_state.astype(np.float32))
```

```python
import contextlib
import numpy as np
import ml_dtypes
import concourse.bass as bass
import concourse.mybir as mybir
from concourse.bass_utils import run_bass_kernel_spmd

F32 = mybir.dt.float32
BF16 = mybir.dt.bfloat16
AF = mybir.ActivationFunctionType
ALU = mybir.AluOpType
NPBF = ml_dtypes.bfloat16

D = 2048
DC = 16
DFF = 5504
FC = 43
TB = 512
CDEC = -float(np.exp(-0.5))
RMS_EPS = 1e-6
GN_EPS = 64e-5

PV = {}
_o = 0
for _n, _c in [("ada_b", 96), ("g1", 16), ("g2", 16), ("gf", 16), ("rcw", 144), ("rcb", 48),
               ("dbase", 32), ("ibase", 32), ("k_k", 16), ("k_a", 16), ("r_k", 16), ("lng", 16),
               ("lnb", 16), ("fcw", 258), ("fcb", 86)]:
    PV[_n] = _o
    _o += _c
NPV = _o


import os as _os
KMAXOPS = int(_os.environ.get('KMAXOPS', '100000000'))


class StopBuild(Exception):
    pass


PSUM_IDS = set()


class Buf:
    def __init__(self, t, multi=False):
        self.t = t
        self.psum = id(t) in PSUM_IDS
        self.w = None
        self.ws = {}
        self.r = {}
        self.multi = multi

    def __getitem__(self, k):
        return self.t[k]


class Eng:
    def __init__(self, name, handle, sems, inc, seen, selfwait):
        self.name = name
        self.h = handle
        self.sems = sems
        self.inc = inc
        self.counts = [0] * len(sems)
        self.rr = 0
        self.seen = seen
        self.selfwait = selfwait


class KB:
    def __init__(self, nc, es):
        self.nc = nc
        self.es = es
        self.nins = 0

        def sem(n):
            return es.enter_context(nc.semaphore(n))
        pool_seen = {}
        self.PE = Eng("pe", nc.tensor, [sem("s_pe")], 1, {}, False)
        self.ACT = Eng("act", nc.scalar, [sem("s_act")], 1, {}, True)
        self.DVE = Eng("dve", nc.vector, [sem("s_dve")], 1, {}, True)
        self.POOL = Eng("pool", nc.gpsimd, [sem("s_pool")], 1, pool_seen, True)
        self.SP = Eng("sp", nc.sync, [sem("s_sp%d" % i) for i in range(16)], 16, {}, True)
        self.PQ = Eng("pq", nc.gpsimd, [sem("s_pq%d" % i) for i in range(6)], 16, pool_seen, True)
        self.engs = [self.PE, self.ACT, self.DVE, self.POOL, self.SP, self.PQ]

    def sb(self, name, shape, dt, stack=None):
        self.uid = getattr(self, "uid", 0) + 1
        return (stack or self.es).enter_context(self.nc.sbuf_tensor("sb%d_%s" % (self.uid, name), list(shape), dt))

    def ps(self, name, shape, dt, stack=None):
        self.uid = getattr(self, "uid", 0) + 1
        t = (stack or self.es).enter_context(self.nc.psum_tensor("ps%d_%s" % (self.uid, name), list(shape), dt))
        PSUM_IDS.add(id(t))
        return t

    def op(self, eng, fn, reads=(), writes=()):
        if getattr(self, 'disabled', False):
            return None
        self.nops = getattr(self, 'nops', 0) + 1
        if self.nops > KMAXOPS:
            self.disabled = True
            return None
        waits = {}

        def add(ev):
            if ev is None:
                return
            s, v = ev
            if waits.get(s, 0) < v:
                waits[s] = v
        for b in reads:
            if b.multi:
                for s, v in b.ws.items():
                    add((s, v))
            else:
                add(b.w)
                if b.psum:
                    for s, v in b.r.items():
                        add((s, v))
        for b in writes:
            if b.multi:
                continue
            add(b.w)
            for s, v in b.r.items():
                add((s, v))
        own = set(id(s) for s in eng.sems)
        for s, v in waits.items():
            if id(s) in own and not eng.selfwait:
                continue
            key = id(s)
            if eng.seen.get(key, 0) < v:
                eng.h.wait_ge(s, v)
                eng.seen[key] = v
                self.nins += 1
        i = eng.rr
        eng.rr = (eng.rr + 1) % len(eng.sems)
        if eng.inc == 16 and eng.counts[i] > 0 and eng.seen.get(id(eng.sems[i]), 0) < eng.counts[i]:
            eng.h.wait_ge(eng.sems[i], eng.counts[i])
            eng.seen[id(eng.sems[i])] = eng.counts[i]
            self.nins += 1
        ins = fn()
        eng.counts[i] += eng.inc
        ins.then_inc(eng.sems[i], eng.inc)
        self.nins += 1
        ev = (eng.sems[i], eng.counts[i])
        for b in writes:
            if b.multi:
                if b.ws.get(ev[0], 0) < ev[1]:
                    b.ws[ev[0]] = ev[1]
            else:
                b.w = ev
                b.r = {}
        for b in reads:
            if not b.multi:
                if b.r.get(ev[0], 0) < ev[1]:
                    b.r[ev[0]] = ev[1]
        return ins

    def dma(self, eng, out, in_, reads=(), writes=()):
        return self.op(eng, lambda: eng.h.dma_start(out=out, in_=in_), reads, writes)

    def barrier(self):
        if getattr(self, 'disabled', False):
            return
        for e in self.engs:
            if e is self.PQ:
                continue
            for o in self.engs:
                for s, c in zip(o.sems, o.counts):
                    if c > 0 and e.seen.get(id(s), 0) < c and not (o is e and not e.selfwait):
                        e.h.wait_ge(s, c)
                        e.seen[id(s)] = c
                        self.nins += 1

    def final_wait(self):
        e = self.SP
        for o in self.engs:
            for s, c in zip(o.sems, o.counts):
                if c > 0:
                    e.h.wait_ge(s, c)


def build_program(NSEQ, TL, dbg=None):
    import os
    NST = int(os.environ.get('KSTAGES', '9'))
    NBC = NSEQ * 256 // TB
    NBL = TL // TB
    NBLK = NBC + NBL
    NTOK = NBLK * TB
    NTL = TL // 128
    NSL = NBL // 4
    NXR = NTOK + NSL * TB
    NYR = (NBC + NSL) * TB
    nc = bass.Bass("TRN2", target_bir_lowering=False)

    def din(name, shape, dt=F32):
        return nc.dram_tensor(name, list(shape), dt, kind="ExternalInput").ap()

    def dscr(name, shape, dt):
        return nc.dram_tensor(name, list(shape), dt, kind="Internal").ap()
    x_all = din("x_all", [NXR, D])
    msel_d = din("msel", [128, NSL, NBL])
    cfm_d = din("cfm", [128, DC, 2])
    st_lat = din("st_lat", [2, 32, 64, 64])
    ada_w = din("ada_w", [D, 6 * D])
    w_in = din("w_in", [D, 11520])
    decay_up = din("decay_up", [2, 64, D])
    iclr_up = din("iclr_up", [2, 64, D])
    gate_up = din("gate_up", [128, D])
    w_out_a = din("w_out_a", [D, D])
    w_fourier = din("w_fourier", [1024, D])
    w_out = din("w_out", [D, D])
    ffn_w_in = din("ffn_w_in", [D, 2 * DFF])
    ffn_w_down = din("ffn_w_down", [DFF, D])
    pvec_d = din("pvec", [128, NPV])
    ident_f_d = din("ident_f", [128, 128])
    cb_d = din("cbf", [128, 6, 128], BF16)
    blk64_d = din("blk64", [128, 128])
    masks_d = din("masks", [128, 2, 6, 128], BF16)
    rmask_d = din("rmask", [128, TB])
    ct256_d = din("ct256", [256, 256], BF16)
    st256_d = din("st256", [256, 256], BF16)
    ctL_d = din("ctL", [TL, TL], BF16)
    stL_d = din("stL", [TL, TL], BF16)
    cc_d = din("ccm", [256, 256], BF16)
    scn_d = din("scn", [256, 256], BF16)
    y_all = nc.dram_tensor("y_all", [NYR, D], F32, kind="ExternalOutput").ap()
    st_out = nc.dram_tensor("st_out", [NSEQ, 2, 32, 64, 64], F32, kind="ExternalOutput").ap()
    FMs = Buf(dscr("FMs", [NBLK, 16, 2, 128, 4, 4, 128], BF16), multi=True)
    TMs = Buf(dscr("TMs", [NBLK, 16, 2, 128, 4, 2, 128], BF16), multi=True)
    TVs = Buf(dscr("TVs", [NBLK, 16, 128, 4, 128], BF16), multi=True)
    SCs = Buf(dscr("SCs", [NBLK, 16, 2, 128, 12], F32), multi=True)
    BGs = Buf(dscr("BGs", [NBLK, 16, 2, 128, TB], F32), multi=True)
    Ys = Buf(dscr("Ys", [NBLK, 16, 2, 128, TB], F32), multi=True)
    XBs = Buf(dscr("XBs", [NTOK, 1024], BF16), multi=True)
    FSs = Buf(dscr("FSs", [8, 128, NTOK], BF16), multi=True)
    YAsel = Buf(dscr("YAsel", [max(NSL, 1), 16, 128, TB], BF16), multi=True)
    FBsel = Buf(dscr("FBsel", [max(NSL, 1), 8, 128, TB], BF16), multi=True)
    DX = Buf(None, multi=True)

    es = contextlib.ExitStack()
    with es:
        kb = KB(nc, es)
        PE, ACT, DVE, POOL, SP, PQ = kb.PE, kb.ACT, kb.DVE, kb.POOL, kb.SP, kb.PQ

        def stage_gate(k):
            if NST < k:
                kb.disabled = True
        pvec = Buf(kb.sb("pvec", [128, NPV], F32))
        ident_f = Buf(kb.sb("ident_f", [128, 128], F32))
        cb = Buf(kb.sb("cb", [128, 6, 128], BF16))
        blk64 = Buf(kb.sb("blk64", [128, 128], F32))
        masks = Buf(kb.sb("masks", [128, 2, 6, 128], BF16))
        rmask = Buf(kb.sb("rmask", [128, TB], F32))
        cfm = Buf(kb.sb("cfm_s", [128, DC, 2], F32))
        modv = Buf(kb.sb("modv", [128, 96, 2], F32))
        eff = Buf(kb.sb("eff", [128, 2, DC, 2], F32))
        for b_, d_ in [(pvec, pvec_d), (ident_f, ident_f_d), (cb, cb_d), (blk64, blk64_d), (masks, masks_d),
                       (rmask, rmask_d), (cfm, cfm_d)]:
            kb.dma(SP, b_.t[:], d_, reads=[DX], writes=[b_])
        ident_b = cb.t[:, 0, :]
        blk1 = cb.t[:, 1, :]
        ones_b = cb.t[:, 2, :]

        def pv(name, col):
            c0 = PV[name] + col
            return pvec.t[:, c0:c0 + 1]

        def make_wpool(stack, n, nbytes_elems):
            return [Buf(kb.sb("wp%d_%d" % (i, nbytes_elems), [128, nbytes_elems], BF16, stack)) for i in range(n)]

        class WStream:
            def __init__(self, pool):
                self.pool = pool
                self.i = 0

            def load(self, src_ap, kc, ncols):
                b = self.pool[self.i % len(self.pool)]
                self.i += 1
                v = b.t[:, 0:kc * ncols].rearrange("p (k n) -> p k n", k=kc)
                if isinstance(src_ap, list):
                    o = 0
                    for sa in src_ap:
                        w_ = sa.shape[2]
                        kb.dma(PQ, v[:, :, o:o + w_], sa, reads=[DX], writes=[b])
                        o += w_
                else:
                    kb.dma(PQ, v, src_ap, reads=[DX], writes=[b])
                return b, v

        stage_gate(0)
        with contextlib.ExitStack() as st:
            wpool = make_wpool(st, 3, 16 * 512)
            ws = WStream(wpool)
            scb = Buf(kb.sb("scb", [128, DC, 2], BF16, st))
            pm = [Buf(kb.ps("pm0_%d" % i, [128, 512], F32, st)) for i in range(2)]
            kb.op(ACT, lambda: nc.scalar.activation(out=scb.t[:], in_=cfm.t[:], func=AF.Silu), [cfm], [scb])
            adv = ada_w.rearrange("(k p) n -> p k n", p=128)
            for g in range(24):
                wb, wv = ws.load(adv[:, :, g * 512:(g + 1) * 512], 16, 512)
                for j in range(4):
                    m = g * 4 + j
                    pb = pm[m % 2]
                    for k in range(16):
                        kb.op(PE, lambda k=k, j=j, pb=pb, wv=wv: nc.tensor.matmul(
                            pb.t[:, 0:2], lhsT=wv[:, k, j * 128:(j + 1) * 128], rhs=scb.t[:, k, :],
                            start=(k == 0), stop=(k == 15)), [wb, scb], [pb])
                    kb.op(ACT, lambda m=m, pb=pb: nc.scalar.activation(
                        out=modv.t[:, m, :], in_=pb.t[:, 0:2], func=AF.Identity, bias=pv("ada_b", m)),
                        [pb, pvec], [modv])
            for w_, (gn, mo) in enumerate([("g1", 16), ("g2", 64)]):
                kb.op(DVE, lambda w_=w_, mo=mo: nc.vector.tensor_scalar(
                    out=eff.t[:, w_, :, :], in0=modv.t[:, mo:mo + 16, :], scalar1=1.0, scalar2=None, op0=ALU.add),
                    [modv], [eff])
                g0 = PV[gn]
                kb.op(DVE, lambda w_=w_, g0=g0: nc.vector.tensor_tensor(
                    out=eff.t[:, w_, :, :], in0=eff.t[:, w_, :, :],
                    in1=pvec.t[:, g0:g0 + 16].rearrange("p (c o) -> p c o", o=1).broadcast_to([128, 16, 2]),
                    op=ALU.mult), [eff, pvec], [eff])
        kb.barrier()

        def modcol(s, c, r):
            return modv.t[:, s * 16 + c, r:r + 1]

        def load_xT(blk, xtm, xT, ptr):
            for t in range(4):
                xb_ = xtm[t % 2]
                r0 = blk * TB + t * 128
                kb.dma(SP, xb_.t[:], x_all[r0:r0 + 128, :], reads=[DX], writes=[xb_])
                for g in range(4):
                    pb = ptr[(t * 4 + g) % len(ptr)]
                    for j in range(4):
                        c = g * 4 + j
                        kb.op(PE, lambda pb=pb, j=j, c=c, xb_=xb_: nc.tensor.transpose(
                            pb.t[:, j, :], xb_.t[:, c * 128:(c + 1) * 128], ident_f.t[:]), [xb_, ident_f], [pb])
                    for j in range(4):
                        c = g * 4 + j
                        e = ACT if j % 2 == 0 else DVE
                        if e is ACT:
                            kb.op(ACT, lambda pb=pb, j=j, c=c, t=t: nc.scalar.copy(
                                out=xT[c].t[:, t * 128:(t + 1) * 128], in_=pb.t[:, j, :]), [pb], [xT[c]])
                        else:
                            kb.op(DVE, lambda pb=pb, j=j, c=c, t=t: nc.vector.tensor_copy(
                                out=xT[c].t[:, t * 128:(t + 1) * 128], in_=pb.t[:, j, :]), [pb], [xT[c]])

        def rstd_of(xT, sqb, pss, rstd):
            for c in range(16):
                sq = sqb[c % 2]
                kb.op(ACT, lambda c=c, sq=sq: nc.scalar.activation(out=sq.t[:], in_=xT[c].t[:], func=AF.Square),
                      [xT[c]], [sq])
                kb.op(PE, lambda c=c, sq=sq: nc.tensor.matmul(pss.t[:], lhsT=ones_b, rhs=sq.t[:],
                                                             start=(c == 0), stop=(c == 15)), [sq, cb], [pss])
            kb.op(DVE, lambda: nc.vector.tensor_scalar(out=rstd.t[:], in0=pss.t[:], scalar1=1.0 / D, scalar2=RMS_EPS,
                                                       op0=ALU.mult, op1=ALU.add), [pss], [rstd])
            kb.op(ACT, lambda: nc.scalar.activation(out=rstd.t[:], in_=rstd.t[:], func=AF.Ln), [rstd], [rstd])
            kb.op(ACT, lambda: nc.scalar.activation(out=rstd.t[:], in_=rstd.t[:], func=AF.Exp, scale=-0.5),
                  [rstd], [rstd])

        def norm_mod(xT, rstd, hT, tmpf, which, r):
            sh = 0 if which == 0 else 3
            for c in range(16):
                tf = tmpf[c % 2]
                kb.op(DVE, lambda c=c, tf=tf: nc.vector.scalar_tensor_tensor(
                    out=tf.t[:], in0=xT[c].t[:], scalar=eff.t[:, which, c, r:r + 1], in1=rstd.t[:],
                    op0=ALU.mult, op1=ALU.mult), [xT[c], eff, rstd], [tf])
                kb.op(ACT, lambda c=c, tf=tf: nc.scalar.activation(
                    out=hT[c].t[:], in_=tf.t[:], func=AF.Identity, bias=modcol(sh, c, r)), [tf, modv], [hT[c]])

        def conv3(ps, out, wname, bname, col, nwc, rl, e2=None):
            nr = TB // rl
            kb.op(ACT, lambda: nc.scalar.activation(out=out.t[:], in_=ps.t[:], func=AF.Identity,
                                                    bias=pv(bname, col), scale=pv(wname, nwc + col)),
                  [ps, pvec], [out])
            ov = out.t[:].rearrange("p (a b) -> p a b", b=rl)
            pv_ = ps.t[:].rearrange("p (a b) -> p a b", b=rl)
            kb.op(DVE, lambda: nc.vector.scalar_tensor_tensor(
                out=ov[:, :, 1:rl], in0=pv_[:, :, 0:rl - 1], scalar=pv(wname, col), in1=ov[:, :, 1:rl],
                op0=ALU.mult, op1=ALU.add), [ps, pvec, out], [out])
            kb.op(DVE, lambda: nc.vector.scalar_tensor_tensor(
                out=ov[:, :, 0:rl - 1], in0=pv_[:, :, 1:rl], scalar=pv(wname, 2 * nwc + col), in1=ov[:, :, 0:rl - 1],
                op0=ALU.mult, op1=ALU.add), [ps, pvec, out], [out])

        def blk_info(blk):
            if blk < NBC:
                return 0, 256
            return 1, 64

        stage_gate(1)
        with contextlib.ExitStack() as st:
            xtm = [Buf(kb.sb("xtm%d" % i, [128, D], F32, st)) for i in range(2)]
            xT = [Buf(kb.sb("xT%d" % i, [128, TB], F32, st)) for i in range(16)]
            hT = [Buf(kb.sb("hT%d" % i, [128, TB], BF16, st)) for i in range(16)]
            sqb = [Buf(kb.sb("sqb%d" % i, [128, TB], BF16, st)) for i in range(2)]
            tmpf = [Buf(kb.sb("tmpf%d" % i, [128, TB], F32, st)) for i in range(2)]
            rstd = Buf(kb.sb("rstd", [128, TB], F32, st))
            ws = WStream(make_wpool(st, 2, 16 * 512))
            dup = Buf(kb.sb("dup", [128, 2, D], BF16, st))
            gup = Buf(kb.sb("gup", [128, D], BF16, st))
            for d in range(2):
                kb.dma(PQ, dup.t[0:64, d, :], decay_up[d], reads=[DX], writes=[dup])
                kb.dma(PQ, dup.t[64:128, d, :], iclr_up[d], reads=[DX], writes=[dup])
            kb.dma(PQ, gup.t[:], gate_up, reads=[DX], writes=[gup])
            lora = Buf(kb.sb("lora", [128, 2, TB], BF16, st))
            F = lambda n: Buf(kb.sb(n, [128, TB], F32, st))
            r_c, k_c, v_c, kk, kkn, t1, t2, sig, aa, Lf, Lm, E1, E2, E3, kd, bb, bon, gg = [
                F("f1_%d" % i) for i in range(18)]
            sqk = Buf(kb.sb("sqk", [128, TB], BF16, st))
            vb = Buf(kb.sb("vb", [128, TB], BF16, st))
            FMo = [Buf(kb.sb("FMo%d" % i, [128, 4, 4, 128], BF16, st)) for i in range(2)]
            TMo = [Buf(kb.sb("TMo%d" % i, [128, 4, 2, 128], BF16, st)) for i in range(2)]
            TVo = Buf(kb.sb("TVo", [128, 4, 128], BF16, st))
            SCo = [Buf(kb.sb("SCo%d" % i, [128, 12], F32, st)) for i in range(2)]
            xbo = [Buf(kb.sb("xbo%d" % i, [128, 512], BF16, st)) for i in range(2)]
            ptr = [Buf(kb.ps("ptr%d" % i, [128, 4, 128], F32, st)) for i in range(2)]
            pp = [Buf(kb.ps("pp%d" % i, [128, TB], F32, st)) for i in range(4)]
            ptb = [Buf(kb.ps("ptb%d" % i, [128, 8, 128], BF16, st)) for i in range(2)]
            ppi = [0]

            def nextp():
                b_ = pp[ppi[0] % 4]
                ppi[0] += 1
                return b_
            wv_in = w_in.rearrange("(k p) n -> p k n", p=128)
            for blk in range(NBLK):
                r, rl = blk_info(blk)
                load_xT(blk, xtm, xT, ptr)
                pss = nextp()
                rstd_of(xT, sqb, pss, rstd)
                norm_mod(xT, rstd, hT, tmpf, 0, r)
                wb, wv = ws.load(wv_in[:, :, 11264:11520], 16, 256)
                p0 = nextp()
                p1 = nextp()
                for j, pb in enumerate([p0, p1]):
                    for k in range(16):
                        kb.op(PE, lambda k=k, j=j, pb=pb, wv=wv: nc.tensor.matmul(
                            pb.t[:], lhsT=wv[:, k, j * 128:(j + 1) * 128], rhs=hT[k].t[:], start=(k == 0),
                            stop=(k == 15)), [wb, hT[k]], [pb])
                kb.op(ACT, lambda: nc.scalar.activation(out=lora.t[0:64, 0, :], in_=p0.t[0:64, :], func=AF.Tanh),
                      [p0], [lora])
                kb.op(ACT, lambda: nc.scalar.copy(out=lora.t[64:128, 0, :], in_=p0.t[64:128, :]), [p0], [lora])
                kb.op(ACT, lambda: nc.scalar.activation(out=lora.t[:, 1, :], in_=p1.t[:], func=AF.Sigmoid),
                      [p1], [lora])
                for g2 in range(2):
                    wb, wv = ws.load(wv_in[:, :, 6144 + g2 * 512:6144 + (g2 + 1) * 512], 16, 512)
                    for t in range(4):
                        pb = nextp()
                        for k in range(16):
                            kb.op(PE, lambda k=k, t=t, pb=pb, wv=wv: nc.tensor.matmul(
                                pb.t[:], lhsT=hT[k].t[:, t * 128:(t + 1) * 128], rhs=wv[:, k, :], start=(k == 0),
                                stop=(k == 15)), [wb, hT[k]], [pb])
                        xo = xbo[(g2 * 4 + t) % 2]
                        kb.op(ACT, lambda pb=pb, xo=xo: nc.scalar.copy(out=xo.t[:], in_=pb.t[:]), [pb], [xo])
                        r0 = blk * TB + t * 128
                        kb.dma(SP, XBs.t[r0:r0 + 128, g2 * 512:(g2 + 1) * 512], xo.t[:], reads=[xo], writes=[XBs])
                for hp in range(16):
                    wb, wv = ws.load([wv_in[:, :, s3 * 2048 + hp * 128:s3 * 2048 + (hp + 1) * 128] for s3 in range(3)], 16, 384)
                    pr, pk, pvv = nextp(), nextp(), nextp()
                    for s_, pb in enumerate([pr, pk, pvv]):
                        for k in range(16):
                            kb.op(PE, lambda k=k, s_=s_, pb=pb, wv=wv: nc.tensor.matmul(
                                pb.t[:], lhsT=wv[:, k, s_ * 128:(s_ + 1) * 128], rhs=hT[k].t[:], start=(k == 0),
                                stop=(k == 15)), [wb, hT[k]], [pb])
                    conv3(pr, r_c, "rcw", "rcb", hp, 48, rl)
                    conv3(pk, k_c, "rcw", "rcb", 16 + hp, 48, rl)
                    conv3(pvv, v_c, "rcw", "rcb", 32 + hp, 48, rl)
                    kb.op(ACT, lambda: nc.scalar.activation(out=kk.t[:], in_=k_c.t[:], func=AF.Identity,
                                                            scale=pv("k_k", hp)), [k_c, pvec], [kk])
                    kb.op(ACT, lambda: nc.scalar.activation(out=sqk.t[:], in_=kk.t[:], func=AF.Square), [kk], [sqk])
                    pn = nextp()
                    kb.op(PE, lambda pn=pn: nc.tensor.matmul(pn.t[:], lhsT=blk1, rhs=sqk.t[:], start=True, stop=True),
                          [sqk, cb], [pn])
                    kb.op(DVE, lambda pn=pn: nc.vector.tensor_scalar(out=t1.t[:], in0=pn.t[:], scalar1=1e-12,
                                                                    scalar2=None, op0=ALU.add), [pn], [t1])
                    kb.op(ACT, lambda: nc.scalar.activation(out=t1.t[:], in_=t1.t[:], func=AF.Ln), [t1], [t1])
                    kb.op(ACT, lambda: nc.scalar.activation(out=t1.t[:], in_=t1.t[:], func=AF.Exp, scale=-0.5),
                          [t1], [t1])
                    kb.op(POOL, lambda: nc.gpsimd.tensor_tensor(out=kkn.t[:], in0=kk.t[:], in1=t1.t[:], op=ALU.mult),
                          [kk, t1], [kkn])
                    kb.op(DVE, lambda: nc.vector.tensor_tensor(out=t2.t[:], in0=r_c.t[:], in1=k_c.t[:], op=ALU.mult),
                          [r_c, k_c], [t2])
                    kb.op(ACT, lambda: nc.scalar.activation(out=sqk.t[:], in_=t2.t[:], func=AF.Identity,
                                                            scale=pv("r_k", hp)), [t2, pvec], [sqk])
                    pn2 = nextp()
                    kb.op(PE, lambda pn2=pn2: nc.tensor.matmul(pn2.t[:], lhsT=blk1, rhs=sqk.t[:], start=True,
                                                               stop=True), [sqk, cb], [pn2])
                    kb.op(DVE, lambda pn2=pn2: nc.vector.tensor_tensor(out=bon.t[:], in0=pn2.t[:], in1=v_c.t[:],
                                                                      op=ALU.mult), [pn2, v_c], [bon])
                    kb.dma(SP, BGs.t[blk, hp, 0], bon.t[:], reads=[bon], writes=[BGs])
                    pg = nextp()
                    kb.op(PE, lambda pg=pg: nc.tensor.matmul(pg.t[:], lhsT=gup.t[:, hp * 128:(hp + 1) * 128],
                                                             rhs=lora.t[:, 1, :], start=True, stop=True),
                          [gup, lora], [pg])
                    kb.op(ACT, lambda pg=pg: nc.scalar.copy(out=gg.t[:], in_=pg.t[:]), [pg], [gg])
                    kb.dma(SP, BGs.t[blk, hp, 1], gg.t[:], reads=[gg], writes=[BGs])
                    kb.op(ACT, lambda: nc.scalar.copy(out=vb.t[:], in_=v_c.t[:]), [v_c], [vb])
                    pt_ = ptb[0]
                    for c in range(4):
                        kb.op(PE, lambda c=c, pt_=pt_: nc.tensor.transpose(pt_.t[:, c, :], vb.t[:, c * 128:(c + 1) * 128],
                                                                           ident_b), [vb, cb], [pt_])
                    kb.op(DVE, lambda pt_=pt_: nc.vector.tensor_copy(out=TVo.t[:], in_=pt_.t[:, 0:4, :]), [pt_], [TVo])
                    kb.dma(SP, TVs.t[blk, hp], TVo.t[:], reads=[TVo], writes=[TVs])
                    for d in range(2):
                        fm = FMo[d]
                        tm = TMo[d]
                        sc = SCo[d]
                        pw, pa = nextp(), nextp()
                        kb.op(PE, lambda pw=pw, d=d: nc.tensor.matmul(
                            pw.t[:], lhsT=dup.t[0:64, d, hp * 128:(hp + 1) * 128], rhs=lora.t[0:64, 0, :],
                            start=True, stop=True), [dup, lora], [pw])
                        kb.op(PE, lambda pa=pa, d=d: nc.tensor.matmul(
                            pa.t[:], lhsT=dup.t[64:128, d, hp * 128:(hp + 1) * 128], rhs=lora.t[64:128, 0, :],
                            start=True, stop=True), [dup, lora], [pa])
                        kb.op(ACT, lambda pw=pw, d=d: nc.scalar.activation(
                            out=sig.t[:], in_=pw.t[:], func=AF.Sigmoid, bias=pv("dbase", d * 16 + hp)),
                            [pw, pvec], [sig])
                        kb.op(ACT, lambda pa=pa, d=d: nc.scalar.activation(
                            out=aa.t[:], in_=pa.t[:], func=AF.Sigmoid, bias=pv("ibase", d * 16 + hp)),
                            [pa, pvec], [aa])
                        kb.op(DVE, lambda: nc.vector.tensor_tensor_scan(
                            out=Lf.t[:], data0=rmask.t[:], data1=sig.t[:], initial=0.0, op0=ALU.mult, op1=ALU.add),
                            [rmask, sig], [Lf])
                        L3 = Lf.t[:].rearrange("p (c t) -> p c t", t=128)
                        Lm3 = Lm.t[:].rearrange("p (c t) -> p c t", t=128)
                        if d == 0:
                            mi, ei = 63, 127
                            Lsrc = Lf
                        else:
                            kb.op(DVE, lambda: nc.vector.tensor_tensor(out=t1.t[:], in0=sig.t[:], in1=Lf.t[:],
                                                                       op=ALU.subtract), [sig, Lf], [t1])
                            t13 = t1.t[:].rearrange("p (c t) -> p c t", t=128)
                            kb.op(DVE, lambda t13=t13, L3=L3: nc.vector.tensor_tensor(
                                out=t13, in0=t13, in1=L3[:, :, 127:128].broadcast_to([128, 4, 128]), op=ALU.add),
                                [t1, Lf], [t1])
                            mi, ei = 64, 0
                            Lsrc = t1
                            L3 = t13
                        kb.op(ACT, lambda L3=L3, mi=mi, sc=sc: nc.scalar.activation(
                            out=sc.t[:, 0:4], in_=L3[:, :, mi], func=AF.Exp, scale=CDEC), [Lsrc], [sc])
                        kb.op(ACT, lambda L3=L3, ei=ei, sc=sc: nc.scalar.activation(
                            out=sc.t[:, 4:8], in_=L3[:, :, ei], func=AF.Exp, scale=CDEC), [Lsrc], [sc])
                        kb.op(DVE, lambda L3=L3, mi=mi, Lm3=Lm3: nc.vector.tensor_tensor(
                            out=Lm3, in0=L3, in1=L3[:, :, mi:mi + 1].broadcast_to([128, 4, 128]), op=ALU.subtract),
                            [Lsrc], [Lm])
                        kb.op(ACT, lambda Lm3=Lm3, ei=ei, sc=sc: nc.scalar.activation(
                            out=sc.t[:, 8:12], in_=Lm3[:, :, ei], func=AF.Exp, scale=CDEC), [Lm], [sc])
                        kb.op(ACT, lambda: nc.scalar.activation(out=E1.t[:], in_=Lm.t[:], func=AF.Exp, scale=CDEC),
                              [Lm], [E1])
                        kb.op(ACT, lambda: nc.scalar.activation(out=E3.t[:], in_=Lm.t[:], func=AF.Exp, scale=-CDEC),
                              [Lm], [E3])
                        kb.op(POOL, lambda: nc.gpsimd.tensor_tensor(out=t2.t[:], in0=Lm.t[:], in1=sig.t[:],
                                                                    op=ALU.subtract), [Lm, sig], [t2])
                        kb.op(ACT, lambda: nc.scalar.activation(out=E2.t[:], in_=t2.t[:], func=AF.Exp, scale=CDEC),
                              [t2], [E2])
                        kb.op(DVE, lambda: nc.vector.tensor_scalar(out=kd.t[:], in0=aa.t[:], scalar1=-1.0,
                                                                   scalar2=pv("k_a", hp), op0=ALU.add, op1=ALU.mult),
                              [aa, pvec], [kd])
                        kb.op(DVE, lambda: nc.vector.scalar_tensor_tensor(out=kd.t[:], in0=kd.t[:], scalar=1.0,
                                                                          in1=k_c.t[:], op0=ALU.add, op1=ALU.mult),
                              [kd, k_c], [kd])
                        kb.op(POOL, lambda: nc.gpsimd.tensor_tensor(out=bb.t[:], in0=kkn.t[:], in1=aa.t[:],
                                                                    op=ALU.mult), [kkn, aa], [bb])

                        def v3(b_):
                            return b_.t[:].rearrange("p (c t) -> p c t", t=128)
                        kb.op(DVE, lambda fm=fm: nc.vector.tensor_tensor(out=fm.t[:, :, 0, :], in0=v3(kkn), in1=v3(E2),
                                                                        op=ALU.mult), [kkn, E2], [fm])
                        kb.op(POOL, lambda fm=fm: nc.gpsimd.tensor_tensor(out=fm.t[:, :, 1, :], in0=v3(r_c), in1=v3(E1),
                                                                         op=ALU.mult), [r_c, E1], [fm])
                        kb.op(DVE, lambda fm=fm: nc.vector.tensor_tensor(out=fm.t[:, :, 2, :], in0=v3(bb), in1=v3(E3),
                                                                        op=ALU.mult), [bb, E3], [fm])
                        kb.op(POOL, lambda fm=fm: nc.gpsimd.tensor_tensor(out=fm.t[:, :, 3, :], in0=v3(kd), in1=v3(E3),
                                                                         op=ALU.mult), [kd, E3], [fm])
                        for q_ in range(2):
                            pt_ = ptb[(q_ + 1) % 2]
                            for c in range(4):
                                kb.op(PE, lambda c=c, pt_=pt_, q_=q_, fm=fm: nc.tensor.transpose(
                                    pt_.t[:, c, :], fm.t[:, c, 2 + q_, :], ident_b), [fm, cb], [pt_])
                            kb.op(DVE if q_ == 0 else ACT,
                                  (lambda pt_=pt_, q_=q_, tm=tm: nc.vector.tensor_copy(out=tm.t[:, :, q_, :], in_=pt_.t[:, 0:4, :]))
                                  if q_ == 0 else
                                  (lambda pt_=pt_, q_=q_, tm=tm: nc.scalar.copy(out=tm.t[:, :, q_, :], in_=pt_.t[:, 0:4, :])),
                                  [pt_], [tm])
                        kb.dma(SP, FMs.t[blk, hp, d], fm.t[:], reads=[fm], writes=[FMs])
                        kb.dma(SP, TMs.t[blk, hp, d], tm.t[:], reads=[tm], writes=[TMs])
                        kb.dma(SP, SCs.t[blk, hp, d], sc.t[:], reads=[sc], writes=[SCs])
        kb.barrier()

        stage_gate(2)
        with contextlib.ExitStack() as st:
            NU = 3
            fmi = [Buf(kb.sb("fmi%d" % i, [128, 4, 4, 128], BF16, st)) for i in range(NU)]
            tmi = [Buf(kb.sb("tmi%d" % i, [128, 4, 2, 128], BF16, st)) for i in range(NU)]
            tvi = [Buf(kb.sb("tvi%d" % i, [128, 4, 128], BF16, st)) for i in range(NU)]
            sci = [Buf(kb.sb("sci%d" % i, [128, 12], F32, st)) for i in range(NU)]
            MBK = [Buf(kb.sb("MBK%d" % c, [128, 2, 4, 128], BF16, st)) for c in range(4)]
            NA = [[Buf(kb.sb("NA%d_%d" % (l, c), [128, 2, 2, 128], BF16, st)) for c in range(4)] for l in range(2)]
            AT0 = [Buf(kb.sb("AT0_%d" % c, [128, 2, 128], BF16, st)) for c in range(4)]
            PT = [Buf(kb.sb("PT%d" % c, [128, 2, 128], BF16, st)) for c in range(4)]
            Zs = [[Buf(kb.sb("Z%d_%d" % (hp, d), [128, 64], F32, st)) for d in range(2)] for hp in range(16)]
            Zb = Buf(kb.sb("Zb", [128, 128], BF16, st))
            kb.op(POOL, lambda: nc.gpsimd.memset(Zb.t[:], 0.0), [], [Zb])
            Xn = Buf(kb.sb("Xn", [128, 2, 64], BF16, st))
            UT = Buf(kb.sb("UT", [128, 2, 64], BF16, st))
            zt = Buf(kb.sb("zt", [128, 64], F32, st))
            yo = [Buf(kb.sb("yo%d" % i, [128, TB], F32, st)) for i in range(2)]
            sti = Buf(kb.sb("sti", [64, 128], F32, st))
            sto = Buf(kb.sb("sto", [64, 128], F32, st))
            pin = [Buf(kb.ps("pin%d" % c, [128, 512], F32, st)) for c in range(4)]
            pX = Buf(kb.ps("pX", [128, 512], F32, st))
            pU = Buf(kb.ps("pU", [128, 512], F32, st))
            pY = Buf(kb.ps("pY", [128, 512], F32, st))
            pZ = Buf(kb.ps("pZ", [128, 512], F32, st))

            def scan_unit(ui, blk, hp, d, chunk_order, seq_starts, seq_ends):
                fm, tm, tv, sc = fmi[ui % NU], tmi[ui % NU], tvi[ui % NU], sci[ui % NU]
                kb.dma(SP, fm.t[:], FMs.t[blk, hp, d], reads=[FMs], writes=[fm])
                kb.dma(SP, tm.t[:], TMs.t[blk, hp, d], reads=[TMs], writes=[tm])
                kb.dma(SP, tv.t[:], TVs.t[blk, hp], reads=[TVs], writes=[tv])
                kb.dma(SP, sc.t[:], SCs.t[blk, hp, d], reads=[SCs], writes=[sc])
                Z = Zs[hp][d]
                mk = masks.t[:, d, 0:4, :]
                for c in range(4):
                    for e in range(2):
                        pb = pin[(2 * c + e) % 4]
                        rs = slice(e * 64, e * 64 + 64)
                        for q_ in range(2):
                            kb.op(PE, lambda pb=pb, rs=rs, c=c, q_=q_: nc.tensor.matmul(
                                pb.t[:, q_ * 256:(q_ + 1) * 256], lhsT=fm.t[rs, c, 2 + q_, :],
                                rhs=fm.t[rs, c, 0:2, :], start=True, stop=True), [fm], [pb])
                        kb.op(DVE, lambda pb=pb, c=c, e=e: nc.vector.tensor_tensor(
                            out=MBK[c].t[:, e, :, :], in0=pb.t[:].rearrange("p (a b) -> p a b", b=128), in1=mk,
                            op=ALU.mult), [pb, masks], [MBK[c]])
                for c in range(4):
                    for e in range(2):
                        pb = pin[(2 * c + e) % 4]
                        rs = slice(e * 64, e * 64 + 64)
                        kb.op(PE, lambda pb=pb, rs=rs, c=c, e=e: nc.tensor.matmul(
                            pb.t[:, 0:128], lhsT=fm.t[rs, c, 0, :], rhs=fm.t[rs, c, 2, :],
                            start=True, stop=True), [fm], [pb])
                        kb.op(DVE, lambda pb=pb, c=c, e=e: nc.vector.tensor_tensor(
                            out=AT0[c].t[:, e, :], in0=pb.t[:, 0:128],
                            in1=masks.t[:, d, 4, :], op=ALU.mult), [pb, masks], [AT0[c]])
                    kb.op(POOL, lambda c=c: nc.gpsimd.tensor_tensor(
                        out=PT[c].t[:], in0=cb.t[:, 0:1, :].broadcast_to([128, 2, 128]), in1=MBK[c].t[:, :, 0, :],
                        op=ALU.subtract), [cb, MBK[c]], [PT[c]])
                for lev in range(6):
                    cur = NA[lev % 2]
                    prv = NA[(lev + 1) % 2]
                    for c in range(4):
                        pb = pin[c]
                        for e in range(2):
                            if lev == 0:
                                Np, Ap = MBK[c].t[:, e, 0, :], AT0[c].t[:, e, :]
                                rd = [MBK[c], AT0[c]]
                            else:
                                Np, Ap = prv[c].t[:, e, 0, :], prv[c].t[:, e, 1, :]
                                rd = [prv[c]]
                            kb.op(PE, lambda pb=pb, e=e, Np=Np, Ap=Ap: nc.tensor.matmul(
                                pb.t[:, e * 256:e * 256 + 128], lhsT=Ap, rhs=Np, start=True, stop=True), rd, [pb])
                            kb.op(PE, lambda pb=pb, e=e, Np=Np, Ap=Ap: nc.tensor.matmul(
                                pb.t[:, e * 256 + 128:e * 256 + 256], lhsT=Np, rhs=Ap, start=True, stop=True), rd, [pb])
                    for c in range(4):
                        pb = pin[c]
                        if c % 2 == 0:
                            kb.op(ACT, lambda pb=pb, c=c, cur=cur: nc.scalar.copy(
                                out=cur[c].t[:].rearrange("p a b c -> p (a b c)"), in_=pb.t[:]), [pb], [cur[c]])
                        else:
                            kb.op(DVE, lambda pb=pb, c=c, cur=cur: nc.vector.tensor_copy(
                                out=cur[c].t[:].rearrange("p a b c -> p (a b c)"), in_=pb.t[:]), [pb], [cur[c]])
                    for c in range(4):
                        pb = pin[c]
                        for e in range(2):
                            kb.op(PE, lambda pb=pb, e=e, c=c, cur=cur: nc.tensor.matmul(
                                pb.t[:, e * 128:(e + 1) * 128], lhsT=cur[c].t[:, e, 1, :], rhs=PT[c].t[:, e, :],
                                start=True, stop=True), [cur[c], PT[c]], [pb])
                    for c in range(4):
                        pb = pin[c]
                        kb.op(DVE, lambda pb=pb, c=c: nc.vector.tensor_tensor(
                            out=PT[c].t[:], in0=pb.t[:, 0:256].rearrange("p (a b) -> p a b", b=128), in1=PT[c].t[:],
                            op=ALU.add), [pb, PT[c]], [PT[c]])
                yb = yo[ui % 2]
                for c in chunk_order:
                    if c in seq_starts:
                        kind, arg = seq_starts[c]
                        if kind == "zero":
                            kb.op(POOL, lambda Z=Z: nc.gpsimd.memset(Z.t[:], 0.0), [], [Z])
                        elif kind == "load":
                            kb.dma(SP, sti.t[:].rearrange("v (e k) -> v e k", e=2),
                                   st_lat[d, 2 * hp:2 * hp + 2].rearrange("e v k -> v e k"), reads=[DX], writes=[sti])
                            kb.op(PE, lambda: nc.tensor.transpose(pZ.t[:, 0:64], sti.t[:], ident_f.t[0:64, 0:64]),
                                  [sti, ident_f], [pZ])
                            kb.op(DVE, lambda Z=Z: nc.vector.tensor_copy(out=Z.t[:], in_=pZ.t[:, 0:64]), [pZ], [Z])
                    for e in range(2):
                        rs = slice(e * 64, e * 64 + 64)
                        kb.op(ACT, lambda Z=Z, c=c, e=e, rs=rs: nc.scalar.activation(
                            out=Zb.t[rs, e * 64:(e + 1) * 64], in_=Z.t[rs, :], func=AF.Identity,
                            scale=sc.t[rs, c:c + 1]), [Z, sc], [Zb])
                    kb.op(PE, lambda c=c: nc.tensor.matmul(
                        pX.t[:, 0:128], lhsT=fm.t[:, c, 0, :], rhs=Zb.t[:], start=True, stop=False, skip_group_check=True), [fm, Zb], [pX])
                    for e in range(2):
                        kb.op(PE, lambda e=e, c=c: nc.tensor.matmul(
                            pX.t[:, e * 64:(e + 1) * 64], lhsT=MBK[c].t[:, e, 2, :], rhs=tv.t[:, c, e * 64:(e + 1) * 64],
                            start=False, stop=(e == 1), skip_group_check=True), [MBK[c], tv], [pX])
                    kb.op(ACT, lambda: nc.scalar.mul(out=Xn.t[:].rearrange("p a b -> p (a b)"), in_=pX.t[:, 0:128],
                                                     mul=-1.0), [pX], [Xn])
                    for e in range(2):
                        kb.op(PE, lambda e=e, c=c: nc.tensor.matmul(
                            pU.t[:, e * 64:(e + 1) * 64], lhsT=PT[c].t[:, e, :], rhs=Xn.t[:, e, :], start=True,
                            stop=True), [PT[c], Xn], [pU])
                    kb.op(DVE, lambda: nc.vector.tensor_copy(out=UT.t[:].rearrange("p a b -> p (a b)"),
                                                             in_=pU.t[:, 0:128]), [pU], [UT])
                    kb.op(PE, lambda c=c: nc.tensor.matmul(
                        pY.t[:, 0:128], lhsT=Zb.t[:], rhs=fm.t[:, c, 1, :], start=True, stop=False, skip_group_check=True), [Zb, fm], [pY])
                    for e in range(2):
                        rs = slice(e * 64, e * 64 + 64)
                        kb.op(PE, lambda e=e, rs=rs, c=c: nc.tensor.matmul(
                            pY.t[rs, 0:128], lhsT=UT.t[:, e, :], rhs=MBK[c].t[:, e, 1, :], start=False, stop=False,
                            skip_group_check=True),
                            [UT, MBK[c]], [pY])
                        kb.op(PE, lambda e=e, rs=rs, c=c: nc.tensor.matmul(
                            pY.t[rs, 0:128], lhsT=tv.t[:, c, e * 64:(e + 1) * 64], rhs=MBK[c].t[:, e, 3, :],
                            start=False, stop=(e == 1), skip_group_check=True), [tv, MBK[c]], [pY])
                    kb.op(ACT, lambda c=c, yb=yb: nc.scalar.copy(out=yb.t[:, c * 128:(c + 1) * 128], in_=pY.t[:, 0:128]),
                          [pY], [yb])
                    for e in range(2):
                        rs = slice(e * 64, e * 64 + 64)
                        kb.op(PE, lambda e=e, rs=rs, c=c: nc.tensor.matmul(
                            pZ.t[rs, 0:64], lhsT=tm.t[:, c, 0, e * 64:(e + 1) * 64], rhs=UT.t[:, e, :], start=True,
                            stop=False), [tm, UT], [pZ])
                        kb.op(PE, lambda e=e, rs=rs, c=c: nc.tensor.matmul(
                            pZ.t[rs, 0:64], lhsT=tm.t[:, c, 1, e * 64:(e + 1) * 64], rhs=tv.t[:, c, e * 64:(e + 1) * 64],
                            start=False, stop=True), [tm, tv], [pZ])
                    kb.op(DVE, lambda c=c: nc.vector.tensor_scalar(out=zt.t[:], in0=pZ.t[:, 0:64],
                                                                   scalar1=sc.t[:, 8 + c:9 + c], scalar2=None,
                                                                   op0=ALU.mult), [pZ, sc], [zt])
                    kb.op(DVE, lambda c=c, Z=Z: nc.vector.scalar_tensor_tensor(
                        out=Z.t[:], in0=Z.t[:], scalar=sc.t[:, 4 + c:5 + c], in1=zt.t[:], op0=ALU.mult, op1=ALU.add),
                        [Z, sc, zt], [Z])
                    if c in seq_ends:
                        sq_ = seq_ends[c]
                        kb.op(PE, lambda Z=Z: nc.tensor.transpose(pZ.t[0:64, 128:256], Z.t[:], ident_f.t[:]),
                              [Z, ident_f], [pZ])
                        kb.op(DVE, lambda: nc.vector.tensor_copy(out=sto.t[:], in_=pZ.t[0:64, 128:256]), [pZ], [sto])
                        kb.dma(SP, st_out[sq_, d, 2 * hp:2 * hp + 2].rearrange("e v k -> v e k"),
                               sto.t[:].rearrange("v (e k) -> v e k", e=2), reads=[sto], writes=[DX])
                kb.dma(SP, Ys.t[blk, hp, d], yb.t[:], reads=[yb], writes=[Ys])

            ui = 0
            for blk in range(NBC):
                for hp in range(16):
                    for d in range(2):
                        order = [0, 1, 2, 3] if d == 0 else [3, 2, 1, 0]
                        if d == 0:
                            starts = {0: ("zero", None), 2: ("zero", None)}
                            ends = {1: blk * 2, 3: blk * 2 + 1}
                        else:
                            starts = {3: ("zero", None), 1: ("zero", None)}
                            ends = {2: blk * 2 + 1, 0: blk * 2}
                        scan_unit(ui, blk, hp, d, order, starts, ends)
                        ui += 1
            for s_ in range(NBL):
                for hp in range(16):
                    for d in range(2):
                        if d == 0:
                            blk = NBC + s_
                            order = [0, 1, 2, 3]
                            starts = {0: ("load", None)} if s_ == 0 else {}
                        else:
                            blk = NBC + NBL - 1 - s_
                            order = [3, 2, 1, 0]
                            starts = {3: ("load", None)} if s_ == 0 else {}
                        scan_unit(ui, blk, hp, d, order, starts, {})
                        ui += 1
        kb.barrier()

        stage_gate(3)
        with contextlib.ExitStack() as st:
            ccs = Buf(kb.sb("ccs", [128, 2, 256], BF16, st))
            scs = Buf(kb.sb("scs", [128, 2, 256], BF16, st))
            kb.dma(SP, ccs.t[:], cc_d.rearrange("(k p) n -> p k n", p=128), reads=[DX], writes=[ccs])
            kb.dma(SP, scs.t[:], scn_d.rearrange("(k p) n -> p k n", p=128), reads=[DX], writes=[scs])
            B12 = [Buf(kb.sb("B12_%d" % i, [128, 2, 2, TB], BF16, st)) for i in range(2)]
            fo = [Buf(kb.sb("fo%d" % i, [128, TB], BF16, st)) for i in range(2)]
            pf = [Buf(kb.ps("pf%d" % i, [128, TB], F32, st)) for i in range(6)]
            pfi = [0]

            def nextpf():
                b_ = pf[pfi[0] % 6]
                pfi[0] += 1
                return b_

            def fourier_seq(tok0, T, ct_d, st_d, ctb, stb, xbg):
                nt = T // 128
                TP = min(TB, T)
                for g in range(4):
                    kb.dma(SP, xbg[g].t[:, 0:nt, :],
                           XBs.t[tok0:tok0 + T, g * 256:(g + 1) * 256].rearrange("(n p) c -> p n c", p=128),
                           reads=[XBs], writes=[xbg[g]])
                for tb_ in range(T // TP):
                    kb.dma(SP, ctb.t[:, 0:nt, 0:TP], ct_d[:, tb_ * TP:(tb_ + 1) * TP].rearrange("(n p) c -> p n c", p=128),
                           reads=[DX], writes=[ctb])
                    kb.dma(SP, stb.t[:, 0:nt, 0:TP], st_d[:, tb_ * TP:(tb_ + 1) * TP].rearrange("(n p) c -> p n c", p=128),
                           reads=[DX], writes=[stb])
                    for g in range(4):
                        bb_ = B12[g % 2]
                        for cc_ in range(2):
                            for q_, mat in enumerate([ctb, stb]):
                                pb = nextpf()
                                for tt in range(nt):
                                    kb.op(PE, lambda pb=pb, tt=tt, cc_=cc_, mat=mat, g=g: nc.tensor.matmul(
                                        pb.t[:, 0:TP], lhsT=xbg[g].t[:, tt, cc_ * 128:(cc_ + 1) * 128],
                                        rhs=mat.t[:, tt, 0:TP], start=(tt == 0), stop=(tt == nt - 1)),
                                        [xbg[g], mat], [pb])
                                if q_ == 0:
                                    kb.op(ACT, lambda pb=pb, cc_=cc_, bb_=bb_: nc.scalar.copy(
                                        out=bb_.t[:, cc_, 0, 0:TP], in_=pb.t[:, 0:TP]), [pb], [bb_])
                                else:
                                    kb.op(DVE, lambda pb=pb, cc_=cc_, bb_=bb_: nc.vector.tensor_copy(
                                        out=bb_.t[:, cc_, 1, 0:TP], in_=pb.t[:, 0:TP]), [pb], [bb_])
                        for cp in range(2):
                            pb = nextpf()
                            n_ = 0
                            for cc_ in range(2):
                                for q_, mat in enumerate([ccs, scs]):
                                    kb.op(PE, lambda pb=pb, cc_=cc_, q_=q_, mat=mat, cp=cp, n_=n_, bb_=bb_: nc.tensor.matmul(
                                        pb.t[:, 0:TP], lhsT=mat.t[:, cc_, cp * 128:(cp + 1) * 128],
                                        rhs=bb_.t[:, cc_, q_, 0:TP], start=(n_ == 0), stop=(n_ == 3)), [mat, bb_], [pb])
                                    n_ += 1
                            f_ = fo[cp]
                            kb.op(ACT if cp == 0 else DVE,
                                  (lambda pb=pb, f_=f_: nc.scalar.copy(out=f_.t[:, 0:TP], in_=pb.t[:, 0:TP])) if cp == 0 else
                                  (lambda pb=pb, f_=f_: nc.vector.tensor_copy(out=f_.t[:, 0:TP], in_=pb.t[:, 0:TP])),
                                  [pb], [f_])
                            t0 = tok0 + tb_ * TP
                            kb.dma(SP, FSs.t[g * 2 + cp, :, t0:t0 + TP], f_.t[:, 0:TP], reads=[f_], writes=[FSs])

            ctb = Buf(kb.sb("ctb", [128, NTL, TB], BF16, st))
            stb = Buf(kb.sb("stb", [128, NTL, TB], BF16, st))
            xbg = [Buf(kb.sb("xbg%d" % g, [128, NTL, 256], BF16, st)) for g in range(4)]
            for s_ in range(NSEQ):
                fourier_seq(s_ * 256, 256, ct256_d, st256_d, ctb, stb, xbg)
            fourier_seq(NBC * TB, TL, ctL_d, stL_d, ctb, stb, xbg)
        kb.barrier()

        stage_gate(4)
        with contextlib.ExitStack() as st:
            xtm = [Buf(kb.sb("xtm%d" % i, [128, D], F32, st)) for i in range(2)]
            xT = [Buf(kb.sb("xT%d" % i, [128, TB], F32, st)) for i in range(16)]
            hT = [Buf(kb.sb("hT%d" % i, [128, TB], BF16, st)) for i in range(16)]
            gA = [Buf(kb.sb("gA%d" % i, [128, TB], BF16, st)) for i in range(16)]
            act = [Buf(kb.sb("act%d" % i, [128, TB], BF16, st)) for i in range(FC)]
            ya, gB, FB = act[0:16], act[16:32], act[32:40]
            sqb = [Buf(kb.sb("sqb%d" % i, [128, TB], BF16, st)) for i in range(2)]
            tmpf = [Buf(kb.sb("tmpf%d" % i, [128, TB], F32, st)) for i in range(2)]
            rstd = Buf(kb.sb("rstd", [128, TB], F32, st))
            F = lambda n: Buf(kb.sb(n, [128, TB], F32, st))
            y0, y1, bo, g_, dl, sq2 = [F("f4_%d" % i) for i in range(6)]
            ug, uv = y0, y1
            ws = WStream(make_wpool(st, 2, FC * 128 + 128))
            wsd = ws
            ptr = [Buf(kb.ps("ptr%d" % i, [128, 4, 128], F32, st)) for i in range(2)]
            pp = [Buf(kb.ps("pp%d" % i, [128, TB], F32, st)) for i in range(5)]
            ppi = [0]

            def nextp():
                b_ = pp[ppi[0] % 5]
                ppi[0] += 1
                return b_

            def proj(wsrc, kc, ncol_total, gw, rhs, evac, wstream):
                for g in range(ncol_total // gw):
                    wb, wv = wstream.load(wsrc[:, :, g * gw:(g + 1) * gw], kc, gw)
                    for j in range(gw // 128):
                        pb = nextp()
                        for k in range(kc):
                            kb.op(PE, lambda k=k, j=j, pb=pb, wv=wv: nc.tensor.matmul(
                                pb.t[:], lhsT=wv[:, k, j * 128:(j + 1) * 128], rhs=rhs[k].t[:], start=(k == 0),
                                stop=(k == kc - 1)), [wb, rhs[k]], [pb])
                        evac(g * (gw // 128) + j, pb)

            wv_in = w_in.rearrange("(k p) n -> p k n", p=128)
            wv_oa = w_out_a.rearrange("(k p) n -> p k n", p=128)
            wv_fo = w_fourier.rearrange("(k p) n -> p k n", p=128)
            wv_o = w_out.rearrange("(k p) n -> p k n", p=128)
            wv_fi2 = ffn_w_in.rearrange("(k p) n -> p k n", p=128)
            wv_fd = ffn_w_down.rearrange("(k p) n -> p k n", p=128)
            def compute_ya(blk, hp, dst):
                kb.dma(SP, y0.t[:], Ys.t[blk, hp, 0], reads=[Ys], writes=[y0])
                kb.dma(SP, y1.t[:], Ys.t[blk, hp, 1], reads=[Ys], writes=[y1])
                kb.dma(SP, bo.t[:], BGs.t[blk, hp, 0], reads=[BGs], writes=[bo])
                kb.dma(SP, g_.t[:], BGs.t[blk, hp, 1], reads=[BGs], writes=[g_])
                kb.op(POOL, lambda: nc.gpsimd.tensor_tensor(out=y0.t[:], in0=y0.t[:], in1=y1.t[:], op=ALU.add),
                      [y0, y1], [y0])
                pmn = nextp()
                kb.op(PE, lambda pmn=pmn: nc.tensor.matmul(pmn.t[:], lhsT=blk64.t[:], rhs=y0.t[:], start=True,
                                                           stop=True), [blk64, y0], [pmn])
                kb.op(DVE, lambda pmn=pmn: nc.vector.tensor_tensor(out=dl.t[:], in0=y0.t[:], in1=pmn.t[:],
                                                                  op=ALU.subtract), [y0, pmn], [dl])
                kb.op(ACT, lambda: nc.scalar.activation(out=sq2.t[:], in_=dl.t[:], func=AF.Square), [dl], [sq2])
                pvr = nextp()
                kb.op(PE, lambda pvr=pvr: nc.tensor.matmul(pvr.t[:], lhsT=blk64.t[:], rhs=sq2.t[:], start=True,
                                                           stop=True), [blk64, sq2], [pvr])
                kb.op(DVE, lambda pvr=pvr: nc.vector.tensor_scalar(out=sq2.t[:], in0=pvr.t[:], scalar1=GN_EPS,
                                                                  scalar2=None, op0=ALU.add), [pvr], [sq2])
                kb.op(ACT, lambda: nc.scalar.activation(out=sq2.t[:], in_=sq2.t[:], func=AF.Ln), [sq2], [sq2])
                kb.op(ACT, lambda: nc.scalar.activation(out=sq2.t[:], in_=sq2.t[:], func=AF.Exp, scale=-0.5),
                      [sq2], [sq2])
                kb.op(POOL, lambda: nc.gpsimd.tensor_tensor(out=dl.t[:], in0=dl.t[:], in1=sq2.t[:], op=ALU.mult),
                      [dl, sq2], [dl])
                kb.op(ACT, lambda hp=hp: nc.scalar.activation(out=dl.t[:], in_=dl.t[:], func=AF.Identity,
                                                              bias=pv("lnb", hp), scale=pv("lng", hp)),
                      [dl, pvec], [dl])
                kb.op(DVE, lambda: nc.vector.tensor_tensor(out=dl.t[:], in0=dl.t[:], in1=bo.t[:], op=ALU.add),
                      [dl, bo], [dl])
                kb.op(DVE, lambda hp=hp: nc.vector.tensor_tensor(out=dst.t[:], in0=dl.t[:], in1=g_.t[:],
                                                                op=ALU.mult), [dl, g_], [dst])

            msel = Buf(kb.sb("msel", [128, NSL, NBL], F32, st))
            kb.dma(SP, msel.t[:], msel_d, reads=[DX], writes=[msel])
            yat = hT[0]
            for hp in range(16):
                for s_ in range(NSL):
                    kb.op(POOL, lambda s_=s_: nc.gpsimd.memset(gA[s_].t[:], 0.0), [], [gA[s_]])
                for lb in range(NBL):
                    compute_ya(NBC + lb, hp, yat)
                    for s_ in range(NSL):
                        kb.op(DVE, lambda s_=s_, lb=lb: nc.vector.scalar_tensor_tensor(
                            out=gA[s_].t[:], in0=yat.t[:], scalar=msel.t[:, s_, lb:lb + 1], in1=gA[s_].t[:],
                            op0=ALU.mult, op1=ALU.add), [yat, msel, gA[s_]], [gA[s_]])
                for s_ in range(NSL):
                    kb.dma(SP, YAsel.t[s_, hp], gA[s_].t[:], reads=[gA[s_]], writes=[YAsel])
            for c8 in range(8):
                for s_ in range(NSL):
                    kb.op(POOL, lambda s_=s_: nc.gpsimd.memset(gA[s_].t[:], 0.0), [], [gA[s_]])
                for lb in range(NBL):
                    t0_ = (NBC + lb) * TB
                    kb.dma(SP, yat.t[:], FSs.t[c8, :, t0_:t0_ + TB], reads=[FSs], writes=[yat])
                    for s_ in range(NSL):
                        kb.op(DVE, lambda s_=s_, lb=lb: nc.vector.scalar_tensor_tensor(
                            out=gA[s_].t[:], in0=yat.t[:], scalar=msel.t[:, s_, lb:lb + 1], in1=gA[s_].t[:],
                            op0=ALU.mult, op1=ALU.add), [yat, msel, gA[s_]], [gA[s_]])
                for s_ in range(NSL):
                    kb.dma(SP, FBsel.t[s_, c8], gA[s_].t[:], reads=[gA[s_]], writes=[FBsel])

            for blk in list(range(NBC)) + [NBLK + s_ for s_ in range(NSL)]:
                slot = blk - NBLK
                r, rl = blk_info(blk)
                load_xT(blk, xtm, xT, ptr)
                pss = nextp()
                rstd_of(xT, sqb, pss, rstd)
                norm_mod(xT, rstd, hT, tmpf, 0, r)
                if blk < NBC:
                    for hp in range(16):
                        compute_ya(blk, hp, ya[hp])
                else:
                    for hp in range(16):
                        kb.dma(SP, ya[hp].t[:], YAsel.t[slot, hp], reads=[YAsel], writes=[ya[hp]])
                proj(wv_in[:, :, 7168:9216], 16, 2048, 256, hT,
                     lambda c, pb: kb.op(ACT, lambda: nc.scalar.activation(out=gA[c].t[:], in_=pb.t[:],
                                                                           func=AF.Sigmoid), [pb], [gA[c]]), ws)
                proj(wv_in[:, :, 9216:11264], 16, 2048, 256, hT,
                     lambda c, pb: kb.op(ACT, lambda: nc.scalar.activation(out=gB[c].t[:], in_=pb.t[:],
                                                                           func=AF.Sigmoid), [pb], [gB[c]]), ws)
                proj(wv_oa, 16, 2048, 256, ya,
                     lambda c, pb: kb.op(DVE, lambda: nc.vector.tensor_tensor(out=gA[c].t[:], in0=pb.t[:],
                                                                              in1=gA[c].t[:], op=ALU.mult),
                                         [pb, gA[c]], [gA[c]]), ws)
                for c8 in range(8):
                    if blk < NBC:
                        kb.dma(SP, FB[c8].t[:], FSs.t[c8, :, blk * TB:(blk + 1) * TB], reads=[FSs], writes=[FB[c8]])
                    else:
                        kb.dma(SP, FB[c8].t[:], FBsel.t[slot, c8], reads=[FBsel], writes=[FB[c8]])

                def ev_f(c, pb):
                    kb.op(DVE, lambda: nc.vector.tensor_tensor(out=gB[c].t[:], in0=pb.t[:], in1=gB[c].t[:],
                                                               op=ALU.mult), [pb, gB[c]], [gB[c]])
                    kb.op(POOL, lambda: nc.gpsimd.tensor_tensor(out=gA[c].t[:], in0=gA[c].t[:], in1=gB[c].t[:],
                                                                op=ALU.add), [gA[c], gB[c]], [gA[c]])
                proj(wv_fo, 8, 2048, 512, FB, ev_f, ws)
                proj(wv_o, 16, 2048, 256, gA,
                     lambda c, pb: kb.op(DVE, lambda: nc.vector.scalar_tensor_tensor(
                         out=xT[c].t[:], in0=pb.t[:], scalar=modcol(2, c, r), in1=xT[c].t[:], op0=ALU.mult,
                         op1=ALU.add), [pb, modv, xT[c]], [xT[c]]), ws)
                pss = nextp()
                rstd_of(xT, sqb, pss, rstd)
                norm_mod(xT, rstd, hT, tmpf, 1, r)
                for f in range(FC):
                    wb, wv = ws.load([wv_fi2[:, :, s3 * DFF + f * 128:s3 * DFF + (f + 1) * 128] for s3 in range(2)], 16, 256)
                    pg, pu = nextp(), nextp()
                    for s_, pb in enumerate([pg, pu]):
                        for k in range(16):
                            kb.op(PE, lambda k=k, s_=s_, pb=pb, wv=wv: nc.tensor.matmul(
                                pb.t[:], lhsT=wv[:, k, s_ * 128:(s_ + 1) * 128], rhs=hT[k].t[:], start=(k == 0),
                                stop=(k == 15)), [wb, hT[k]], [pb])
                    conv3(pg, ug, "fcw", "fcb", f, 86, rl)
                    conv3(pu, uv, "fcw", "fcb", FC + f, 86, rl)
                    kb.op(ACT, lambda: nc.scalar.activation(out=ug.t[:], in_=ug.t[:], func=AF.Silu), [ug], [ug])
                    kb.op(POOL, lambda f=f: nc.gpsimd.tensor_tensor(out=act[f].t[:], in0=ug.t[:], in1=uv.t[:],
                                                                    op=ALU.mult), [ug, uv], [act[f]])
                proj(wv_fd, FC, 2048, 128, act,
                     lambda c, pb: kb.op(DVE, lambda: nc.vector.scalar_tensor_tensor(
                         out=xT[c].t[:], in0=pb.t[:], scalar=modcol(5, c, r), in1=xT[c].t[:], op0=ALU.mult,
                         op1=ALU.add), [pb, modv, xT[c]], [xT[c]]), wsd)
                pss = nextp()
                rstd_of(xT, sqb, pss, rstd)
                for c in range(16):
                    kb.op(DVE, lambda c=c: nc.vector.scalar_tensor_tensor(
                        out=xT[c].t[:], in0=xT[c].t[:], scalar=pv("gf", c), in1=rstd.t[:], op0=ALU.mult,
                        op1=ALU.mult), [xT[c], pvec, rstd], [xT[c]])
                for t in range(4):
                    ob = xtm[t % 2]
                    for g in range(4):
                        pb = ptr[(t * 4 + g) % 2]
                        for j in range(4):
                            c = g * 4 + j
                            kb.op(PE, lambda pb=pb, j=j, c=c, t=t: nc.tensor.transpose(
                                pb.t[:, j, :], xT[c].t[:, t * 128:(t + 1) * 128], ident_f.t[:]), [xT[c], ident_f], [pb])
                        if g % 2 == 0:
                            kb.op(ACT, lambda pb=pb, g=g, ob=ob: nc.scalar.copy(
                                out=ob.t[:, g * 512:(g + 1) * 512], in_=pb.t[:].rearrange("p a b -> p (a b)")),
                                [pb], [ob])
                        else:
                            kb.op(DVE, lambda pb=pb, g=g, ob=ob: nc.vector.tensor_copy(
                                out=ob.t[:, g * 512:(g + 1) * 512], in_=pb.t[:].rearrange("p a b -> p (a b)")),
                                [pb], [ob])
                    r0 = (blk if blk < NBC else NBC + slot) * TB + t * 128
                    kb.dma(SP, y_all[r0:r0 + 128, :], ob.t[:], reads=[ob], writes=[DX])
        kb.final_wait()
        print("instructions:", kb.nins, "ops:", getattr(kb, "nops", 0))
    return nc


def fm_vec(v):
    v = np.asarray(v, np.float32).reshape(-1, 128)
    return np.ascontiguousarray(v.T)


_CACHE = {}


def host_consts(TL):
    if TL in _CACHE:
        return _CACHE[TL]
    c = {}
    c["ident_f"] = np.eye(128, dtype=np.float32)
    cbf = np.zeros((128, 6, 128), np.float32)
    cbf[:, 0, :] = np.eye(128)
    blk = np.zeros((128, 128), np.float32)
    blk[:64, :64] = 1
    blk[64:, 64:] = 1
    cbf[:, 1, :] = blk
    cbf[:, 2, :] = 1
    c["cbf"] = cbf.astype(NPBF)
    c["blk64"] = (blk / 64.0).astype(np.float32)
    idx = np.arange(128)
    m = np.zeros((128, 2, 6, 128), np.float32)
    for d in range(2):
        before = (idx[:, None] > idx[None, :]) if d == 1 else (idx[:, None] < idx[None, :])
        beq = before | np.eye(128, dtype=bool)
        m[:, d, 0] = before
        m[:, d, 1] = beq
        m[:, d, 2] = before
        m[:, d, 3] = beq
        m[:, d, 4] = before.T
    c["masks"] = m.astype(NPBF)
    rm = np.ones((128, TB), np.float32)
    rm[:, ::128] = 0
    c["rmask"] = rm

    def dft(T):
        t = np.arange(T, dtype=np.float64)
        ang = 2 * np.pi * ((t[:, None] * t[None, :]) % T) / T
        return np.cos(ang), np.sin(ang)
    c256, s256 = dft(256)
    c["ct256"] = (c256 / 16.0).astype(NPBF)
    c["st256"] = (s256 / 16.0).astype(NPBF)
    cL, sL = dft(TL)
    c["ctL"] = (cL / np.sqrt(TL)).astype(NPBF)
    c["stL"] = (sL / np.sqrt(TL)).astype(NPBF)
    c["ccm"] = (c256 / 16.0).astype(NPBF)
    c["scn"] = (-s256 / 16.0).astype(NPBF)
    _CACHE[TL] = c
    return c


_PROG = {}


def kernel(x_prompt, x_sample, state_rwkv, c, c_ctx, ada_w, ada_b, norm_mix_g, w_in,
           rkv_conv_w, rkv_conv_b, decay_up, decay_base, iclr_up, iclr_base, gate_up,
           k_k, k_a, r_k, lnx_g, lnx_b, w_out_a, w_fourier, w_out, norm_ffn_g,
           ffn_w_in, ffn_conv_w, ffn_conv_b, ffn_w_down, final_norm_g):
    f = lambda a: np.asarray(a, np.float32)
    x_prompt, x_sample, state_rwkv, c, c_ctx = f(x_prompt), f(x_sample), f(state_rwkv), f(c), f(c_ctx)
    B, S, _ = x_prompt.shape
    NBAT, TL, _ = x_sample.shape
    NCORE = 8
    NSEQ = B // NCORE
    key = (NSEQ, TL)
    if key not in _PROG:
        _PROG[key] = build_program(NSEQ, TL)
    nc = _PROG[key]
    consts = host_consts(TL)
    pvec = np.concatenate([
        fm_vec(f(ada_b)[0]), fm_vec(f(norm_mix_g)[0]), fm_vec(f(norm_ffn_g)[0]), fm_vec(f(final_norm_g)),
        fm_vec(f(rkv_conv_w)[0, 0]), fm_vec(f(rkv_conv_w)[0, 1]), fm_vec(f(rkv_conv_w)[0, 2]),
        fm_vec(f(rkv_conv_b)[0]),
        fm_vec(f(decay_base)[0, 0]), fm_vec(f(decay_base)[0, 1]),
        fm_vec(f(iclr_base)[0, 0]), fm_vec(f(iclr_base)[0, 1]),
        fm_vec(f(k_k)[0]), fm_vec(f(k_a)[0]), fm_vec(f(r_k)[0]), fm_vec(f(lnx_g)[0]), fm_vec(f(lnx_b)[0]),
        fm_vec(f(ffn_conv_w)[0, 0]), fm_vec(f(ffn_conv_w)[0, 1]), fm_vec(f(ffn_conv_w)[0, 2]),
        fm_vec(f(ffn_conv_b)[0]),
    ], axis=1)
    assert pvec.shape[1] == NPV
    shared = dict(
        ada_w=f(ada_w)[0], w_in=f(w_in)[0], decay_up=f(decay_up)[0], iclr_up=f(iclr_up)[0],
        gate_up=f(gate_up)[0], w_out_a=f(w_out_a)[0], w_fourier=f(w_fourier)[0], w_out=f(w_out)[0],
        ffn_w_in=f(ffn_w_in)[0], ffn_w_down=f(ffn_w_down)[0], pvec=np.ascontiguousarray(pvec), **consts)
    in_maps = []
    cpb = NCORE // NBAT
    for j in range(NCORE):
        b = j // cpb
        q = j % cpb
        NSLh = (TL // TB) // 4
        mine = x_sample[b][q * NSLh * TB:(q + 1) * NSLh * TB]
        xa = np.concatenate([x_prompt[j * NSEQ:(j + 1) * NSEQ].reshape(-1, D), x_sample[b], mine], axis=0)
        ms = np.zeros((128, NSLh, TL // TB), np.float32)
        for s_ in range(NSLh):
            ms[:, s_, q * NSLh + s_] = 1.0
        cf = np.stack([fm_vec(c_ctx), fm_vec(c[b])], axis=-1)
        m = dict(shared)
        m["x_all"] = np.ascontiguousarray(xa)
        m["msel"] = ms
        m["cfm"] = np.ascontiguousarray(cf)
        m["st_lat"] = np.ascontiguousarray(state_rwkv[b, 0])
        in_maps.append(m)
    import os
    kc_ = int(os.environ.get('KCORES', NCORE))
    if os.environ.get('KTRACE'):
        res = run_bass_kernel_spmd(nc, in_maps[:kc_], core_ids=list(range(kc_)), trace=True)
        print('exec_time_ns', res.exec_time_ns)
    else:
        res = run_bass_kernel_spmd(nc, in_maps[:kc_], core_ids=list(range(kc_)))
    outs = list(res.results)
    while len(outs) < NCORE:
        outs.append(outs[0])
    nctx = NSEQ * S
    y_prompt = np.concatenate([np.asarray(outs[j]["y_all"])[:nctx].reshape(NSEQ, S, D) for j in range(NCORE)], axis=0)
    y_sample = np.stack([np.concatenate([np.asarray(outs[b * cpb + q]["y_all"])[nctx:] for q in range(cpb)], axis=0) for b in range(NBAT)], axis=0)
    new_state = np.concatenate([np.asarray(outs[j]["st_out"]) for j in range(NCORE)], axis=0)[:, None]
    return (y_prompt.astype(np.float32), y_sample.astype(np.float32), new_state.astype(np.float32))
```
